# Optimizing a Trainium2 kernel written in Bass

```python
import math
import jax, jax.numpy as jnp
from jax import lax
import numpy as np

D_MODEL = 1024
BATCH = 16
SEQ = 256
DEPTH = 1
DEC_BATCH = 2
DEC_SEQ = 2048
PAST_LEN = 256

GRID_W = 64
MLA_HEADS = 8
MLA_NOPE = 64
MLA_ROPE = 32
MLA_QK = MLA_NOPE + MLA_ROPE
MLA_V = 64
Q_LORA = 384
KV_LORA = 256
GLA_HEADS = 4
GLA_DK = 64
GLA_DV = 128
GATE_RANK = 16
GATE_NORM = 16.0
CHUNK = 64
D_MIX = MLA_HEADS * MLA_V + GLA_HEADS * GLA_DV
D_FF = 2816
QBLOCK = 128
ROPE_BASE = 10000.0
EPS = 1e-6
IN_SIZES = (Q_LORA, KV_LORA, MLA_ROPE, GLA_HEADS * GLA_DK, GLA_HEADS * GLA_DK,
            GLA_HEADS * GLA_DV, GATE_RANK, GATE_RANK, GLA_HEADS * GLA_DV)
IN_COLS = Q_LORA + KV_LORA + MLA_ROPE + 2 * GLA_HEADS * GLA_DK + 2 * GLA_HEADS * GLA_DV + 2 * GATE_RANK

kernel_name = "hybrid_mla_gla_diffusion_step"


def _rmsnorm(x, w):
    xf = x.astype(jnp.float32)
    xf = xf * lax.rsqrt(jnp.mean(xf * xf, axis=-1, keepdims=True) + EPS)
    return (xf * w.astype(jnp.float32)).astype(x.dtype)


def _split_in(z):
    idx = []
    acc = 0
    for s in IN_SIZES[:-1]:
        acc += s
        idx.append(acc)
    return jnp.split(z, idx, axis=-1)


def _axial_rope_tables(n_tokens):
    rows = n_tokens // GRID_W
    t = jnp.arange(rows * GRID_W)
    row = (t // GRID_W).astype(jnp.float32)
    col = (t % GRID_W).astype(jnp.float32)
    half = MLA_ROPE // 2
    inv = ROPE_BASE ** (-jnp.arange(0, half, 2, dtype=jnp.float32) / half)
    ang_r = row[:, None] * inv
    ang_c = col[:, None] * inv
    ang = jnp.concatenate([ang_r, ang_r, ang_c, ang_c], axis=-1)
    return jnp.cos(ang), jnp.sin(ang)


def _rotate_half(x):
    x1, x2 = jnp.split(x, 2, axis=-1)
    return jnp.concatenate([-x2, x1], axis=-1)


def _apply_axial_rope(x, cos, sin):
    xf = x.astype(jnp.float32)
    half = MLA_ROPE // 2
    rot = jnp.concatenate([_rotate_half(xf[..., :half]), _rotate_half(xf[..., half:])], axis=-1)
    return (xf * cos + rot * sin).astype(x.dtype)


def _mla_decompress(ckv, k_rope, w_ukv):
    b, s, _ = ckv.shape
    kv = (ckv @ w_ukv).reshape(b, s, MLA_HEADS, MLA_NOPE + MLA_V)
    k_nope, v = kv[..., :MLA_NOPE], kv[..., MLA_NOPE:]
    k_r = jnp.broadcast_to(k_rope[:, :, None, :], (b, s, MLA_HEADS, MLA_ROPE))
    return jnp.concatenate([k_nope, k_r], axis=-1), v


def _blocked_attention(q, k, v):
    b, t, h, d = q.shape
    nb = t // QBLOCK
    qb = q.reshape(b, nb, QBLOCK, h, d).transpose(1, 0, 2, 3, 4)
    scale = MLA_QK ** -0.5

    def one_block(qi):
        s = jnp.einsum('bqhd,bkhd->bhqk', qi, k).astype(jnp.float32) * scale
        p = jax.nn.softmax(s, axis=-1).astype(v.dtype)
        return jnp.einsum('bhqk,bkhd->bqhd', p, v)

    out = lax.map(one_block, qb)
    return out.transpose(1, 0, 2, 3, 4).reshape(b, t, h, v.shape[-1])


def _gla_chunk(q, k, v, g, s0):
    b, h, t, dk = q.shape
    dv = v.shape[-1]
    n = t // CHUNK
    f32 = jnp.float32
    q = q.astype(f32).reshape(b, h, n, CHUNK, dk) * (dk ** -0.5)
    k = k.astype(f32).reshape(b, h, n, CHUNK, dk)
    v = v.astype(f32).reshape(b, h, n, CHUNK, dv)
    cum = jnp.cumsum(g.astype(f32).reshape(b, h, n, CHUNK, dk), axis=3)
    cum_last = cum[:, :, :, -1:, :]
    qe = q * jnp.exp(cum)
    ke = k * jnp.exp(-cum)
    kd = k * jnp.exp(cum_last - cum)
    mask = jnp.tril(jnp.ones((CHUNK, CHUNK), dtype=bool))
    att = jnp.where(mask, jnp.einsum('bhncd,bhnsd->bhncs', qe, ke), 0.0)
    o_intra = jnp.einsum('bhncs,bhnse->bhnce', att, v)
    decay = jnp.exp(cum_last[:, :, :, 0, :])

    def step(state, inp):
        qe_n, kd_n, v_n, dec_n = inp
        o_n = jnp.einsum('bhcd,bhde->bhce', qe_n, state)
        state = dec_n[..., None] * state + jnp.einsum('bhcd,bhce->bhde', kd_n, v_n)
        return state, o_n

    xs = (jnp.moveaxis(qe, 2, 0), jnp.moveaxis(kd, 2, 0), jnp.moveaxis(v, 2, 0), jnp.moveaxis(decay, 2, 0))
    s_final, o_inter = lax.scan(step, s0.astype(f32), xs)
    o = o_intra + jnp.moveaxis(o_inter, 0, 2)
    return o.reshape(b, h, t, dv), s_final


def _gla_gate(low, w2, b2):
    return jax.nn.log_sigmoid((low @ w2 + b2).astype(jnp.float32)) / GATE_NORM


def _layer(x, cond, lp, rope, ctx):
    (w_ada, b_ada, norm_attn, w_in, q_norm, w_uq, kv_norm, w_ukv, w_gate_f, b_gate_f,
     w_gate_b, b_gate_b, gla_norm, w_out, norm_ffn, w_ffn_in, w_ffn_out) = lp
    b, t, _ = x.shape
    mod = jax.nn.silu(cond) @ w_ada + b_ada
    sh1, sc1, gt1, sh2, sc2, gt2 = jnp.split(mod, 6, axis=-1)
    h = _rmsnorm(x, norm_attn) * (1 + sc1) + sh1
    q_lat, kv_lat, k_rope, gq, gk, gv, gf_low, gb_low, g_out = _split_in(h @ w_in)

    q = (_rmsnorm(q_lat, q_norm) @ w_uq).reshape(b, t, MLA_HEADS, MLA_QK)
    ckv = _rmsnorm(kv_lat, kv_norm)
    if rope is not None:
        cos, sin = rope
        q = jnp.concatenate([q[..., :MLA_NOPE],
                             _apply_axial_rope(q[..., MLA_NOPE:], cos[:, None, :], sin[:, None, :])], axis=-1)
        k_rope_pos = _apply_axial_rope(k_rope, cos, sin)
    else:
        k_rope_pos = k_rope
    k, v = _mla_decompress(ckv, k_rope_pos, w_ukv)
    if ctx is not None:
        ckv_ctx, krope_ctx, s_f0, s_b0 = ctx
        k_c, v_c = _mla_decompress(ckv_ctx.astype(x.dtype), krope_ctx.astype(x.dtype), w_ukv)
        k = jnp.concatenate([k_c, k], axis=1)
        v = jnp.concatenate([v_c, v], axis=1)
    else:
        s_f0 = jnp.zeros((b, GLA_HEADS, GLA_DK, GLA_DV), jnp.float32)
        s_b0 = jnp.zeros((b, GLA_HEADS, GLA_DK, GLA_DV), jnp.float32)
    attn = _blocked_attention(q, k, v).reshape(b, t, MLA_HEADS * MLA_V)

    def heads(a, d):
        return a.reshape(b, t, GLA_HEADS, d).transpose(0, 2, 1, 3)
    qg, kg, vg = heads(gq, GLA_DK), heads(gk, GLA_DK), heads(gv, GLA_DV)
    g_f = heads(_gla_gate(gf_low, w_gate_f, b_gate_f), GLA_DK)
    g_b = heads(_gla_gate(gb_low, w_gate_b, b_gate_b), GLA_DK)
    o_f, s_f = _gla_chunk(qg, kg, vg, g_f, s_f0)
    o_b, s_b = _gla_chunk(jnp.flip(qg, 2), jnp.flip(kg, 2), jnp.flip(vg, 2), jnp.flip(g_b, 2), s_b0)
    o = (o_f + jnp.flip(o_b, 2)).transpose(0, 2, 1, 3).astype(x.dtype)
    o = _rmsnorm(o, gla_norm).reshape(b, t, GLA_HEADS * GLA_DV) * jax.nn.silu(g_out)

    mix = jnp.concatenate([attn, o], axis=-1) @ w_out
    x = x + gt1 * mix
    h2 = _rmsnorm(x, norm_ffn) * (1 + sc2) + sh2
    a, g = jnp.split(h2 @ w_ffn_in, 2, axis=-1)
    x = x + gt2 * ((jax.nn.silu(a) * g) @ w_ffn_out)
    return x, (ckv, k_rope, s_f, s_b)


def setup_inputs(seed: int = 0) -> dict:
    key = jax.random.key(seed)
    ks = jax.random.split(key, 32)
    f32 = jnp.float32

    def nrm(k, shape, scale):
        return jax.random.normal(k, shape, f32) * scale

    def gain(k, shape):
        return 1.0 + 0.1 * jax.random.normal(k, shape, f32)

    L = DEPTH
    return {
        "x_prompt": nrm(ks[0], (BATCH, SEQ, D_MODEL), 1.0),
        "x_sample": nrm(ks[1], (DEC_BATCH, DEC_SEQ, D_MODEL), 1.0),
        "cache_kv_latent": nrm(ks[2], (DEC_BATCH, L, PAST_LEN, KV_LORA), 1.0),
        "cache_k_rope": nrm(ks[3], (DEC_BATCH, L, PAST_LEN, MLA_ROPE), 1.0),
        "state_gla_fwd": nrm(ks[4], (DEC_BATCH, L, GLA_HEADS, GLA_DK, GLA_DV), 0.3),
        "state_gla_bwd": nrm(ks[5], (DEC_BATCH, L, GLA_HEADS, GLA_DK, GLA_DV), 0.3),
        "c": nrm(ks[6], (DEC_BATCH, D_MODEL), 1.0),
        "c_ctx": nrm(ks[7], (D_MODEL,), 1.0),
        "w_ada": nrm(ks[8], (L, D_MODEL, 6 * D_MODEL), D_MODEL ** -0.5),
        "b_ada": nrm(ks[9], (L, 6 * D_MODEL), 0.02),
        "norm_attn": gain(ks[10], (L, D_MODEL)),
        "w_in": nrm(ks[11], (L, D_MODEL, IN_COLS), D_MODEL ** -0.5),
        "mla_q_norm": gain(ks[12], (L, Q_LORA)),
        "w_uq": nrm(ks[13], (L, Q_LORA, MLA_HEADS * MLA_QK), Q_LORA ** -0.5),
        "mla_kv_norm": gain(ks[14], (L, KV_LORA)),
        "w_ukv": nrm(ks[15], (L, KV_LORA, MLA_HEADS * (MLA_NOPE + MLA_V)), KV_LORA ** -0.5),
        "w_gate_f": nrm(ks[16], (L, GATE_RANK, GLA_HEADS * GLA_DK), GATE_RANK ** -0.5),
        "b_gate_f": nrm(ks[17], (L, GLA_HEADS * GLA_DK), 0.1),
        "w_gate_b": nrm(ks[18], (L, GATE_RANK, GLA_HEADS * GLA_DK), GATE_RANK ** -0.5),
        "b_gate_b": nrm(ks[19], (L, GLA_HEADS * GLA_DK), 0.1),
        "gla_norm": gain(ks[20], (L, GLA_DV)),
        "w_out": nrm(ks[21], (L, D_MIX, D_MODEL), D_MIX ** -0.5),
        "norm_ffn": gain(ks[22], (L, D_MODEL)),
        "w_ffn_in": nrm(ks[23], (L, D_MODEL, 2 * D_FF), D_MODEL ** -0.5),
        "w_ffn_out": nrm(ks[24], (L, D_FF, D_MODEL), D_FF ** -0.5),
        "final_norm": gain(ks[25], (D_MODEL,)),
    }


def reference(x_prompt, x_sample, cache_kv_latent, cache_k_rope, state_gla_fwd, state_gla_bwd,
              c, c_ctx, w_ada, b_ada, norm_attn, w_in, mla_q_norm, w_uq, mla_kv_norm, w_ukv,
              w_gate_f, b_gate_f, w_gate_b, b_gate_b, gla_norm, w_out, norm_ffn, w_ffn_in,
              w_ffn_out, final_norm):
    rope = _axial_rope_tables(x_sample.shape[1])
    cond_ctx = c_ctx[None, None, :]
    cond_lat = c[:, None, :]
    xp, xs = x_prompt, x_sample
    kv_list, kr_list, sf_list, sb_list = [], [], [], []
    for l in range(DEPTH):
        lp = (w_ada[l], b_ada[l], norm_attn[l], w_in[l], mla_q_norm[l], w_uq[l], mla_kv_norm[l],
              w_ukv[l], w_gate_f[l], b_gate_f[l], w_gate_b[l], b_gate_b[l], gla_norm[l], w_out[l],
              norm_ffn[l], w_ffn_in[l], w_ffn_out[l])
        xp, (ckv, kr, sf, sb) = _layer(xp, cond_ctx, lp, None, None)
        kv_list.append(ckv)
        kr_list.append(kr)
        sf_list.append(sf.astype(x_prompt.dtype))
        sb_list.append(sb.astype(x_prompt.dtype))
        ctx = (cache_kv_latent[:, l], cache_k_rope[:, l], state_gla_fwd[:, l], state_gla_bwd[:, l])
        xs, _ = _layer(xs, cond_lat, lp, rope, ctx)
    y_prompt = _rmsnorm(xp, final_norm)
    y_sample = _rmsnorm(xs, final_norm)
    new_kv_latent = jnp.stack(kv_list, axis=1)
    new_k_rope = jnp.stack(kr_list, axis=1)
    new_state_fwd = jnp.stack(sf_list, axis=1)
    new_state_bwd = jnp.stack(sb_list, axis=1)
    return (y_prompt, y_sample, new_kv_latent, new_k_rope, new_state_fwd, new_state_bwd)
```

```python
import math
import os
import numpy as np
import concourse.bass as bass
import concourse.mybir as mybir
from concourse.bass_utils import run_bass_kernel_spmd
from contextlib import ExitStack

F32 = mybir.dt.float32
BF16 = mybir.dt.bfloat16
AF = mybir.ActivationFunctionType
ALU = mybir.AluOpType

N_DMA_SEMS = 8


def _env(name, default):
    if os.environ.get("KDEBUG") == "1":
        return os.environ.get(name, default)
    return default
ENGINES = ("pe", "act", "dve", "pool", "sp")


class T:
    def __init__(self, arena, base, esize, p0, p1, c0, c1):
        self.arena, self.base, self.esize = arena, base, esize
        self.p0, self.p1, self.c0, self.c1 = p0, p1, c0, c1

    def __getitem__(self, key):
        if not isinstance(key, tuple):
            key = (key, slice(None))
        ps, cs = key
        a, b, _ = ps.indices(self.p1 - self.p0)
        c, d, _ = cs.indices(self.c1 - self.c0)
        return T(self.arena, self.base, self.esize, self.p0 + a, self.p0 + b, self.c0 + c, self.c0 + d)

    @property
    def ap(self):
        return self.base[self.p0:self.p1, self.c0:self.c1]

    @property
    def reg(self):
        return (self.arena, self.p0, self.p1, self.c0 * self.esize, self.c1 * self.esize)

    def v3(self, a):
        return self.ap.rearrange("p (a b) -> p a b", a=a)


class Prog:
    WINDOW = int(_env("KWINDOW", "100000"))
    SCHED = _env("KSCHED", "1") == "1"
    BARRIERS = _env("KBARRIERS", "0") == "1"
    SLACK = float(_env("KSLACK", "0.0"))

    SCHED_PH = _env("KSCHED_PH", "")

    def __init__(self):
        self.all = []
        self.cur_phase = 0.0
        self.ops = {e: [] for e in ENGINES}
        self.count = {e: 0 for e in ENGINES}
        self.dma_last = {}

    @staticmethod
    def _overlap(a, b):
        return a[1] < b[2] and b[1] < a[2] and a[3] < b[4] and b[3] < a[4]

    @staticmethod
    def _contains(a, b):
        return a[1] <= b[1] and b[2] <= a[2] and a[3] <= b[3] and b[4] <= a[4]

    enabled = True

    def op(self, eng, fn, reads=(), writes=(), dma=False, cost=None):
        if not self.enabled:
            return None

        def norm(r):
            r = r.reg if isinstance(r, T) else r
            if r[0] == "ps":
                return ("ps", r[1] // 32 * 32, (r[2] + 31) // 32 * 32, r[3] // 2048 * 2048, (r[4] + 2047) // 2048 * 2048)
            return r

        raw_r = [r.reg if isinstance(r, T) else r for r in reads]
        raw_w = [w.reg if isinstance(w, T) else w for w in writes]
        self.all.append(dict(eng=eng, fn=fn, reads=[norm(r) for r in reads], writes=[norm(w) for w in writes],
                             dma=dma, raw_r=raw_r, raw_w=raw_w, cost=cost, ph=self.cur_phase))
        return None

    def _deps(self):
        records = {}
        preds = []
        for i, o in enumerate(self.all):
            p = set()
            for r in o["reads"]:
                for rec in records.get(r[0], ()):
                    if rec[1] == "W" and self._overlap(r, rec[0]):
                        p.add(rec[2])
                    elif (r[0] == "ps" and rec[1] == "R" and r[3] < rec[0][4] and rec[0][3] < r[4]
                          and self.all[rec[2]]["eng"] != o["eng"]):
                        p.add(rec[2])
            for w in o["writes"]:
                for rec in records.get(w[0], ()):
                    if self._overlap(w, rec[0]):
                        p.add(rec[2])
            p.discard(i)
            preds.append(p)
            for w in o["writes"]:
                lst = records.setdefault(w[0], [])
                lst[:] = [rec for rec in lst if not self._contains(w, rec[0])]
                lst.append((w, "W", i))
            for r in o["reads"]:
                lst = records.setdefault(r[0], [])
                lst[:] = [rec for rec in lst if not (rec[1] == "R" and rec[0] == r and rec[2] in p)]
                lst.append((r, "R", i))
        return preds

    def _cost(self, o):
        if o["cost"] is not None:
            return o["cost"]
        w = o["raw_w"][0] if o["raw_w"] else o["raw_r"][0]
        width = max(1, (w[4] - w[3]) // 4)
        e = o["eng"]
        if o["dma"]:
            return 0.15
        if e == "pe":
            return 0.05 + width / 2100.0
        if e == "act":
            return 0.2 + width / 1100.0
        if e == "pool":
            return 0.2 + width / 400.0
        return 0.12 + width / 800.0

    def _dma_latency(self, o):
        w = o["raw_w"][0] if o["raw_w"] else o["raw_r"][0]
        nbytes = (w[2] - w[1]) * (w[4] - w[3])
        return 2.0 + nbytes / 120e3

    def finalize(self):
        n = len(self.all)
        preds = self._deps()
        order = list(range(n))
        order_only = [set() for _ in range(n)]
        fix = [e for e in _env("KFIX", "").split(",") if e]
        for e in fix:
            prev = None
            for i, o in enumerate(self.all):
                if o["eng"] == e:
                    if prev is not None and prev not in preds[i]:
                        preds[i].add(prev)
                        order_only[i].add(prev)
                    prev = i
        if self.SCHED:
            succs = [[] for _ in range(n)]
            npred = [len(p) for p in preds]
            for i, p in enumerate(preds):
                for j in p:
                    succs[j].append(i)
            finish = [0.0] * n
            rdy_t = [0.0] * n
            blevel = [0.0] * n
            for i in range(n - 1, -1, -1):
                o = self.all[i]
                c = self._cost(o) + (self._dma_latency(o) if o["dma"] else 0.0)
                m = 0.0
                for k in succs[i]:
                    if blevel[k] > m:
                        m = blevel[k]
                blevel[i] = c + m
            efree = {e: 0.0 for e in ENGINES}
            dma_free = [0.0]
            ready = [i for i in range(n) if npred[i] == 0]
            allowed = None
            if self.SCHED_PH:
                allowed = {float(x) for x in self.SCHED_PH.split(",")}
            done = [False] * n
            lowest = 0
            order = []
            LAT = 0.25
            while len(order) < n:
                while lowest < n and done[lowest]:
                    lowest += 1
                ph = self.all[lowest]["ph"]
                lim = lowest + self.WINDOW
                if allowed is not None and ph not in allowed:
                    lim = lowest + 1
                elif self.BARRIERS:
                    k = lowest
                    while k < n and k < lim and self.all[k]["ph"] == ph:
                        k += 1
                    lim = k
                best, best_t, bkey = None, None, None
                cands = []
                tmin = None
                for i in ready:
                    if i >= lim:
                        continue
                    t = max(efree[self.all[i]["eng"]], rdy_t[i])
                    cands.append((t, i))
                    if tmin is None or t < tmin:
                        tmin = t
                for t, i in cands:
                    if t <= tmin + self.SLACK:
                        key = (-blevel[i], i)
                        if best is None or key < bkey:
                            best, best_t, bkey = i, t, key
                if best is None:
                    best = min(ready)
                    best_t = max(efree[self.all[best]["eng"]], rdy_t[best])
                o = self.all[best]
                c = self._cost(o)
                efree[o["eng"]] = best_t + c
                if o["dma"]:
                    st = max(best_t + c, dma_free[0])
                    lat = self._dma_latency(o)
                    dma_free[0] = st + (lat - 2.0)
                    finish[best] = st + lat
                else:
                    finish[best] = best_t + c
                done[best] = True
                ready.remove(best)
                order.append(best)
                for k in succs[best]:
                    npred[k] -= 1
                    rdy_t[k] = max(rdy_t[k], finish[best] + LAT)
                    if npred[k] == 0:
                        ready.append(k)
        if self.SCHED and _env("KVERBOSE", ""):
            print("sched makespan(us)", max(finish), "window", self.WINDOW, "slack", self.SLACK)
        tok = [None] * n
        seen = {e: {} for e in ENGINES}
        dma_rr = {"sp": 0, "pool": 0}
        pe_seq = []
        for i in order:
            o = self.all[i]
            eng = o["eng"]
            need = {}

            def want(t):
                if need.get(t[0], 0) < t[1]:
                    need[t[0]] = t[1]

            for j in preds[i]:
                assert tok[j] is not None, "dependency scheduled after its consumer"
                if eng == "pe" and self.all[j]["eng"] == "pe":
                    continue
                if j in order_only[i]:
                    continue
                want(tok[j])
            if o["dma"]:
                k = dma_rr[eng]
                dma_rr[eng] = (k + 1) % N_DMA_SEMS
                skey = f"dma_{eng}_{k}"
                last = self.dma_last.get(skey, 0)
                if last:
                    want((skey, last))
                tok[i] = (skey, last + 16)
                self.dma_last[skey] = last + 16
                inc = (skey, 16)
            else:
                self.count[eng] += 1
                tok[i] = (f"e_{eng}", self.count[eng])
                inc = (f"e_{eng}", 1)
                if eng == "pe":
                    pe_seq.append(i)
            waits = []
            sn = seen[eng]
            for sk, v in need.items():
                if sn.get(sk, 0) < v:
                    sn[sk] = v
                    waits.append((sk, v))
            self.ops[eng].append((waits, o["fn"], inc))
        info = []
        for i in pe_seq:
            o = self.all[i]
            r, w = o["raw_r"][0], o["raw_w"][0]
            info.append(((r[1] // 32, (r[2] + 31) // 32), (w[3] // 2048, (w[4] + 2047) // 2048)))
        for a in range(len(info)):
            for b in range(a + 1, min(a + 5, len(info))):
                (a0, a1), (b0, b1) = info[a]
                (c0, c1), (d0, d1) = info[b]
                if (a1 <= c0 or c1 <= a0) and (b0 < d1 and d0 < b1):
                    raise RuntimeError(f"PE row-group/bank hazard between PE ops {a} and {b}: {info[a]} {info[b]}")

    def emit(self, nc, es):
        keys = [f"e_{e}" for e in ENGINES]
        for q in ("sp", "pool"):
            keys += [f"dma_{q}_{i}" for i in range(N_DMA_SEMS)]
        sems = {k: es.enter_context(nc.semaphore(k)) for k in keys}
        finals = [(f"e_{e}", self.count[e]) for e in ENGINES if self.count[e]]
        finals += list(self.dma_last.items())
        ops = self.ops
        block = es.enter_context(nc.Block())

        def run(engine_obj, name, last=False):
            for waits, fn, inc in ops[name]:
                for s, v in waits:
                    engine_obj.wait_ge(sems[s], v)
                fn(engine_obj).then_inc(sems[inc[0]], inc[1])
            if last:
                for s, v in finals:
                    engine_obj.wait_ge(sems[s], v)

        @block.tensor
        def _(e):
            run(e, "pe")

        @block.scalar
        def _(e):
            run(e, "act")

        @block.vector
        def _(e):
            run(e, "dve")

        @block.gpsimd
        def _(e):
            run(e, "pool")

        @block.sync
        def _(e):
            run(e, "sp", last=True)


D = 1024
KC = 8
NH = 8
QK = 96
GH = 4
DFF = 2816
NFF = DFF // 128
EPS = 1e-6
SB_BYTES = 204 * 1024
NPRE = 12
SCALE = QK ** -0.5

K1 = 1024
A_BASE = 0
C_BASE = 26 * K1
D_BASE = 78 * K1
E_BASE = 119 * K1


def build_program():
    nc = bass.Bass("TRN2", target_bir_lowering=False)
    P = Prog()
    _kstop = float(_env("KSTOP", "99"))

    def phase(n):
        P.enabled = n <= _kstop
        P.cur_phase = float(n)

    def din(name, shape):
        return nc.dram_tensor(name, list(shape), F32, kind="ExternalInput").ap()

    def dout(name, shape):
        return nc.dram_tensor(name, list(shape), F32, kind="ExternalOutput").ap()

    xp_d = din("xp", [512, D])
    xpre_d = din("xs_pre", [NPRE * 128, D])
    xown_d = din("xs_own", [512, D])
    ckvctx_d = din("ckv_ctx", [256, 256])
    krctx_d = din("kr_ctx", [256, 32])
    cosk_d = din("cosk", [2048, 32])
    sink_d = din("sink", [2048, 32])
    cosq_d = din("cosq", [32, 512])
    sinq_d = din("sinq", [32, 512])
    r0_d = din("R0", [2, 128, 128])
    sb0_d = din("SB0", [2, 128, 128])
    init_d = din("INIT", [13, 2, 128, 128])
    keepcap_d = din("keepcap", [128, 26])
    wstep_d = din("Wstep", [33, NPRE * 256])
    wown_d = din("Wown", [33, 512])
    condT_d = din("condT", [128, 16])
    wada_d = din("w_ada", [D, 6 * D])
    bada_d = din("b_ada", [1, 6 * D])
    nattn_d = din("norm_attn_b", [128, D])
    nffn_d = din("norm_ffn_b", [128, D])
    win_d = din("w_in", [D, 2272])
    wuq_d = din("w_uq", [384, 768])
    wuqp_d = din("w_uqp", [384, 768])
    wukv_d = din("w_ukv2", [256, 1024])
    wout_d = din("w_out", [D, D])
    wffi_d = din("w_ffn_in", [D, 2 * DFF])
    wffo_d = din("w_ffn_out", [DFF, D])
    kvn_d = din("kv_norm_b", [128, 256])
    qn_d = din("q_norm_b", [128, 384])
    gla4_d = din("gla_norm4_b", [128, 512])
    fin_d = din("final_norm_b", [128, D])
    ident_d = din("ident", [128, 128])
    tri_d = din("tri", [128, 4 * 128])
    mask_d = din("mask", [128, 2 * 128])
    yp_d = dout("y_p", [512, D])
    ys_d = dout("y_s", [512, D])
    nkv_d = dout("new_kv", [512, 256])
    nkr_d = dout("new_kr", [512, 32])
    nsf_d = dout("new_sf", [2, 2, 128, 128])
    nsb_d = dout("new_sb", [2, 2, 128, 128])
    mod2_d = nc.dram_tensor("mod2_scr", [8, 128, D], F32).ap()

    es = ExitStack()
    with es:
        sb = es.enter_context(nc.sbuf_tensor("sb", [128, SB_BYTES // 4], F32))
        ps = es.enter_context(nc.psum_tensor("ps", [128, 4096], F32))
        sbF = sb[:]
        sbB = sb[:].bitcast(BF16)
        psF = ps[:]
        psB = ps[:].bitcast(BF16)

        def SF(off, n, p=128):
            assert off % 4 == 0 and off + 4 * n <= SB_BYTES, (off, n)
            return T("sb", sbF, 4, 0, p, off // 4, off // 4 + n)

        def SBf(off, n, p=128):
            assert off % 2 == 0 and off + 2 * n <= SB_BYTES, (off, n)
            return T("sb", sbB, 2, 0, p, off // 2, off // 2 + n)

        def PF(bank, n=512, c0=0, p=128):
            return T("ps", psF, 4, 0, p, bank * 512 + c0, bank * 512 + c0 + n)

        def PB(bank, n=1024, c0=0, p=128):
            return T("ps", psB, 2, 0, p, bank * 1024 + c0, bank * 1024 + c0 + n)

        class Bump:
            def __init__(self, base, limit):
                self.cur, self.limit = base, limit

            def f(self, n, p=128):
                self.cur = (self.cur + 63) // 64 * 64
                t = SF(self.cur, n, p)
                self.cur += 4 * n
                assert self.cur <= self.limit, (self.cur, self.limit)
                return t

            def b(self, n, p=128):
                self.cur = (self.cur + 63) // 64 * 64
                t = SBf(self.cur, n, p)
                self.cur += 2 * n
                assert self.cur <= self.limit, (self.cur, self.limit)
                return t

        def mm(out, lhsT, rhs, start, stop):
            P.op("pe", lambda e: e.matmul(out=out.ap, lhsT=lhsT.ap, rhs=rhs.ap, start=start, stop=stop),
                 reads=[lhsT, rhs], writes=[out])

        def tr(out, in_, ident):
            P.op("pe", lambda e: e.transpose(out=out.ap, in_=in_.ap, identity=ident.ap),
                 reads=[in_, ident], writes=[out])

        def act(out, in_, func, scale=None, bias=None, accum=None, oap=None, iap=None):
            kw = {}
            if scale is not None:
                kw["scale"] = scale.ap if isinstance(scale, T) else scale
            if bias is not None:
                kw["bias"] = bias
            if accum is not None:
                kw["accum_out"] = accum.ap
            rd = [in_] + ([scale] if isinstance(scale, T) else [])
            wr = [out] + ([accum] if accum is not None else [])
            o = oap if oap is not None else out.ap
            i = iap if iap is not None else in_.ap
            P.op("act", lambda e: e.activation(out=o, in_=i, func=func, **kw), reads=rd, writes=wr)

        def ts(out, in0, s1, s2, op0, op1=None, eng="dve", oap=None, iap=None):
            rd = [in0] + [s for s in (s1, s2) if isinstance(s, T)]
            a1 = s1.ap if isinstance(s1, T) else s1
            a2 = s2.ap if isinstance(s2, T) else s2
            o = oap if oap is not None else out.ap
            i = iap if iap is not None else in0.ap
            if op1 is None:
                fn = lambda e: e.tensor_scalar(out=o, in0=i, scalar1=a1, scalar2=None, op0=op0)
            else:
                fn = lambda e: e.tensor_scalar(out=o, in0=i, scalar1=a1, scalar2=a2, op0=op0, op1=op1)
            P.op(eng, fn, reads=rd, writes=[out])

        def tt(out, a, b, op, eng="dve", oap=None, aap=None, bap=None):
            o = oap if oap is not None else out.ap
            x = aap if aap is not None else a.ap
            y = bap if bap is not None else b.ap
            P.op(eng, lambda e: e.tensor_tensor(out=o, in0=x, in1=y, op=op), reads=[a, b], writes=[out])

        def stt(out, a, s, b, op0, op1):
            sa = s.ap if isinstance(s, T) else s
            rd = [a, b] + ([s] if isinstance(s, T) else [])
            P.op("dve", lambda e: e.scalar_tensor_tensor(out=out.ap, in0=a.ap, scalar=sa, in1=b.ap,
                                                         op0=op0, op1=op1), reads=rd, writes=[out])

        def cp(out, in_, eng="dve", oap=None, iap=None):
            o = oap if oap is not None else out.ap
            i = iap if iap is not None else in_.ap
            if eng == "act":
                P.op("act", lambda e: e.activation(out=o, in_=i, func=AF.Copy), reads=[in_], writes=[out])
            else:
                P.op(eng, lambda e: e.tensor_copy(out=o, in_=i), reads=[in_], writes=[out])

        def recip(out, in_):
            P.op("dve", lambda e: e.reciprocal(out=out.ap, in_=in_.ap), reads=[in_], writes=[out])

        def mset(out, val, eng="pool", oap=None):
            o = oap if oap is not None else out.ap
            P.op(eng, lambda e: e.memset(o, val), writes=[out])

        def dma(out_ap, in_ap, reads=(), writes=(), q="sp"):
            P.op(q, lambda e: e.dma_start(out=out_ap, in_=in_ap), reads=reads, writes=writes, dma=True)

        def rstd_from_ss(rstd, ss, tmp1, tmp2, n):
            ts(tmp1, ss, 1.0 / n, EPS, ALU.mult, ALU.add)
            act(tmp2, tmp1, AF.Ln)
            act(rstd, tmp2, AF.Exp, scale=-0.5)

        A = Bump(A_BASE, C_BASE)
        ident = A.b(128)
        maskf = A.b(128)
        maskb = A.b(128)
        ones_bf = A.b(512)
        tri = A.f(512)
        keepcap = A.f(26)
        kvn_b = A.f(256)
        qn_b = A.f(384)
        gla4_b = A.f(512)
        tiny = A.f(192)
        _tslot = [0]

        def tslot():
            k = _tslot[0] % 40
            _tslot[0] += 1
            return tuple(tiny[:, 32 + 4 * k + c: 33 + 4 * k + c] for c in range(4))
        condT = A.f(16)
        ones_fA = A.f(64)
        scond = A.f(16)
        assert A.cur <= 10 * K1, A.cur
        mods = [SF(10 * K1 + 4096 * i, 1024) for i in range(4)]
        SH1, GM1 = 0, 1

        tri_f, tri_b, tris_f, tris_b = (tri[:, 128 * i:128 * (i + 1)] for i in range(4))

        dma(ident.ap, ident_d, writes=[ident], q="pool")
        dma(maskf.ap, mask_d[:, 0:128], writes=[maskf], q="pool")
        dma(maskb.ap, mask_d[:, 128:256], writes=[maskb], q="pool")
        dma(tri.ap, tri_d, writes=[tri])
        dma(keepcap.ap, keepcap_d, writes=[keepcap])
        dma(kvn_b.ap, kvn_d, writes=[kvn_b])
        dma(qn_b.ap, qn_d, writes=[qn_b])
        dma(gla4_b.ap, gla4_d, writes=[gla4_b])
        dma(condT.ap, condT_d, writes=[condT])
        mset(ones_bf, 1.0)
        mset(ones_fA, 1.0)

        Cb = Bump(C_BASE, D_BASE)
        w_in = Cb.b(KC * 2272)
        w_uq = Cb.b(3 * 768)
        w_uqp = Cb.b(3 * 768)
        w_own = Cb.b(512, p=33)
        w_step = Cb.b(NPRE * 256, p=33)

        def w_in_s(kc, c0, c1):
            d0, d1 = _win_col(c0, c1)
            return w_in[:, kc * 2272 + d0: kc * 2272 + d1]

        phase(0.5)
        E = Bump(E_BASE, SB_BYTES)
        wada_blk = [E.b(KC * 1024) for _ in range(2)]
        bada_blk = [E.b(1024, p=1) for _ in range(2)]
        lhs_rep = E.b(2 * KC * 128)
        nrm_b = [E.f(1024) for _ in range(2)]
        mstage = [E.f(1024) for _ in range(2)]
        sig = E.f(16)
        dma(nrm_b[0].ap, nattn_d, writes=[nrm_b[0]])
        dma(nrm_b[1].ap, nffn_d, writes=[nrm_b[1]])
        act(sig, condT, AF.Exp, scale=-1.0)
        ts(sig, sig, 1.0, None, ALU.add)
        recip(scond, sig)
        tt(scond, scond, condT, ALU.mult)
        ones_f = E.f(128)
        mset(ones_f, 1.0)
        for r in range(2):
            for kc in range(KC):
                dst = lhs_rep[:, (r * KC + kc) * 128:(r * KC + kc + 1) * 128]
                ts(dst, ones_f, scond[:, kc * 2 + r: kc * 2 + r + 1], None, ALU.mult)

        def load_wada(kind):
            buf = wada_blk[kind % 2]
            dma(buf.v3(KC), wada_d[:, kind * 1024:(kind + 1) * 1024].rearrange("(k p) n -> p k n", p=128),
                writes=[buf], q="pool")
            bb = bada_blk[kind % 2]
            dma(bb.ap, bada_d[:, kind * 1024:(kind + 1) * 1024], writes=[bb], q="pool")

        load_wada(0)
        pending_w = []

        def queue_weight_loads():
            for (d0, d1) in ((0, _WIN_SPLIT), (_WIN_SPLIT, 2272)):
                dma(w_in.v3(KC)[:, :, d0:d1], win_d[:, d0:d1].rearrange("(k p) n -> p k n", p=128),
                    writes=[w_in[:, kc * 2272 + d0: kc * 2272 + d1] for kc in range(KC)], q="pool")
            dma(w_uq.v3(3), wuq_d.rearrange("(k p) n -> p k n", p=128), writes=[w_uq], q="pool")
            dma(w_uqp.v3(3), wuqp_d.rearrange("(k p) n -> p k n", p=128), writes=[w_uqp], q="pool")
            dma(w_own.ap, wown_d, writes=[w_own], q="pool")
            dma(w_step.ap, wstep_d, writes=[w_step], q="pool")

        mod_kind = 0
        for blk in range(12):
            kind = blk // 2
            half = blk % 2
            if blk == 2:
                queue_weight_loads()
            if half == 0 and kind + 1 < 6:
                load_wada(kind + 1)
            buf = wada_blk[kind % 2]
            bb = bada_blk[kind % 2]
            for r in range(2):
                pt = PF(r * 2 + half % 2)
                for kc in range(KC):
                    mm(pt, lhs_rep[:, (r * KC + kc) * 128:(r * KC + kc + 1) * 128],
                       buf[:, kc * 1024 + half * 512: kc * 1024 + (half + 1) * 512], kc == 0, False)
                mm(pt, ones_bf[0:1, 0:128], bb[:, half * 512:(half + 1) * 512], False, True)
                cs = slice(half * 512, (half + 1) * 512)
                if kind < 2:
                    dst = mods[r * 2 + kind][:, cs]
                else:
                    dst = mstage[r][:, cs]
                if kind in (1, 4):
                    nb = nrm_b[0 if kind == 1 else 1][:, cs]
                    stt(dst, pt, 1.0, nb, ALU.add, ALU.mult)
                else:
                    cp(dst, pt, eng="act")
                if kind >= 2 and half == 1:
                    mi = 6 + r if kind == 2 else (kind - 3) * 2 + r
                    dma(mod2_d[mi], mstage[r].ap, reads=[mstage[r]],
                        writes=[("mod2", 0, 1, mi * 10, mi * 10 + 10)])

        Db = Bump(D_BASE, E_BASE)
        ckvT_s = Db.b(2 * 2304)
        krt_s = Db.b(2304, p=96)
        qT_s = Db.b(NH * 512, p=96)
        ogT_s = Db.b(GH * 512)
        ckvT_p = Db.b(2 * 512)
        krt_p = Db.b(512, p=96)
        qT_p = Db.b(NH * 512, p=96)
        ogT_p = Db.b(GH * 512)

        E = Bump(E_BASE, SB_BYTES)
        xst = [E.f(1024) for _ in range(2)]
        junk = E.f(1024)
        junkA = E.b(1024)
        hbf = [E.b(1024) for _ in range(2)]
        hT = E.b(KC * 512)
        kvst = [E.b(384) for _ in range(2)]
        kvf = E.f(320)
        ckvf = [E.f(256) for _ in range(2)]
        krf = [E.f(32) for _ in range(2)]
        ropet = E.f(64)
        cosk_t = [E.f(32) for _ in range(2)]
        sink_t = [E.f(32) for _ in range(2)]
        U1 = E.f(2048)
        cosq_t = T("sb", sbF, 4, 0, 96, U1.c0, U1.c0 + 512)
        sinq_t = T("sb", sbF, 4, 0, 96, U1.c0 + 512, U1.c0 + 1024)
        ropeq1 = T("sb", sbF, 4, 0, 96, U1.c0 + 1024, U1.c0 + 1536)
        ropeq2 = T("sb", sbF, 4, 0, 96, U1.c0 + 1536, U1.c0 + 2048)
        U2 = E.f(1536)
        _o2 = U2.c0 * 4
        qlat = SF(_o2, 384)
        qnb = SBf(_o2 + 1536, 384)
        qnT = SBf(_o2 + 2304, 3 * 512)
        Gg = SF(_o2, 512)
        osum = SF(_o2 + 2048, 512)
        ogb = SBf(_o2 + 4096, 512)
        lowsT = E.b(512, p=33)
        Lg = E.f(512)
        ktok = E.f(256)
        vbf = [E.b(512) for _ in range(4)]
        kdf = [E.b(256) for _ in range(4)]
        kdb = [E.b(256) for _ in range(4)]
        qTf = [E.f(512) for _ in range(2)]
        kTf = [E.f(512) for _ in range(2)]
        Ef = E.f(256)
        eg = E.f(512)
        Einv = eg[:, 0:256]
        expD = eg[:, 256:512]
        qeT = {d: [E.b(512) for _ in range(2)] for d in "fb"}
        keT = {d: [E.b(512) for _ in range(2)] for d in "fb"}
        dec = {d: E.f(8) for d in "fb"}
        attm = [E.b(128) for _ in range(4)]
        assert attm[1].c0 == attm[0].c1 and attm[3].c0 == attm[2].c1 and maskb.c0 == maskf.c1
        attm2 = [SBf(attm[0].c0 * 2, 256), SBf(attm[2].c0 * 2, 256)]
        mask2 = SBf(maskf.c0 * 2, 256)
        o_f = [U1[:, 512 * i:512 * (i + 1)] for i in range(4)]
        Sst = {d: [[E.f(128) for _ in range(2)] for _ in range(2)] for d in "fb"}
        Sbf = {d: [E.b(128) for _ in range(2)] for d in "fb"}
        _u = [U1[:, 128 * i:128 * (i + 1)] for i in range(14)]
        Rst = [[_u[0], _u[1]], [_u[2], _u[3]]]
        Rin = [_u[4], _u[5]]
        Sbacc = [[_u[6], _u[7]], [_u[8], _u[9]]]
        initb = [[_u[10], _u[11]], [_u[12], _u[13]]]
        decp = E.f(2)
        ssq = E.f(4)

        def norm1_tile(x_src_ap, gi, tix, modr, xbuf):
            xt = xst[xbuf]
            dma(xt.ap, x_src_ap, writes=[xt])
            ss, t1, t2, rs = tslot()
            act(junkA, xt, AF.Square, accum=ss)
            rstd_from_ss(rs, ss, t1, t2, D)
            hb = hbf[xbuf]
            stt(junk, xt, rs, mods[modr * 2 + GM1], ALU.mult, ALU.mult)
            tt(hb, junk, mods[modr * 2 + SH1], ALU.add)
            pt = PB(4)
            for kc in range(KC):
                tr(pt[:, kc * 128:(kc + 1) * 128], hb[:, kc * 128:(kc + 1) * 128], ident)
            cp(hT, pt, eng="act",
               oap=hT.v3(KC)[:, :, tix * 128:(tix + 1) * 128], iap=pt.v3(KC))

        def kv_tile(tix, rope_row0, ckvT, krt, key0, prompt_out_row=None, kbuf=0):
            pk = PF(5, 320)
            for kc in range(KC):
                mm(pk[:, 0:288], hT[:, kc * 512 + tix * 128: kc * 512 + (tix + 1) * 128],
                   w_in_s(kc, 384, 672), kc == 0, kc == KC - 1)
            if rope_row0 is not None:
                for kc in range(KC):
                    mm(pk[:, 288:320], hT[:, kc * 512 + tix * 128: kc * 512 + (tix + 1) * 128],
                       w_in_s(kc, 2240, 2272), kc == 0, kc == KC - 1)
            ncv = 320 if rope_row0 is not None else 288
            cp(kvf[:, 0:ncv], pk[:, 0:ncv], eng="act")
            ss, t1, t2, rs = tslot()
            act(junkA[:, 0:256], kvf[:, 0:256], AF.Square, accum=ss)
            rstd_from_ss(rs, ss, t1, t2, 256)
            st = kvst[kbuf]
            if prompt_out_row is not None:
                cf = ckvf[kbuf]
                stt(cf, kvf[:, 0:256], rs, kvn_b, ALU.mult, ALU.mult)
                cp(st[:, 0:256], cf, eng="pool")
                dma(nkv_d[prompt_out_row:prompt_out_row + 128, :], cf.ap, reads=[cf])
                kf = krf[kbuf]
                cp(kf, kvf[:, 256:288], eng="pool")
                cp(st[:, 320:352], kvf[:, 256:288], eng="pool")
                dma(nkr_d[prompt_out_row:prompt_out_row + 128, :], kf.ap, reads=[kf])
            else:
                stt(st[:, 0:256], kvf[:, 0:256], rs, kvn_b, ALU.mult, ALU.mult)
                ck, sk = cosk_t[kbuf], sink_t[kbuf]
                dma(ck.ap, cosk_d[rope_row0:rope_row0 + 128, :], writes=[ck])
                dma(sk.ap, sink_d[rope_row0:rope_row0 + 128, :], writes=[sk])
                tt(ropet[:, 0:32], kvf[:, 256:288], ck, ALU.mult)
                tt(ropet[:, 32:64], kvf[:, 288:320], sk, ALU.mult)
                tt(st[:, 320:352], ropet[:, 0:32], ropet[:, 32:64], ALU.add)
            kv_transposes(st, ckvT, krt, key0)

        def kv_transposes(st, ckvT, krt, key0):
            nk = (ckvT.c1 - ckvT.c0) // 2
            pt = PB(4, 384)
            tr(pt[:, 0:128], st[:, 0:128], ident)
            tr(pt[:, 128:256], st[:, 128:256], ident)
            tr(pt[0:96, 256:384], st[:, 256:352], ident)
            cp(ckvT[:, key0:key0 + 128], pt[:, 0:128], eng="act")
            cp(ckvT[:, nk + key0:nk + key0 + 128], pt[:, 128:256], eng="act")
            cp(krt[64:96, key0:key0 + 128], pt[64:96, 256:384], eng="dve")

        def gla_common_tile(tix, wg, ncol, slot):
            pa = PF(0, 512)
            pb = PF(1, 256)
            for kc in range(KC):
                lhs = hT[:, kc * 512 + tix * 128: kc * 512 + (tix + 1) * 128]
                mm(pa, lhs, w_in_s(kc, 1184, 1696), kc == 0, kc == KC - 1)
            for kc in range(KC):
                lhs = hT[:, kc * 512 + tix * 128: kc * 512 + (tix + 1) * 128]
                mm(pb, lhs, w_in_s(kc, 928, 1184), kc == 0, kc == KC - 1)
            cp(vbf[slot], pa, eng="act")
            cp(ktok, pb, eng="dve")
            pg = PF(5, ncol)
            mm(pg, lowsT[:, tix * 128:(tix + 1) * 128], wg, True, True)
            act(eg[:, 0:ncol], pg, AF.Exp, scale=-1.0)
            act(Lg[:, 0:ncol], eg[:, 0:ncol], AF.Ln, bias=1.0)

        def lows_group():
            pl = PF(6, 512)
            for kc in range(KC):
                mm(pl[0:32, :], w_in_s(kc, 1696, 1728), hT[:, kc * 512:(kc + 1) * 512], kc == 0, kc == KC - 1)
            cp(lowsT[0:32, :], pl[0:32, :], eng="dve")

        mset(lowsT[32:33, :], 1.0)
        for _k in range(2):
            mset(kvst[_k], 0.0)

        phase(1)
        for pr in range(2):
            dma(Rst[0][pr].ap, r0_d[pr], writes=[Rst[0][pr]])
            dma(Sbacc[0][pr].ap, sb0_d[pr], writes=[Sbacc[0][pr]])
        for t in range(2):
            cst = ckvf[t]
            dma(cst.ap, ckvctx_d[t * 128:(t + 1) * 128, :], writes=[cst])
            kf = krf[t]
            dma(kf.ap, krctx_d[t * 128:(t + 1) * 128, :], writes=[kf])
            st = kvst[t]
            cp(st[:, 0:256], cst, eng="pool")
            cp(st[:, 320:352], kf, eng="pool")
            kv_transposes(st, ckvT_s, krt_s, t * 128)

        rp = 0
        for g in range(3):
            for tix in range(4):
                j = g * 4 + tix
                norm1_tile(xpre_d[j * 128:(j + 1) * 128, :], g, tix, 1, j % 2)
            lows_group()
            for tix in range(4):
                j = g * 4 + tix
                kv_tile(tix, j * 128, ckvT_s, krt_s, 256 + j * 128, kbuf=j % 2)
                for pr in range(2):
                    dma(initb[j % 2][pr].ap, init_d[j, pr], writes=[initb[j % 2][pr]])
                gla_common_tile(tix, w_step[:, j * 256:(j + 1) * 256], 256, 0)
                pd = PF(6, 256)
                P.op("pe", lambda e, pd=pd: e.matmul(out=pd.ap, lhsT=tris_f.ap, rhs=Lg[:, 0:256].ap,
                                                     start=True, stop=True),
                     reads=[tris_f, Lg[:, 0:256]], writes=[pd])
                pdec = PF(7, 2)
                for pr in range(2):
                    P.op("pe", lambda e, pr=pr, pdec=pdec: e.matmul(
                        out=pdec[:, pr:pr + 1].ap, lhsT=Lg[:, pr * 128:(pr + 1) * 128].ap,
                        rhs=tri_f[:, 127:128].ap, start=True, stop=True),
                        reads=[Lg[:, pr * 128:(pr + 1) * 128], tri_f], writes=[pdec[:, pr:pr + 1]])
                act(expD, pd, AF.Exp)
                act(decp, pdec, AF.Exp)
                tt(kdf[0], ktok, expD, ALU.mult)
                for pr in range(2):
                    pu = PF(2 + pr, 256)
                    mm(pu, kdf[0][:, pr * 128:(pr + 1) * 128], vbf[0][:, pr * 256:(pr + 1) * 256], True, True)
                    Rold = Rst[rp][pr]
                    Rnew = Rst[1 - rp][pr]
                    stt(Rin[pr], Rold, keepcap[:, j:j + 1], initb[j % 2][pr], ALU.mult, ALU.add)
                    stt(Sbacc[1 - rp][pr], Rold, keepcap[:, 13 + j:14 + j], Sbacc[rp][pr], ALU.mult, ALU.add)
                    for hh in range(2):
                        rows = slice(hh * 64, (hh + 1) * 64)
                        stt(Rnew[rows, :], Rin[pr][rows, :], decp[rows, pr:pr + 1],
                            pu[rows, hh * 128:(hh + 1) * 128], ALU.mult, ALU.add)
                rp = 1 - rp
        for pr in range(2):
            dma(initb[0][pr].ap, init_d[12, pr], writes=[initb[0][pr]])
            stt(Sst["f"][0][pr], Rst[rp][pr], keepcap[:, 12:13], initb[0][pr], ALU.mult, ALU.add)
            stt(Sst["b"][0][pr], Rst[rp][pr], keepcap[:, 25:26], Sbacc[rp][pr], ALU.mult, ALU.add)

        def own_group(x_d, modr, is_prompt, ckvT, krt, key0s, qT, ogT, seqs):
            for tix in range(4):
                norm1_tile(x_d[tix * 128:(tix + 1) * 128, :], 0, tix, modr, tix % 2)
            lows_group()
            for tix in range(4):
                if is_prompt:
                    kv_tile(tix, None, ckvT, krt, key0s[tix], prompt_out_row=tix * 128, kbuf=tix % 2)
                else:
                    kv_tile(tix, 1536 + tix * 128, ckvT, krt, key0s[tix], kbuf=tix % 2)
            for tix in range(4):
                pq = PF(0, 384)
                for kc in range(KC):
                    mm(pq, hT[:, kc * 512 + tix * 128: kc * 512 + (tix + 1) * 128],
                       w_in_s(kc, 0, 384), kc == 0, kc == KC - 1)
                cp(qlat, pq, eng="act")
                ss, t1, t2, rs = tslot()
                act(junkA[:, 0:384], qlat, AF.Square, accum=ss)
                rstd_from_ss(rs, ss, t1, t2, 384)
                stt(qnb, qlat, rs, qn_b, ALU.mult, ALU.mult)
                pt = PB(4, 384)
                for c3 in range(3):
                    tr(pt[:, c3 * 128:(c3 + 1) * 128], qnb[:, c3 * 128:(c3 + 1) * 128], ident)
                cp(qnT, pt, eng="act", oap=qnT.v3(3)[:, :, tix * 128:(tix + 1) * 128], iap=pt.v3(3))
            if not is_prompt:
                dma(cosq_t[64:96, :].ap, cosq_d, writes=[cosq_t[64:96, :]])
                dma(sinq_t[64:96, :].ap, sinq_d, writes=[sinq_t[64:96, :]])
            for h in range(NH):
                pa = PF(2 + (h % 2) * 2, 512)
                for c3 in range(3):
                    mm(pa[0:96, :], w_uq[:, c3 * 768 + h * 96: c3 * 768 + (h + 1) * 96],
                       qnT[:, c3 * 512:(c3 + 1) * 512], c3 == 0, c3 == 2)
                dst = qT[:, h * 512:(h + 1) * 512]
                if is_prompt:
                    cp(dst[0:96, :], pa[0:96, :], eng="act")
                else:
                    pb = PF(3 + (h % 2) * 2, 512)
                    for c3 in range(3):
                        mm(pb[0:96, :], w_uqp[:, c3 * 768 + h * 96: c3 * 768 + (h + 1) * 96],
                           qnT[:, c3 * 512:(c3 + 1) * 512], c3 == 0, c3 == 2)
                    cp(dst[0:64, :], pa[0:64, :], eng="act")
                    tt(ropeq1[64:96, :], pa[64:96, :], cosq_t[64:96, :], ALU.mult)
                    tt(ropeq2[64:96, :], pb[64:96, :], sinq_t[64:96, :], ALU.mult)
                    tt(dst[64:96, :], ropeq1[64:96, :], ropeq2[64:96, :], ALU.add)
            for pr in range(2):
                pq = PF(2 + pr, 512)
                for kc in range(KC):
                    mm(pq, w_in_s(kc, 672 + pr * 128, 672 + (pr + 1) * 128), hT[:, kc * 512:(kc + 1) * 512],
                       kc == 0, kc == KC - 1)
                act(qTf[pr], pq, AF.Copy, scale=0.125)
                pk = PF(6 + pr, 512)
                for kc in range(KC):
                    mm(pk, w_in_s(kc, 928 + pr * 128, 928 + (pr + 1) * 128), hT[:, kc * 512:(kc + 1) * 512],
                       kc == 0, kc == KC - 1)
                cp(kTf[pr], pk, eng="dve")
            for tix in range(4):
                gla_common_tile(tix, w_own, 512, tix)
                tcs = slice(tix * 128, (tix + 1) * 128)
                for di, d in enumerate("fb"):
                    Lc = Lg[:, di * 256:(di + 1) * 256]
                    trim = tri_f if d == "f" else tri_b
                    tris = tris_f if d == "f" else tris_b
                    pc = PF(6, 256)
                    for pr in range(2):
                        P.op("pe", lambda e, pr=pr, pc=pc, Lc=Lc, trim=trim: e.matmul(
                            out=pc[:, pr * 128:(pr + 1) * 128].ap, lhsT=Lc[:, pr * 128:(pr + 1) * 128].ap,
                            rhs=trim.ap, start=True, stop=True),
                            reads=[Lc[:, pr * 128:(pr + 1) * 128], trim], writes=[pc[:, pr * 128:(pr + 1) * 128]])
                    pd = PF(7, 256)
                    P.op("pe", lambda e, pd=pd, Lc=Lc, tris=tris: e.matmul(
                        out=pd.ap, lhsT=tris.ap, rhs=Lc.ap, start=True, stop=True),
                        reads=[tris, Lc], writes=[pd])
                    act(Ef, pc, AF.Exp)
                    act(Einv, pc, AF.Exp, scale=-1.0)
                    act(expD, pd, AF.Exp)
                    for pr in range(2):
                        ecol = 127 if d == "f" else 0
                        cp(dec[d][:, pr * 4 + tix: pr * 4 + tix + 1],
                           Ef[:, pr * 128 + ecol: pr * 128 + ecol + 1], eng="pool")
                        tt(qeT[d][pr][:, tcs], qTf[pr][:, tcs], Ef[:, pr * 128:(pr + 1) * 128], ALU.mult)
                        tt(keT[d][pr][:, tcs], kTf[pr][:, tcs], Einv[:, pr * 128:(pr + 1) * 128], ALU.mult)
                    tt((kdf if d == "f" else kdb)[tix], ktok, expD, ALU.mult)
            for seq in seqs:
                cur = {"f": 0, "b": 0}
                if is_prompt:
                    for d in "fb":
                        for pr in range(2):
                            mset(Sst[d][0][pr], 0.0)
                for d in "fb":
                    for pr in range(2):
                        cp(Sbf[d][pr], Sst[d][0][pr], eng="pool")

                def state_update(d, tix):
                    kd = (kdf if d == "f" else kdb)[tix]
                    c = cur[d]
                    for pr in range(2):
                        pu = PF(2 + pr, 256)
                        mm(pu, kd[:, pr * 128:(pr + 1) * 128], vbf[tix][:, pr * 256:(pr + 1) * 256], True, True)
                        for hh in range(2):
                            rows = slice(hh * 64, (hh + 1) * 64)
                            stt(Sst[d][1 - c][pr][rows, :], Sst[d][c][pr][rows, :],
                                dec[d][rows, pr * 4 + tix: pr * 4 + tix + 1],
                                pu[rows, hh * 128:(hh + 1) * 128], ALU.mult, ALU.add)
                        cp(Sbf[d][pr], Sst[d][1 - c][pr], eng="pool")
                    cur[d] = 1 - c

                for tix in seq:
                    tcs = slice(tix * 128, (tix + 1) * 128)
                    for h in range(GH):
                        pr, hh = h // 2, h % 2
                        rows = slice(hh * 64, (hh + 1) * 64)
                        for di, d in enumerate("fb"):
                            pat = PF(6 + hh, 128, c0=di * 128)
                            mm(pat, keT[d][pr][rows, tcs], qeT[d][pr][rows, tcs], True, True)
                        tt(attm2[hh], PF(6 + hh, 256), mask2, ALU.mult)
                        po = PF(2 * hh, 512)[:, h * 128:(h + 1) * 128]
                        mm(po, qeT["f"][pr][rows, tcs], Sbf["f"][pr][rows, :], True, False)
                        mm(po, attm2[hh][:, 0:128], vbf[tix][:, h * 128:(h + 1) * 128], False, False)
                        mm(po, attm2[hh][:, 128:256], vbf[tix][:, h * 128:(h + 1) * 128], False, True)
                    for hh in range(2):
                        cs_ = slice(hh * 128, (hh + 1) * 128)
                        pof = PF(2 * hh, 512)
                        cp(o_f[tix], pof, eng="act", oap=o_f[tix].v3(2)[:, :, cs_], iap=pof.v3(2)[:, :, cs_])
                    state_update("f", tix)
                for tix in reversed(seq):
                    tcs = slice(tix * 128, (tix + 1) * 128)
                    pob = [PF(1, 512), PF(3, 512)]
                    for h in range(GH):
                        pr, hh = h // 2, h % 2
                        rows = slice(hh * 64, (hh + 1) * 64)
                        mm(pob[hh][:, h * 128:(h + 1) * 128], qeT["b"][pr][rows, tcs], Sbf["b"][pr][rows, :],
                           True, True)
                    for hh in range(2):
                        cs_ = slice(hh * 128, (hh + 1) * 128)
                        tt(osum, pob[hh], o_f[tix], ALU.add,
                           oap=osum.v3(2)[:, :, cs_], aap=pob[hh].v3(2)[:, :, cs_], bap=o_f[tix].v3(2)[:, :, cs_])
                    state_update("b", tix)
                    pg = PF(5, 512)
                    for kc in range(KC):
                        mm(pg, hT[:, kc * 512 + tix * 128: kc * 512 + (tix + 1) * 128],
                           w_in_s(kc, 1728, 2240), kc == 0, kc == KC - 1)
                    act(eg, pg, AF.Exp, scale=-1.0)
                    act(Lg, eg, AF.Ln, bias=1.0)
                    act(eg, Lg, AF.Exp, scale=-1.0)
                    tt(Gg, pg, gla4_b, ALU.mult)
                    tt(Gg, Gg, eg, ALU.mult)
                    for h in range(GH):
                        act(junkA[:, h * 128:(h + 1) * 128], osum[:, h * 128:(h + 1) * 128], AF.Square,
                            accum=ssq[:, h:h + 1])
                    rstd_from_ss(tiny[:, 16:20], ssq, tiny[:, 20:24], tiny[:, 24:28], 128)
                    for h in range(GH):
                        stt(ogb[:, h * 128:(h + 1) * 128], osum[:, h * 128:(h + 1) * 128], tiny[:, 16 + h:17 + h],
                            Gg[:, h * 128:(h + 1) * 128], ALU.mult, ALU.mult)
                    pt = PB(4, 512)
                    for h in range(GH):
                        tr(pt[:, h * 128:(h + 1) * 128], ogb[:, h * 128:(h + 1) * 128], ident)
                    cp(ogT, pt, eng="act", oap=ogT.v3(GH)[:, :, tcs], iap=pt.v3(GH))
                if is_prompt:
                    si = seqs.index(seq)
                    for pr in range(2):
                        dma(nsf_d[si, pr], Sst["f"][cur["f"]][pr].ap, reads=[Sst["f"][cur["f"]][pr]])
                        dma(nsb_d[si, pr], Sst["b"][cur["b"]][pr].ap, reads=[Sst["b"][cur["b"]][pr]])
                else:
                    pass

        phase(2)
        own_group(xown_d, 1, False, ckvT_s, krt_s, [256 + 1536 + t * 128 for t in range(4)], qT_s, ogT_s,
                  [[0, 1, 2, 3]])
        phase(3)
        own_group(xp_d, 0, True, ckvT_p, krt_p, [0, 128, 256, 384], qT_p, ogT_p, [[0, 1], [2, 3]])

        phase(4)
        Cb = Bump(C_BASE, D_BASE)
        w_ukv = Cb.b(2 * 1024)
        w_outA = Cb.b(NH * 1024, p=64)
        w_outB = Cb.b(GH * 1024)
        kTh = [Cb.b(2304, p=96) for _ in range(2)]
        dma(w_ukv.v3(2), wukv_d.rearrange("(k p) n -> p k n", p=128), writes=[w_ukv], q="pool")
        dma(w_outA.v3(NH), wout_d[0:512, :].rearrange("(h d) n -> d h n", d=64), writes=[w_outA], q="pool")
        dma(w_outB.v3(GH), wout_d[512:1024, :].rearrange("(h e) n -> e h n", e=128), writes=[w_outB], q="pool")

        E = Bump(E_BASE, SB_BYTES)
        x1 = [E.f(1024) for _ in range(8)]
        Vp = E.b(18 * NH * 65)
        attnT = E.b(NH * 512, p=65)
        PT = [E.b(512) for _ in range(4)]
        rden = E.f(512, p=65)
        bcs = E.f(512, p=64)
        mixh = E.f(512)
        gt1s = E.f(1024)
        assert E.cur <= SB_BYTES

        def attention(ckvT, krt, nkeys_list, qT, q_groups):
            nk = (ckvT.c1 - ckvT.c0) // 2
            ntile = nk // 128
            mset(Vp, 1.0, oap=Vp.ap)
            for kt in range(ntile):
                pv = PF(kt % 2, 512)
                for c2 in range(2):
                    mm(pv, ckvT[:, c2 * nk + kt * 128: c2 * nk + (kt + 1) * 128],
                       w_ukv[:, c2 * 1024 + 512: c2 * 1024 + 1024], c2 == 0, c2 == 1)
                dstap = Vp[:, kt * NH * 65:(kt + 1) * NH * 65].v3(NH)[:, :, 0:64]
                cp(Vp[:, kt * NH * 65:(kt + 1) * NH * 65], pv, eng="act" if kt % 2 else "dve",
                   oap=dstap, iap=pv.v3(NH))
            def build(h):
                kt_h = kTh[h % 2]
                for kb in range(0, nk, 512):
                    n = min(512, nk - kb)
                    pk = PF([3, 1][(kb // 512) % 2], n)
                    for c2 in range(2):
                        mm(pk[0:64, :], w_ukv[:, c2 * 1024 + h * 64: c2 * 1024 + (h + 1) * 64],
                           ckvT[:, c2 * nk + kb: c2 * nk + kb + n], c2 == 0, c2 == 1)
                    cp(kt_h[0:64, kb:kb + n], pk[0:64, :], eng="act" if (kb // 512) % 2 else "dve")
                cp(kt_h[64:96, 0:nk], krt[64:96, 0:nk], eng="dve")

            pending = []

            def finish(h, q0, nq, pacc):
                if nq == 512:
                    recip(rden[64:65, 0:nq], pacc[64:65, :])
                else:
                    act(rden[64:65, 256:256 + nq], pacc[64:65, :], AF.Ln)
                    act(rden[64:65, 0:nq], rden[64:65, 256:256 + nq], AF.Exp, scale=-1.0)
                pbc = PF(0, nq)
                mm(pbc[0:64, :], ones_fA[64:65, 0:64], rden[64:65, 0:nq], True, True)
                cp(bcs[:, 0:nq], pbc[0:64, :], eng="act")
                tt(attnT[0:64, h * 512 + q0: h * 512 + q0 + nq], pacc[0:64, :], bcs[:, 0:nq], ALU.mult)

            SCB = [4, 5, 2]
            build(0)
            gi = 0
            for h in range(NH):
                kt_h = kTh[h % 2]
                if h + 1 < NH:
                    build(h + 1)
                for (q0, nq, ktiles) in q_groups:
                    pacc = PF(6 + gi % 2, nq)
                    gi += 1
                    n = len(ktiles)
                    pscs = {}

                    def score(i, kt_h=kt_h, h=h, q0=q0, nq=nq, ktiles=ktiles, pscs=pscs):
                        psc = PF(SCB[i % 3], nq)
                        pscs[i] = psc
                        kt = ktiles[i]
                        mm(psc, kt_h[0:96, kt * 128:(kt + 1) * 128], qT[0:96, h * 512 + q0: h * 512 + q0 + nq],
                           True, True)

                    for i in range(min(2, n)):
                        score(i)
                    while pending:
                        finish(*pending.pop(0))
                    for i, kt in enumerate(ktiles):
                        pt_ = PT[i % 4][:, 0:nq]
                        act(pt_, pscs[i], AF.Exp, scale=SCALE)
                        if i + 2 < n:
                            score(i + 2)
                        mm(pacc[0:65, :], Vp[:, (kt * NH + h) * 65:(kt * NH + h + 1) * 65], pt_,
                           i == 0, i == n - 1)
                    pending.append((h, q0, nq, pacc))
            while pending:
                finish(*pending.pop(0))

        def out_proj(x_d, modr, ogT, x1s):
            dma(gt1s.ap, mod2_d[6 + modr], reads=[("mod2", 0, 1, (6 + modr) * 10, (6 + modr) * 10 + 10)],
                writes=[gt1s])
            for tix in range(4):
                xt = x1s[tix]
                dma(xt.ap, x_d[tix * 128:(tix + 1) * 128, :], writes=[xt])
                for half in range(2):
                    pm = PF(2 + half, 512)
                    cs = slice(half * 512, (half + 1) * 512)
                    for h in range(NH):
                        mm(pm, attnT[0:64, h * 512 + tix * 128: h * 512 + (tix + 1) * 128],
                           w_outA[0:64, h * 1024 + half * 512: h * 1024 + (half + 1) * 512], h == 0, False)
                    for h in range(GH):
                        mm(pm, ogT[:, h * 512 + tix * 128: h * 512 + (tix + 1) * 128],
                           w_outB[:, h * 1024 + half * 512: h * 1024 + (half + 1) * 512], False, h == GH - 1)
                    tt(mixh, pm, gt1s[:, cs], ALU.mult)
                    tt(xt[:, cs], xt[:, cs], mixh, ALU.add)

        wblk = [SBf(70 * K1, KC * 512), SBf(196 * K1, KC * 512)]
        NWB = NFF // 2

        def load_wblk(wb):
            buf = wblk[wb % 2]
            v = buf.v3(KC)
            dma(v[:, :, 0:256], wffi_d[:, wb * 256:(wb + 1) * 256].rearrange("(k p) n -> p k n", p=128),
                writes=[buf], q="pool")
            dma(v[:, :, 256:512],
                wffi_d[:, DFF + wb * 256: DFF + (wb + 1) * 256].rearrange("(k p) n -> p k n", p=128),
                writes=[buf], q="pool")

        load_wblk(0)
        load_wblk(1)
        attention(ckvT_s, krt_s, None, qT_s, [(0, 512, list(range(18)))])
        out_proj(xown_d, 1, ogT_s, x1[0:4])
        phase(5)
        attention(ckvT_p, krt_p, None, qT_p, [(0, 256, [0, 1]), (256, 256, [2, 3])])
        out_proj(xp_d, 0, ogT_p, x1[4:8])

        phase(6)
        w_ffo = SBf(C_BASE, NFF * 1024)
        h2T = SBf(D_BASE, KC * 1024)
        uT = [SBf(94 * K1, NFF * 512), SBf(151 * K1, NFF * 512)]
        Ff = Bump(174 * K1, SB_BYTES)
        silu_t = [Ff.f(512) for _ in range(2)]
        h2b = [Ff.b(1024) for _ in range(2)]
        junk2 = Ff.f(1024)
        junkA2 = Ff.b(1024)
        mods2 = [SF(10 * K1 + 4096 * i, 1024) for i in range(4)] + [Ff.f(1024) for _ in range(2)]
        for i in range(6):
            dma(mods2[i].ap, mod2_d[i], reads=[("mod2", 0, 1, i * 10, i * 10 + 10)], writes=[mods2[i]])

        for t8 in range(8):
            r = 1 if t8 < 4 else 0
            xt = x1[t8]
            ss, t1, t2, rs = tslot()
            act(junkA2, xt, AF.Square, accum=ss)
            rstd_from_ss(rs, ss, t1, t2, D)
            hb = h2b[t8 % 2]
            stt(junk2, xt, rs, mods2[1 * 2 + r], ALU.mult, ALU.mult)
            tt(hb, junk2, mods2[0 * 2 + r], ALU.add)
            pt = PB(4 + t8 % 2)
            for kc in range(KC):
                tr(pt[:, kc * 128:(kc + 1) * 128], hb[:, kc * 128:(kc + 1) * 128], ident)
            cp(h2T, pt, eng="act", oap=h2T.v3(KC)[:, :, t8 * 128:(t8 + 1) * 128], iap=pt.v3(KC))
        for wb in range(NWB):
            buf = wblk[wb % 2]
            for sub in range(2):
                fb = 2 * wb + sub
                for g in range(2):
                    pa = PF(0 + g * 2, 512)
                    pg = PF(1 + g * 2, 512)
                    for kc in range(KC):
                        mm(pa, buf[:, kc * 512 + sub * 128: kc * 512 + sub * 128 + 128],
                           h2T[:, kc * 1024 + g * 512: kc * 1024 + (g + 1) * 512], kc == 0, kc == KC - 1)
                    for kc in range(KC):
                        mm(pg, buf[:, kc * 512 + 256 + sub * 128: kc * 512 + 256 + sub * 128 + 128],
                           h2T[:, kc * 1024 + g * 512: kc * 1024 + (g + 1) * 512], kc == 0, kc == KC - 1)
                    st_ = silu_t[g]
                    act(st_, pa, AF.Silu)
                    tt(uT[g][:, fb * 512:(fb + 1) * 512], st_, pg, ALU.mult)
            if wb + 2 < NWB:
                load_wblk(wb + 2)
            if wb < 3:
                k0, k1 = [(0, 8), (8, 16), (16, NFF)][wb]
                dma(w_ffo.v3(NFF)[:, k0:k1, :], wffo_d[k0 * 128:k1 * 128, :].rearrange("(k p) n -> p k n", p=128),
                    writes=[w_ffo[:, k0 * 1024:k1 * 1024]], q="pool")
        fin_b = SF(D_BASE, 1024)
        dma(fin_b.ap, fin_d, reads=[h2T], writes=[fin_b])
        yst = [SF(D_BASE + 4096 * (1 + i), 1024) for i in range(2)]
        for t8 in range(8):
            r = 1 if t8 < 4 else 0
            g, tg = t8 // 4, t8 % 4
            xt = x1[t8]
            for half in range(2):
                pm = PF(4 + half + 2 * (t8 % 2), 512)
                cs = slice(half * 512, (half + 1) * 512)
                for fc in range(NFF):
                    mm(pm, uT[g][:, fc * 512 + tg * 128: fc * 512 + (tg + 1) * 128],
                       w_ffo[:, fc * 1024 + half * 512: fc * 1024 + (half + 1) * 512], fc == 0, fc == NFF - 1)
                tt(junk2[:, cs], pm, mods2[2 * 2 + r][:, cs], ALU.mult)
            tt(xt, xt, junk2, ALU.add)
            ss, t1, t2, rs = tslot()
            act(junkA2, xt, AF.Square, accum=ss)
            rstd_from_ss(rs, ss, t1, t2, D)
            yo = yst[t8 % 2]
            stt(yo, xt, rs, fin_b, ALU.mult, ALU.mult)
            if t8 < 4:
                dma(ys_d[t8 * 128:(t8 + 1) * 128, :], yo.ap, reads=[yo])
            else:
                dma(yp_d[(t8 - 4) * 128:(t8 - 3) * 128, :], yo.ap, reads=[yo])

        P.finalize()
        P.emit(nc, es)
    return nc


_ROPE_P = np.array(list(range(8, 16)) + list(range(0, 8)) + list(range(24, 32)) + list(range(16, 24)))
_ROPE_S = np.array([-1.0] * 8 + [1.0] * 8 + [-1.0] * 8 + [1.0] * 8, dtype=np.float32)


def _rope_tables(n_tokens):
    t = np.arange(n_tokens)
    row = (t // 64).astype(np.float32)
    col = (t % 64).astype(np.float32)
    half = 16
    inv = (np.float32(10000.0) ** (-np.arange(0, half, 2, dtype=np.float32) / np.float32(half))).astype(np.float32)
    ang_r = row[:, None] * inv
    ang_c = col[:, None] * inv
    ang = np.concatenate([ang_r, ang_r, ang_c, ang_c], axis=-1).astype(np.float32)
    return np.cos(ang).astype(np.float32), (np.sin(ang).astype(np.float32) * _ROPE_S[None, :])


def _bc(v, n=128):
    return np.ascontiguousarray(np.broadcast_to(np.asarray(v, np.float32).reshape(1, -1), (n, v.size)))


_WIN_SEGS = ((384, 672, 0), (2240, 2272, 288), (928, 1184, 320), (1184, 1696, 576), (1696, 1728, 1088),
             (0, 384, 1120), (672, 928, 1504), (1728, 2240, 1760))
_WIN_SPLIT = 1120


def _win_col(c0, c1):
    for a, b, d in _WIN_SEGS:
        if a <= c0 and c1 <= b:
            return d + (c0 - a), d + (c1 - a)
    raise ValueError((c0, c1))


_NC_CACHE = {}


def kernel(x_prompt, x_sample, cache_kv_latent, cache_k_rope, state_gla_fwd, state_gla_bwd,
           c, c_ctx, w_ada, b_ada, norm_attn, w_in, mla_q_norm, w_uq, mla_kv_norm, w_ukv,
           w_gate_f, b_gate_f, w_gate_b, b_gate_b, gla_norm, w_out, norm_ffn, w_ffn_in,
           w_ffn_out, final_norm):
    f32 = np.float32
    A = lambda a: np.ascontiguousarray(np.asarray(a, dtype=f32))
    x_prompt, x_sample = A(x_prompt), A(x_sample)
    cos_all, sin_all = _rope_tables(2048)
    w_in0 = A(w_in)[0]
    w_in_nat = np.concatenate([w_in0, w_in0[:, 640:672][:, _ROPE_P]], axis=1)
    w_in_dev = np.zeros_like(w_in_nat)
    for a_, b_, d_ in _WIN_SEGS:
        w_in_dev[:, d_:d_ + (b_ - a_)] = w_in_nat[:, a_:b_]
    w_in_dev = np.ascontiguousarray(w_in_dev)
    w_uq0 = A(w_uq)[0]
    w_uqp = np.zeros((384, 768), f32)
    for h in range(8):
        w_uqp[:, h * 96 + 64:(h + 1) * 96] = w_uq0[:, h * 96 + 64:(h + 1) * 96][:, _ROPE_P]
    w_ukv0 = A(w_ukv)[0].reshape(256, 8, 128)
    w_ukv2 = np.ascontiguousarray(np.concatenate(
        [w_ukv0[:, :, :64].reshape(256, 512), w_ukv0[:, :, 64:].reshape(256, 512)], axis=1))
    wgf, wgb = A(w_gate_f)[0], A(w_gate_b)[0]
    bgf, bgb = A(b_gate_f)[0], A(b_gate_b)[0]
    Wf = np.zeros((33, 256), f32); Wf[0:16] = wgf; Wf[32] = bgf
    Wb = np.zeros((33, 256), f32); Wb[16:32] = wgb; Wb[32] = bgb
    Wown = np.ascontiguousarray(np.concatenate([Wf, Wb], axis=1))
    s, t = np.arange(128)[:, None], np.arange(128)[None, :]
    ng = f32(-1.0 / 16.0)
    tri = np.concatenate([(s <= t) * ng, (s >= t) * ng, (s > t) * ng, (s < t) * ng], axis=1).astype(f32)
    mask = np.concatenate([(s <= t), (s >= t)], axis=1).astype(f32)
    ident = np.eye(128, dtype=f32)
    shared = {
        "w_ada": A(w_ada)[0], "b_ada": A(b_ada)[0].reshape(1, -1),
        "norm_attn_b": _bc(A(norm_attn)[0]), "norm_ffn_b": _bc(A(norm_ffn)[0]),
        "w_in": w_in_dev, "w_uq": w_uq0, "w_uqp": w_uqp, "w_ukv2": w_ukv2, "w_out": A(w_out)[0],
        "w_ffn_in": A(w_ffn_in)[0], "w_ffn_out": A(w_ffn_out)[0],
        "kv_norm_b": _bc(A(mla_kv_norm)[0]), "q_norm_b": _bc(A(mla_q_norm)[0]),
        "gla_norm4_b": _bc(np.tile(A(gla_norm)[0], 4)), "final_norm_b": _bc(A(final_norm)),
        "ident": ident, "tri": np.ascontiguousarray(tri), "mask": np.ascontiguousarray(mask),
        "Wown": Wown,
    }
    sf = A(state_gla_fwd)[:, 0].reshape(2, 2, 128, 128)
    sbw = A(state_gla_bwd)[:, 0].reshape(2, 2, 128, 128)
    in_maps = []
    for core in range(8):
        b, qd = core // 4, core % 4
        nb = 12 - 4 * qd
        bw_tiles = list(range(15, 4 * qd + 3, -1))
        fw_tiles = list(range(0, 4 * qd))
        xs = x_sample[b]
        pre, cosk, sink = [], [], []
        Wstep = np.zeros((33, 12 * 256), f32)
        for j, tl in enumerate(bw_tiles + fw_tiles):
            rows = np.arange(tl * 128, (tl + 1) * 128)
            if j < nb:
                rows = rows[::-1]
            pre.append(xs[rows]); cosk.append(cos_all[rows]); sink.append(sin_all[rows])
            Wstep[:, j * 256:(j + 1) * 256] = Wb if j < nb else Wf
        own_rows = np.arange(4 * qd * 128, (4 * qd + 4) * 128)
        cosk.append(cos_all[own_rows]); sink.append(sin_all[own_rows])
        keep = np.ones(13, f32); cap = np.zeros(13, f32)
        INIT = np.zeros((13, 2, 128, 128), f32)
        keep[0] = 0.0
        INIT[0] = sbw[b] if nb > 0 else sf[b]
        if nb > 0:
            keep[nb] = 0.0; cap[nb] = 1.0; INIT[nb] = sf[b]
        SB0 = sbw[b] if nb == 0 else np.zeros((2, 128, 128), f32)
        keepcap = _bc(np.concatenate([keep, cap]))
        cond2 = np.stack([A(c_ctx), A(c)[b]], axis=0)
        condT = np.ascontiguousarray(cond2.reshape(2, 8, 128).transpose(2, 1, 0).reshape(128, 16))
        m = dict(shared)
        m.update({
            "xp": np.ascontiguousarray(x_prompt[2 * core:2 * core + 2].reshape(512, 1024)),
            "xs_pre": np.ascontiguousarray(np.concatenate(pre, axis=0)),
            "xs_own": np.ascontiguousarray(xs[own_rows]),
            "ckv_ctx": A(cache_kv_latent)[b, 0], "kr_ctx": A(cache_k_rope)[b, 0],
            "cosk": np.ascontiguousarray(np.concatenate(cosk, axis=0)),
            "sink": np.ascontiguousarray(np.concatenate(sink, axis=0)),
            "cosq": np.ascontiguousarray(cos_all[own_rows].T), "sinq": np.ascontiguousarray(sin_all[own_rows].T),
            "R0": np.zeros((2, 128, 128), f32), "SB0": np.ascontiguousarray(SB0),
            "INIT": INIT, "keepcap": keepcap, "Wstep": Wstep, "condT": condT,
        })
        in_maps.append(m)
    if "nc" not in _NC_CACHE:
        _NC_CACHE["nc"] = build_program()
    res = run_bass_kernel_spmd(_NC_CACHE["nc"], in_maps, core_ids=list(range(8)))
    R = res.results
    y_prompt = np.stack([np.asarray(R[cidx]["y_p"]).reshape(2, 256, 1024) for cidx in range(8)]).reshape(16, 256, 1024)
    y_sample = np.stack([np.asarray(R[cidx]["y_s"]) for cidx in range(8)]).reshape(2, 2048, 1024)
    new_kv = np.stack([np.asarray(R[cidx]["new_kv"]).reshape(2, 256, 256) for cidx in range(8)]).reshape(16, 1, 256, 256)
    new_kr = np.stack([np.asarray(R[cidx]["new_kr"]).reshape(2, 256, 32) for cidx in range(8)]).reshape(16, 1, 256, 32)
    new_sf = np.stack([np.asarray(R[cidx]["new_sf"]) for cidx in range(8)]).reshape(16, 1, 4, 64, 128)
    new_sb = np.stack([np.asarray(R[cidx]["new_sb"]) for cidx in range(8)]).reshape(16, 1, 4, 64, 128)
    return (y_prompt.astype(f32), y_sample.astype(f32), new_kv.astype(f32), new_kr.astype(f32),
            new_sf.astype(f32), new_sb.astype(f32))
```

```python
import math
import os
import numpy as np
import concourse.bass as bass
import concourse.mybir as mybir
from concourse.bass_utils import run_bass_kernel_spmd
from contextlib import ExitStack

F32 = mybir.dt.float32
BF16 = mybir.dt.bfloat16
AF = mybir.ActivationFunctionType
ALU = mybir.AluOpType

N_DMA_SEMS = 8


def _env(name, default):
    if os.environ.get("KDEBUG") == "1":
        return os.environ.get(name, default)
    return default
ENGINES = ("pe", "act", "dve", "pool", "sp")


class T:
    def __init__(self, arena, base, esize, p0, p1, c0, c1):
        self.arena, self.base, self.esize = arena, base, esize
        self.p0, self.p1, self.c0, self.c1 = p0, p1, c0, c1

    def __getitem__(self, key):
        if not isinstance(key, tuple):
            key = (key, slice(None))
        ps, cs = key
        a, b, _ = ps.indices(self.p1 - self.p0)
        c, d, _ = cs.indices(self.c1 - self.c0)
        return T(self.arena, self.base, self.esize, self.p0 + a, self.p0 + b, self.c0 + c, self.c0 + d)

    @property
    def ap(self):
        return self.base[self.p0:self.p1, self.c0:self.c1]

    @property
    def reg(self):
        return (self.arena, self.p0, self.p1, self.c0 * self.esize, self.c1 * self.esize)

    def v3(self, a):
        return self.ap.rearrange("p (a b) -> p a b", a=a)


class Prog:
    WINDOW = int(_env("KWINDOW", "100000"))
    SCHED = _env("KSCHED", "1") == "1"
    BARRIERS = _env("KBARRIERS", "0") == "1"
    SLACK = float(_env("KSLACK", "0.0"))

    SCHED_PH = _env("KSCHED_PH", "")

    def __init__(self):
        self.all = []
        self.cur_phase = 0.0
        self.ops = {e: [] for e in ENGINES}
        self.count = {e: 0 for e in ENGINES}
        self.dma_last = {}

    @staticmethod
    def _overlap(a, b):
        return a[1] < b[2] and b[1] < a[2] and a[3] < b[4] and b[3] < a[4]

    @staticmethod
    def _contains(a, b):
        return a[1] <= b[1] and b[2] <= a[2] and a[3] <= b[3] and b[4] <= a[4]

    enabled = True

    def op(self, eng, fn, reads=(), writes=(), dma=False, cost=None):
        if not self.enabled:
            return None

        def norm(r):
            r = r.reg if isinstance(r, T) else r
            if r[0] == "ps":
                return ("ps", r[1] // 32 * 32, (r[2] + 31) // 32 * 32, r[3] // 2048 * 2048, (r[4] + 2047) // 2048 * 2048)
            return r

        raw_r = [r.reg if isinstance(r, T) else r for r in reads]
        raw_w = [w.reg if isinstance(w, T) else w for w in writes]
        self.all.append(dict(eng=eng, fn=fn, reads=[norm(r) for r in reads], writes=[norm(w) for w in writes],
                             dma=dma, raw_r=raw_r, raw_w=raw_w, cost=cost, ph=self.cur_phase))
        return None

    def _deps(self):
        records = {}
        preds = []
        for i, o in enumerate(self.all):
            p = set()
            for r in o["reads"]:
                for rec in records.get(r[0], ()):
                    if rec[1] == "W" and self._overlap(r, rec[0]):
                        p.add(rec[2])
                    elif (r[0] == "ps" and rec[1] == "R" and r[3] < rec[0][4] and rec[0][3] < r[4]
                          and self.all[rec[2]]["eng"] != o["eng"]):
                        p.add(rec[2])
            for w in o["writes"]:
                for rec in records.get(w[0], ()):
                    if self._overlap(w, rec[0]):
                        p.add(rec[2])
            p.discard(i)
            preds.append(p)
            for w in o["writes"]:
                lst = records.setdefault(w[0], [])
                lst[:] = [rec for rec in lst if not self._contains(w, rec[0])]
                lst.append((w, "W", i))
            for r in o["reads"]:
                lst = records.setdefault(r[0], [])
                lst[:] = [rec for rec in lst if not (rec[1] == "R" and rec[0] == r and rec[2] in p)]
                lst.append((r, "R", i))
        return preds

    def _cost(self, o):
        if o["cost"] is not None:
            return o["cost"]
        w = o["raw_w"][0] if o["raw_w"] else o["raw_r"][0]
        width = max(1, (w[4] - w[3]) // 4)
        e = o["eng"]
        if o["dma"]:
            return 0.15
        if e == "pe":
            return 0.05 + width / 2100.0
        if e == "act":
            return 0.2 + width / 1100.0
        if e == "pool":
            return 0.2 + width / 400.0
        return 0.12 + width / 800.0

    def _dma_latency(self, o):
        w = o["raw_w"][0] if o["raw_w"] else o["raw_r"][0]
        nbytes = (w[2] - w[1]) * (w[4] - w[3])
        return 2.0 + nbytes / 120e3

    def finalize(self):
        n = len(self.all)
        preds = self._deps()
        order = list(range(n))
        order_only = [set() for _ in range(n)]
        fix = [e for e in _env("KFIX", "").split(",") if e]
        for e in fix:
            prev = None
            for i, o in enumerate(self.all):
                if o["eng"] == e:
                    if prev is not None and prev not in preds[i]:
                        preds[i].add(prev)
                        order_only[i].add(prev)
                    prev = i
        if self.SCHED:
            succs = [[] for _ in range(n)]
            npred = [len(p) for p in preds]
            for i, p in enumerate(preds):
                for j in p:
                    succs[j].append(i)
            finish = [0.0] * n
            rdy_t = [0.0] * n
            blevel = [0.0] * n
            for i in range(n - 1, -1, -1):
                o = self.all[i]
                c = self._cost(o) + (self._dma_latency(o) if o["dma"] else 0.0)
                m = 0.0
                for k in succs[i]:
                    if blevel[k] > m:
                        m = blevel[k]
                blevel[i] = c + m
            efree = {e: 0.0 for e in ENGINES}
            dma_free = [0.0]
            ready = [i for i in range(n) if npred[i] == 0]
            allowed = None
            if self.SCHED_PH:
                allowed = {float(x) for x in self.SCHED_PH.split(",")}
            done = [False] * n
            lowest = 0
            order = []
            LAT = 0.25
            while len(order) < n:
                while lowest < n and done[lowest]:
                    lowest += 1
                ph = self.all[lowest]["ph"]
                lim = lowest + self.WINDOW
                if allowed is not None and ph not in allowed:
                    lim = lowest + 1
                elif self.BARRIERS:
                    k = lowest
                    while k < n and k < lim and self.all[k]["ph"] == ph:
                        k += 1
                    lim = k
                best, best_t, bkey = None, None, None
                cands = []
                tmin = None
                for i in ready:
                    if i >= lim:
                        continue
                    t = max(efree[self.all[i]["eng"]], rdy_t[i])
                    cands.append((t, i))
                    if tmin is None or t < tmin:
                        tmin = t
                for t, i in cands:
                    if t <= tmin + self.SLACK:
                        key = (-blevel[i], i)
                        if best is None or key < bkey:
                            best, best_t, bkey = i, t, key
                if best is None:
                    best = min(ready)
                    best_t = max(efree[self.all[best]["eng"]], rdy_t[best])
                o = self.all[best]
                c = self._cost(o)
                efree[o["eng"]] = best_t + c
                if o["dma"]:
                    st = max(best_t + c, dma_free[0])
                    lat = self._dma_latency(o)
                    dma_free[0] = st + (lat - 2.0)
                    finish[best] = st + lat
                else:
                    finish[best] = best_t + c
                done[best] = True
                ready.remove(best)
                order.append(best)
                for k in succs[best]:
                    npred[k] -= 1
                    rdy_t[k] = max(rdy_t[k], finish[best] + LAT)
                    if npred[k] == 0:
                        ready.append(k)
        if self.SCHED and _env("KVERBOSE", ""):
            print("sched makespan(us)", max(finish), "window", self.WINDOW, "slack", self.SLACK)
        tok = [None] * n
        seen = {e: {} for e in ENGINES}
        dma_rr = {"sp": 0, "pool": 0}
        pe_seq = []
        for i in order:
            o = self.all[i]
            eng = o["eng"]
            need = {}

            def want(t):
                if need.get(t[0], 0) < t[1]:
                    need[t[0]] = t[1]

            for j in preds[i]:
                assert tok[j] is not None, "dependency scheduled after its consumer"
                if eng == "pe" and self.all[j]["eng"] == "pe":
                    continue
                if j in order_only[i]:
                    continue
                want(tok[j])
            if o["dma"]:
                k = dma_rr[eng]
                dma_rr[eng] = (k + 1) % N_DMA_SEMS
                skey = f"dma_{eng}_{k}"
                last = self.dma_last.get(skey, 0)
                if last:
                    want((skey, last))
                tok[i] = (skey, last + 16)
                self.dma_last[skey] = last + 16
                inc = (skey, 16)
            else:
                self.count[eng] += 1
                tok[i] = (f"e_{eng}", self.count[eng])
                inc = (f"e_{eng}", 1)
                if eng == "pe":
                    pe_seq.append(i)
            waits = []
            sn = seen[eng]
            for sk, v in need.items():
                if sn.get(sk, 0) < v:
                    sn[sk] = v
                    waits.append((sk, v))
            self.ops[eng].append((waits, o["fn"], inc))
        info = []
        for i in pe_seq:
            o = self.all[i]
            r, w = o["raw_r"][0], o["raw_w"][0]
            info.append(((r[1] // 32, (r[2] + 31) // 32), (w[3] // 2048, (w[4] + 2047) // 2048)))
        for a in range(len(info)):
            for b in range(a + 1, min(a + 5, len(info))):
                (a0, a1), (b0, b1) = info[a]
                (c0, c1), (d0, d1) = info[b]
                if (a1 <= c0 or c1 <= a0) and (b0 < d1 and d0 < b1):
                    raise RuntimeError(f"PE row-group/bank hazard between PE ops {a} and {b}: {info[a]} {info[b]}")

    def emit(self, nc, es):
        keys = [f"e_{e}" for e in ENGINES]
        for q in ("sp", "pool"):
            keys += [f"dma_{q}_{i}" for i in range(N_DMA_SEMS)]
        sems = {k: es.enter_context(nc.semaphore(k)) for k in keys}
        finals = [(f"e_{e}", self.count[e]) for e in ENGINES if self.count[e]]
        finals += list(self.dma_last.items())
        ops = self.ops
        block = es.enter_context(nc.Block())

        def run(engine_obj, name, last=False):
            for waits, fn, inc in ops[name]:
                for s, v in waits:
                    engine_obj.wait_ge(sems[s], v)
                fn(engine_obj).then_inc(sems[inc[0]], inc[1])
            if last:
                for s, v in finals:
                    engine_obj.wait_ge(sems[s], v)

        @block.tensor
        def _(e):
            run(e, "pe")

        @block.scalar
        def _(e):
            run(e, "act")

        @block.vector
        def _(e):
            run(e, "dve")

        @block.gpsimd
        def _(e):
            run(e, "pool")

        @block.sync
        def _(e):
            run(e, "sp", last=True)


D = 1024
KC = 8
NH = 8
QK = 96
GH = 4
DFF = 2816
NFF = DFF // 128
EPS = 1e-6
SB_BYTES = 204 * 1024
NPRE = 12
SCALE = QK ** -0.5

K1 = 1024
A_BASE = 0
C_BASE = 26 * K1
D_BASE = 78 * K1
E_BASE = 119 * K1


def build_program():
    nc = bass.Bass("TRN2", target_bir_lowering=False)
    P = Prog()
    _kstop = float(_env("KSTOP", "99"))

    def phase(n):
        P.enabled = n <= _kstop
        P.cur_phase = float(n)

    def din(name, shape):
        return nc.dram_tensor(name, list(shape), F32, kind="ExternalInput").ap()

    def dout(name, shape):
        return nc.dram_tensor(name, list(shape), F32, kind="ExternalOutput").ap()

    xp_d = din("xp", [512, D])
    xpre_d = din("xs_pre", [NPRE * 128, D])
    xown_d = din("xs_own", [512, D])
    ckvctx_d = din("ckv_ctx", [256, 256])
    krctx_d = din("kr_ctx", [256, 32])
    cosk_d = din("cosk", [2048, 32])
    sink_d = din("sink", [2048, 32])
    cosq_d = din("cosq", [32, 512])
    sinq_d = din("sinq", [32, 512])
    r0_d = din("R0", [2, 128, 128])
    sb0_d = din("SB0", [2, 128, 128])
    init_d = din("INIT", [13, 2, 128, 128])
    keepcap_d = din("keepcap", [128, 26])
    wstep_d = din("Wstep", [33, NPRE * 256])
    wown_d = din("Wown", [33, 512])
    condT_d = din("condT", [128, 16])
    wada_d = din("w_ada", [D, 6 * D])
    bada_d = din("b_ada", [1, 6 * D])
    nattn_d = din("norm_attn_b", [128, D])
    nffn_d = din("norm_ffn_b", [128, D])
    win_d = din("w_in", [D, 2272])
    wuq_d = din("w_uq", [384, 768])
    wuqp_d = din("w_uqp", [384, 768])
    wukv_d = din("w_ukv2", [256, 1024])
    wout_d = din("w_out", [D, D])
    wffi_d = din("w_ffn_in", [D, 2 * DFF])
    wffo_d = din("w_ffn_out", [DFF, D])
    kvn_d = din("kv_norm_b", [128, 256])
    qn_d = din("q_norm_b", [128, 384])
    gla4_d = din("gla_norm4_b", [128, 512])
    fin_d = din("final_norm_b", [128, D])
    ident_d = din("ident", [128, 128])
    tri_d = din("tri", [128, 4 * 128])
    mask_d = din("mask", [128, 2 * 128])
    yp_d = dout("y_p", [512, D])
    ys_d = dout("y_s", [512, D])
    nkv_d = dout("new_kv", [512, 256])
    nkr_d = dout("new_kr", [512, 32])
    nsf_d = dout("new_sf", [2, 2, 128, 128])
    nsb_d = dout("new_sb", [2, 2, 128, 128])
    mod2_d = nc.dram_tensor("mod2_scr", [8, 128, D], F32).ap()

    es = ExitStack()
    with es:
        sb = es.enter_context(nc.sbuf_tensor("sb", [128, SB_BYTES // 4], F32))
        ps = es.enter_context(nc.psum_tensor("ps", [128, 4096], F32))
        sbF = sb[:]
        sbB = sb[:].bitcast(BF16)
        psF = ps[:]
        psB = ps[:].bitcast(BF16)

        def SF(off, n, p=128):
            assert off % 4 == 0 and off + 4 * n <= SB_BYTES, (off, n)
            return T("sb", sbF, 4, 0, p, off // 4, off // 4 + n)

        def SBf(off, n, p=128):
            assert off % 2 == 0 and off + 2 * n <= SB_BYTES, (off, n)
            return T("sb", sbB, 2, 0, p, off // 2, off // 2 + n)

        def PF(bank, n=512, c0=0, p=128):
            return T("ps", psF, 4, 0, p, bank * 512 + c0, bank * 512 + c0 + n)

        def PB(bank, n=1024, c0=0, p=128):
            return T("ps", psB, 2, 0, p, bank * 1024 + c0, bank * 1024 + c0 + n)

        class Bump:
            def __init__(self, base, limit):
                self.cur, self.limit = base, limit

            def f(self, n, p=128):
                self.cur = (self.cur + 63) // 64 * 64
                t = SF(self.cur, n, p)
                self.cur += 4 * n
                assert self.cur <= self.limit, (self.cur, self.limit)
                return t

            def b(self, n, p=128):
                self.cur = (self.cur + 63) // 64 * 64
                t = SBf(self.cur, n, p)
                self.cur += 2 * n
                assert self.cur <= self.limit, (self.cur, self.limit)
                return t

        def mm(out, lhsT, rhs, start, stop):
            P.op("pe", lambda e: e.matmul(out=out.ap, lhsT=lhsT.ap, rhs=rhs.ap, start=start, stop=stop),
                 reads=[lhsT, rhs], writes=[out])

        def tr(out, in_, ident):
            P.op("pe", lambda e: e.transpose(out=out.ap, in_=in_.ap, identity=ident.ap),
                 reads=[in_, ident], writes=[out])

        def act(out, in_, func, scale=None, bias=None, accum=None, oap=None, iap=None):
            kw = {}
            if scale is not None:
                kw["scale"] = scale.ap if isinstance(scale, T) else scale
            if bias is not None:
                kw["bias"] = bias
            if accum is not None:
                kw["accum_out"] = accum.ap
            rd = [in_] + ([scale] if isinstance(scale, T) else [])
            wr = [out] + ([accum] if accum is not None else [])
            o = oap if oap is not None else out.ap
            i = iap if iap is not None else in_.ap
            P.op("act", lambda e: e.activation(out=o, in_=i, func=func, **kw), reads=rd, writes=wr)

        def ts(out, in0, s1, s2, op0, op1=None, eng="dve", oap=None, iap=None):
            rd = [in0] + [s for s in (s1, s2) if isinstance(s, T)]
            a1 = s1.ap if isinstance(s1, T) else s1
            a2 = s2.ap if isinstance(s2, T) else s2
            o = oap if oap is not None else out.ap
            i = iap if iap is not None else in0.ap
            if op1 is None:
                fn = lambda e: e.tensor_scalar(out=o, in0=i, scalar1=a1, scalar2=None, op0=op0)
            else:
                fn = lambda e: e.tensor_scalar(out=o, in0=i, scalar1=a1, scalar2=a2, op0=op0, op1=op1)
            P.op(eng, fn, reads=rd, writes=[out])

        def tt(out, a, b, op, eng="dve", oap=None, aap=None, bap=None):
            o = oap if oap is not None else out.ap
            x = aap if aap is not None else a.ap
            y = bap if bap is not None else b.ap
            P.op(eng, lambda e: e.tensor_tensor(out=o, in0=x, in1=y, op=op), reads=[a, b], writes=[out])

        def stt(out, a, s, b, op0, op1):
            sa = s.ap if isinstance(s, T) else s
            rd = [a, b] + ([s] if isinstance(s, T) else [])
            P.op("dve", lambda e: e.scalar_tensor_tensor(out=out.ap, in0=a.ap, scalar=sa, in1=b.ap,
                                                         op0=op0, op1=op1), reads=rd, writes=[out])

        def cp(out, in_, eng="dve", oap=None, iap=None):
            o = oap if oap is not None else out.ap
            i = iap if iap is not None else in_.ap
            if eng == "act":
                P.op("act", lambda e: e.activation(out=o, in_=i, func=AF.Copy), reads=[in_], writes=[out])
            else:
                P.op(eng, lambda e: e.tensor_copy(out=o, in_=i), reads=[in_], writes=[out])

        def recip(out, in_):
            P.op("dve", lambda e: e.reciprocal(out=out.ap, in_=in_.ap), reads=[in_], writes=[out])

        def mset(out, val, eng="pool", oap=None):
            o = oap if oap is not None else out.ap
            P.op(eng, lambda e: e.memset(o, val), writes=[out])

        def dma(out_ap, in_ap, reads=(), writes=(), q="sp"):
            P.op(q, lambda e: e.dma_start(out=out_ap, in_=in_ap), reads=reads, writes=writes, dma=True)

        def rstd_from_ss(rstd, ss, tmp1, tmp2, n):
            ts(tmp1, ss, 1.0 / n, EPS, ALU.mult, ALU.add)
            act(tmp2, tmp1, AF.Ln)
            act(rstd, tmp2, AF.Exp, scale=-0.5)

        A = Bump(A_BASE, C_BASE)
        ident = A.b(128)
        maskf = A.b(128)
        maskb = A.b(128)
        ones_bf = A.b(512)
        tri = A.f(512)
        keepcap = A.f(26)
        kvn_b = A.f(256)
        qn_b = A.f(384)
        gla4_b = A.f(512)
        tiny = A.f(192)
        _tslot = [0]

        def tslot():
            k = _tslot[0] % 40
            _tslot[0] += 1
            return tuple(tiny[:, 32 + 4 * k + c: 33 + 4 * k + c] for c in range(4))
        condT = A.f(16)
        ones_fA = A.f(64)
        scond = A.f(16)
        assert A.cur <= 10 * K1, A.cur
        mods = [SF(10 * K1 + 4096 * i, 1024) for i in range(4)]
        SH1, GM1 = 0, 1

        tri_f, tri_b, tris_f, tris_b = (tri[:, 128 * i:128 * (i + 1)] for i in range(4))

        dma(ident.ap, ident_d, writes=[ident], q="pool")
        dma(maskf.ap, mask_d[:, 0:128], writes=[maskf], q="pool")
        dma(maskb.ap, mask_d[:, 128:256], writes=[maskb], q="pool")
        dma(tri.ap, tri_d, writes=[tri])
        dma(keepcap.ap, keepcap_d, writes=[keepcap])
        dma(kvn_b.ap, kvn_d, writes=[kvn_b])
        dma(qn_b.ap, qn_d, writes=[qn_b])
        dma(gla4_b.ap, gla4_d, writes=[gla4_b])
        dma(condT.ap, condT_d, writes=[condT])
        mset(ones_bf, 1.0)
        mset(ones_fA, 1.0)

        Cb = Bump(C_BASE, D_BASE)
        w_in = Cb.b(KC * 2272)
        w_uq = Cb.b(3 * 768)
        w_uqp = Cb.b(3 * 768)
        w_own = Cb.b(512, p=33)
        w_step = Cb.b(NPRE * 256, p=33)

        def w_in_s(kc, c0, c1):
            d0, d1 = _win_col(c0, c1)
            return w_in[:, kc * 2272 + d0: kc * 2272 + d1]

        phase(0.5)
        E = Bump(E_BASE, SB_BYTES)
        wada_blk = [E.b(KC * 1024) for _ in range(2)]
        bada_blk = [E.b(1024, p=1) for _ in range(2)]
        lhs_rep = E.b(2 * KC * 128)
        nrm_b = [E.f(1024) for _ in range(2)]
        mstage = [E.f(1024) for _ in range(2)]
        sig = E.f(16)
        dma(nrm_b[0].ap, nattn_d, writes=[nrm_b[0]])
        dma(nrm_b[1].ap, nffn_d, writes=[nrm_b[1]])
        act(sig, condT, AF.Exp, scale=-1.0)
        ts(sig, sig, 1.0, None, ALU.add)
        recip(scond, sig)
        tt(scond, scond, condT, ALU.mult)
        ones_f = E.f(128)
        mset(ones_f, 1.0)
        for r in range(2):
            for kc in range(KC):
                dst = lhs_rep[:, (r * KC + kc) * 128:(r * KC + kc + 1) * 128]
                ts(dst, ones_f, scond[:, kc * 2 + r: kc * 2 + r + 1], None, ALU.mult)

        def load_wada(kind):
            buf = wada_blk[kind % 2]
            dma(buf.v3(KC), wada_d[:, kind * 1024:(kind + 1) * 1024].rearrange("(k p) n -> p k n", p=128),
                writes=[buf], q="pool")
            bb = bada_blk[kind % 2]
            dma(bb.ap, bada_d[:, kind * 1024:(kind + 1) * 1024], writes=[bb], q="pool")

        load_wada(0)
        pending_w = []

        def queue_weight_loads():
            for (d0, d1) in ((0, _WIN_SPLIT), (_WIN_SPLIT, 2272)):
                dma(w_in.v3(KC)[:, :, d0:d1], win_d[:, d0:d1].rearrange("(k p) n -> p k n", p=128),
                    writes=[w_in[:, kc * 2272 + d0: kc * 2272 + d1] for kc in range(KC)], q="pool")
            dma(w_uq.v3(3), wuq_d.rearrange("(k p) n -> p k n", p=128), writes=[w_uq], q="pool")
            dma(w_uqp.v3(3), wuqp_d.rearrange("(k p) n -> p k n", p=128), writes=[w_uqp], q="pool")
            dma(w_own.ap, wown_d, writes=[w_own], q="pool")
            dma(w_step.ap, wstep_d, writes=[w_step], q="pool")

        mod_kind = 0
        for blk in range(12):
            kind = blk // 2
            half = blk % 2
            if blk == 2:
                queue_weight_loads()
            if half == 0 and kind + 1 < 6:
                load_wada(kind + 1)
            buf = wada_blk[kind % 2]
            bb = bada_blk[kind % 2]
            for r in range(2):
                pt = PF(r * 2 + half % 2)
                for kc in range(KC):
                    mm(pt, lhs_rep[:, (r * KC + kc) * 128:(r * KC + kc + 1) * 128],
                       buf[:, kc * 1024 + half * 512: kc * 1024 + (half + 1) * 512], kc == 0, False)
                mm(pt, ones_bf[0:1, 0:128], bb[:, half * 512:(half + 1) * 512], False, True)
                cs = slice(half * 512, (half + 1) * 512)
                if kind < 2:
                    dst = mods[r * 2 + kind][:, cs]
                else:
                    dst = mstage[r][:, cs]
                if kind in (1, 4):
                    nb = nrm_b[0 if kind == 1 else 1][:, cs]
                    stt(dst, pt, 1.0, nb, ALU.add, ALU.mult)
                else:
                    cp(dst, pt, eng="act")
                if kind >= 2 and half == 1:
                    mi = 6 + r if kind == 2 else (kind - 3) * 2 + r
                    dma(mod2_d[mi], mstage[r].ap, reads=[mstage[r]],
                        writes=[("mod2", 0, 1, mi * 10, mi * 10 + 10)])

        Db = Bump(D_BASE, E_BASE)
        ckvT_s = Db.b(2 * 2304)
        krt_s = Db.b(2304, p=96)
        qT_s = Db.b(NH * 512, p=96)
        ogT_s = Db.b(GH * 512)
        ckvT_p = Db.b(2 * 512)
        krt_p = Db.b(512, p=96)
        qT_p = Db.b(NH * 512, p=96)
        ogT_p = Db.b(GH * 512)

        E = Bump(E_BASE, SB_BYTES)
        xst = [E.f(1024) for _ in range(2)]
        junk = E.f(1024)
        junkA = E.b(1024)
        hbf = [E.b(1024) for _ in range(2)]
        hT = E.b(KC * 512)
        kvst = [E.b(384) for _ in range(2)]
        kvf = E.f(320)
        ckvf = [E.f(256) for _ in range(2)]
        krf = [E.f(32) for _ in range(2)]
        ropet = E.f(64)
        cosk_t = [E.f(32) for _ in range(2)]
        sink_t = [E.f(32) for _ in range(2)]
        U1 = E.f(2048)
        cosq_t = T("sb", sbF, 4, 0, 96, U1.c0, U1.c0 + 512)
        sinq_t = T("sb", sbF, 4, 0, 96, U1.c0 + 512, U1.c0 + 1024)
        ropeq1 = T("sb", sbF, 4, 0, 96, U1.c0 + 1024, U1.c0 + 1536)
        ropeq2 = T("sb", sbF, 4, 0, 96, U1.c0 + 1536, U1.c0 + 2048)
        U2 = E.f(1536)
        _o2 = U2.c0 * 4
        qlat = SF(_o2, 384)
        qnb = SBf(_o2 + 1536, 384)
        qnT = SBf(_o2 + 2304, 3 * 512)
        Gg = SF(_o2, 512)
        osum = SF(_o2 + 2048, 512)
        ogb = SBf(_o2 + 4096, 512)
        lowsT = E.b(512, p=33)
        Lg = E.f(512)
        ktok = E.f(256)
        vbf = [E.b(512) for _ in range(4)]
        kdf = [E.b(256) for _ in range(4)]
        kdb = [E.b(256) for _ in range(4)]
        qTf = [E.f(512) for _ in range(2)]
        kTf = [E.f(512) for _ in range(2)]
        Ef = E.f(256)
        eg = E.f(512)
        Einv = eg[:, 0:256]
        expD = eg[:, 256:512]
        qeT = {d: [E.b(512) for _ in range(2)] for d in "fb"}
        keT = {d: [E.b(512) for _ in range(2)] for d in "fb"}
        dec = {d: E.f(8) for d in "fb"}
        assert qTf[1].c0 == qTf[0].c1 and kTf[1].c0 == kTf[0].c1
        assert all(qeT[d_][1].c0 == qeT[d_][0].c1 and keT[d_][1].c0 == keT[d_][0].c1 for d_ in "fb")
        q2 = SF(qTf[0].c0 * 4, 1024)
        k2 = SF(kTf[0].c0 * 4, 1024)
        attm = [E.b(128) for _ in range(4)]
        assert attm[1].c0 == attm[0].c1 and attm[3].c0 == attm[2].c1 and maskb.c0 == maskf.c1
        attm2 = [SBf(attm[0].c0 * 2, 256), SBf(attm[2].c0 * 2, 256)]
        mask2 = SBf(maskf.c0 * 2, 256)
        o_f = [U1[:, 512 * i:512 * (i + 1)] for i in range(4)]
        Sst = {d: [[E.f(128) for _ in range(2)] for _ in range(2)] for d in "fb"}
        Sbf = {d: [E.b(128) for _ in range(2)] for d in "fb"}
        _u = [U1[:, 128 * i:128 * (i + 1)] for i in range(14)]
        Rst = [[_u[0], _u[1]], [_u[2], _u[3]]]
        Rin = [_u[4], _u[5]]
        Sbacc = [[_u[6], _u[7]], [_u[8], _u[9]]]
        initb = [[_u[10], _u[11]], [_u[12], _u[13]]]
        decp = E.f(2)
        ssq = E.f(4)

        def norm1_tile(x_src_ap, gi, tix, modr, xbuf):
            xt = xst[xbuf]
            dma(xt.ap, x_src_ap, writes=[xt])
            ss, t1, t2, rs = tslot()
            act(junkA, xt, AF.Square, accum=ss)
            rstd_from_ss(rs, ss, t1, t2, D)
            hb = hbf[xbuf]
            stt(junk, xt, rs, mods[modr * 2 + GM1], ALU.mult, ALU.mult)
            tt(hb, junk, mods[modr * 2 + SH1], ALU.add)
            pt = PB(4)
            for kc in range(KC):
                tr(pt[:, kc * 128:(kc + 1) * 128], hb[:, kc * 128:(kc + 1) * 128], ident)
            cp(hT, pt, eng="act",
               oap=hT.v3(KC)[:, :, tix * 128:(tix + 1) * 128], iap=pt.v3(KC))

        def kv_tile(tix, rope_row0, ckvT, krt, key0, prompt_out_row=None, kbuf=0):
            pk = PF(5, 320)
            for kc in range(KC):
                mm(pk[:, 0:288], hT[:, kc * 512 + tix * 128: kc * 512 + (tix + 1) * 128],
                   w_in_s(kc, 384, 672), kc == 0, kc == KC - 1)
            if rope_row0 is not None:
                for kc in range(KC):
                    mm(pk[:, 288:320], hT[:, kc * 512 + tix * 128: kc * 512 + (tix + 1) * 128],
                       w_in_s(kc, 2240, 2272), kc == 0, kc == KC - 1)
            ncv = 320 if rope_row0 is not None else 288
            cp(kvf[:, 0:ncv], pk[:, 0:ncv], eng="act")
            ss, t1, t2, rs = tslot()
            act(junkA[:, 0:256], kvf[:, 0:256], AF.Square, accum=ss)
            rstd_from_ss(rs, ss, t1, t2, 256)
            st = kvst[kbuf]
            if prompt_out_row is not None:
                cf = ckvf[kbuf]
                stt(cf, kvf[:, 0:256], rs, kvn_b, ALU.mult, ALU.mult)
                cp(st[:, 0:256], cf, eng="pool")
                dma(nkv_d[prompt_out_row:prompt_out_row + 128, :], cf.ap, reads=[cf])
                kf = krf[kbuf]
                cp(kf, kvf[:, 256:288], eng="pool")
                cp(st[:, 320:352], kvf[:, 256:288], eng="pool")
                dma(nkr_d[prompt_out_row:prompt_out_row + 128, :], kf.ap, reads=[kf])
            else:
                stt(st[:, 0:256], kvf[:, 0:256], rs, kvn_b, ALU.mult, ALU.mult)
                ck, sk = cosk_t[kbuf], sink_t[kbuf]
                dma(ck.ap, cosk_d[rope_row0:rope_row0 + 128, :], writes=[ck])
                dma(sk.ap, sink_d[rope_row0:rope_row0 + 128, :], writes=[sk])
                tt(ropet[:, 0:32], kvf[:, 256:288], ck, ALU.mult)
                tt(ropet[:, 32:64], kvf[:, 288:320], sk, ALU.mult)
                tt(st[:, 320:352], ropet[:, 0:32], ropet[:, 32:64], ALU.add)
            kv_transposes(st, ckvT, krt, key0)

        def kv_transposes(st, ckvT, krt, key0):
            nk = (ckvT.c1 - ckvT.c0) // 2
            pt = PB(4, 384)
            tr(pt[:, 0:128], st[:, 0:128], ident)
            tr(pt[:, 128:256], st[:, 128:256], ident)
            tr(pt[0:96, 256:384], st[:, 256:352], ident)
            cp(ckvT[:, key0:key0 + 128], pt[:, 0:128], eng="act")
            cp(ckvT[:, nk + key0:nk + key0 + 128], pt[:, 128:256], eng="act")
            cp(krt[64:96, key0:key0 + 128], pt[64:96, 256:384], eng="dve")

        def gla_common_tile(tix, wg, ncol, slot):
            pa = PF(0, 512)
            pb = PF(1, 256)
            for kc in range(KC):
                lhs = hT[:, kc * 512 + tix * 128: kc * 512 + (tix + 1) * 128]
                mm(pa, lhs, w_in_s(kc, 1184, 1696), kc == 0, kc == KC - 1)
            for kc in range(KC):
                lhs = hT[:, kc * 512 + tix * 128: kc * 512 + (tix + 1) * 128]
                mm(pb, lhs, w_in_s(kc, 928, 1184), kc == 0, kc == KC - 1)
            cp(vbf[slot], pa, eng="act")
            cp(ktok, pb, eng="dve")
            pg = PF(5, ncol)
            mm(pg, lowsT[:, tix * 128:(tix + 1) * 128], wg, True, True)
            act(eg[:, 0:ncol], pg, AF.Exp, scale=-1.0)
            act(Lg[:, 0:ncol], eg[:, 0:ncol], AF.Ln, bias=1.0)

        def lows_group():
            pl = PF(6, 512)
            for kc in range(KC):
                mm(pl[0:32, :], w_in_s(kc, 1696, 1728), hT[:, kc * 512:(kc + 1) * 512], kc == 0, kc == KC - 1)
            cp(lowsT[0:32, :], pl[0:32, :], eng="dve")

        mset(lowsT[32:33, :], 1.0)
        for _k in range(2):
            mset(kvst[_k], 0.0)

        phase(1)
        for pr in range(2):
            dma(Rst[0][pr].ap, r0_d[pr], writes=[Rst[0][pr]])
            dma(Sbacc[0][pr].ap, sb0_d[pr], writes=[Sbacc[0][pr]])
        for t in range(2):
            cst = ckvf[t]
            dma(cst.ap, ckvctx_d[t * 128:(t + 1) * 128, :], writes=[cst])
            kf = krf[t]
            dma(kf.ap, krctx_d[t * 128:(t + 1) * 128, :], writes=[kf])
            st = kvst[t]
            cp(st[:, 0:256], cst, eng="pool")
            cp(st[:, 320:352], kf, eng="pool")
            kv_transposes(st, ckvT_s, krt_s, t * 128)

        rp = 0
        for g in range(3):
            for tix in range(4):
                j = g * 4 + tix
                norm1_tile(xpre_d[j * 128:(j + 1) * 128, :], g, tix, 1, j % 2)
            lows_group()
            for tix in range(4):
                j = g * 4 + tix
                kv_tile(tix, j * 128, ckvT_s, krt_s, 256 + j * 128, kbuf=j % 2)
                for pr in range(2):
                    dma(initb[j % 2][pr].ap, init_d[j, pr], writes=[initb[j % 2][pr]])
                gla_common_tile(tix, w_step[:, j * 256:(j + 1) * 256], 256, 0)
                pd = PF(6, 256)
                P.op("pe", lambda e, pd=pd: e.matmul(out=pd.ap, lhsT=tris_f.ap, rhs=Lg[:, 0:256].ap,
                                                     start=True, stop=True),
                     reads=[tris_f, Lg[:, 0:256]], writes=[pd])
                pdec = PF(7, 2)
                for pr in range(2):
                    P.op("pe", lambda e, pr=pr, pdec=pdec: e.matmul(
                        out=pdec[:, pr:pr + 1].ap, lhsT=Lg[:, pr * 128:(pr + 1) * 128].ap,
                        rhs=tri_f[:, 127:128].ap, start=True, stop=True),
                        reads=[Lg[:, pr * 128:(pr + 1) * 128], tri_f], writes=[pdec[:, pr:pr + 1]])
                act(expD, pd, AF.Exp)
                act(decp, pdec, AF.Exp)
                tt(kdf[0], ktok, expD, ALU.mult)
                for pr in range(2):
                    pu = PF(2 + pr, 256)
                    mm(pu, kdf[0][:, pr * 128:(pr + 1) * 128], vbf[0][:, pr * 256:(pr + 1) * 256], True, True)
                    Rold = Rst[rp][pr]
                    Rnew = Rst[1 - rp][pr]
                    stt(Rin[pr], Rold, keepcap[:, j:j + 1], initb[j % 2][pr], ALU.mult, ALU.add)
                    stt(Sbacc[1 - rp][pr], Rold, keepcap[:, 13 + j:14 + j], Sbacc[rp][pr], ALU.mult, ALU.add)
                    for hh in range(2):
                        rows = slice(hh * 64, (hh + 1) * 64)
                        stt(Rnew[rows, :], Rin[pr][rows, :], decp[rows, pr:pr + 1],
                            pu[rows, hh * 128:(hh + 1) * 128], ALU.mult, ALU.add)
                rp = 1 - rp
        for pr in range(2):
            dma(initb[0][pr].ap, init_d[12, pr], writes=[initb[0][pr]])
            stt(Sst["f"][0][pr], Rst[rp][pr], keepcap[:, 12:13], initb[0][pr], ALU.mult, ALU.add)
            stt(Sst["b"][0][pr], Rst[rp][pr], keepcap[:, 25:26], Sbacc[rp][pr], ALU.mult, ALU.add)

        def own_group(x_d, modr, is_prompt, ckvT, krt, key0s, qT, ogT, seqs):
            for tix in range(4):
                norm1_tile(x_d[tix * 128:(tix + 1) * 128, :], 0, tix, modr, tix % 2)
            lows_group()
            for tix in range(4):
                if is_prompt:
                    kv_tile(tix, None, ckvT, krt, key0s[tix], prompt_out_row=tix * 128, kbuf=tix % 2)
                else:
                    kv_tile(tix, 1536 + tix * 128, ckvT, krt, key0s[tix], kbuf=tix % 2)
            for tix in range(4):
                pq = PF(0, 384)
                for kc in range(KC):
                    mm(pq, hT[:, kc * 512 + tix * 128: kc * 512 + (tix + 1) * 128],
                       w_in_s(kc, 0, 384), kc == 0, kc == KC - 1)
                cp(qlat, pq, eng="act")
                ss, t1, t2, rs = tslot()
                act(junkA[:, 0:384], qlat, AF.Square, accum=ss)
                rstd_from_ss(rs, ss, t1, t2, 384)
                stt(qnb, qlat, rs, qn_b, ALU.mult, ALU.mult)
                pt = PB(4, 384)
                for c3 in range(3):
                    tr(pt[:, c3 * 128:(c3 + 1) * 128], qnb[:, c3 * 128:(c3 + 1) * 128], ident)
                cp(qnT, pt, eng="act", oap=qnT.v3(3)[:, :, tix * 128:(tix + 1) * 128], iap=pt.v3(3))
            if not is_prompt:
                dma(cosq_t[64:96, :].ap, cosq_d, writes=[cosq_t[64:96, :]])
                dma(sinq_t[64:96, :].ap, sinq_d, writes=[sinq_t[64:96, :]])
            for h in range(NH):
                pa = PF(2 + (h % 2) * 2, 512)
                for c3 in range(3):
                    mm(pa[0:96, :], w_uq[:, c3 * 768 + h * 96: c3 * 768 + (h + 1) * 96],
                       qnT[:, c3 * 512:(c3 + 1) * 512], c3 == 0, c3 == 2)
                dst = qT[:, h * 512:(h + 1) * 512]
                if is_prompt:
                    cp(dst[0:96, :], pa[0:96, :], eng="act")
                else:
                    pb = PF(3 + (h % 2) * 2, 512)
                    for c3 in range(3):
                        mm(pb[0:96, :], w_uqp[:, c3 * 768 + h * 96: c3 * 768 + (h + 1) * 96],
                           qnT[:, c3 * 512:(c3 + 1) * 512], c3 == 0, c3 == 2)
                    cp(dst[0:64, :], pa[0:64, :], eng="act")
                    tt(ropeq1[64:96, :], pa[64:96, :], cosq_t[64:96, :], ALU.mult)
                    tt(ropeq2[64:96, :], pb[64:96, :], sinq_t[64:96, :], ALU.mult)
                    tt(dst[64:96, :], ropeq1[64:96, :], ropeq2[64:96, :], ALU.add)
            for pr in range(2):
                pq = PF(2 + pr, 512)
                for kc in range(KC):
                    mm(pq, w_in_s(kc, 672 + pr * 128, 672 + (pr + 1) * 128), hT[:, kc * 512:(kc + 1) * 512],
                       kc == 0, kc == KC - 1)
                act(qTf[pr], pq, AF.Copy, scale=0.125)
                pk = PF(6 + pr, 512)
                for kc in range(KC):
                    mm(pk, w_in_s(kc, 928 + pr * 128, 928 + (pr + 1) * 128), hT[:, kc * 512:(kc + 1) * 512],
                       kc == 0, kc == KC - 1)
                cp(kTf[pr], pk, eng="dve")
            for tix in range(4):
                gla_common_tile(tix, w_own, 512, tix)
                tcs = slice(tix * 128, (tix + 1) * 128)
                for di, d in enumerate("fb"):
                    Lc = Lg[:, di * 256:(di + 1) * 256]
                    trim = tri_f if d == "f" else tri_b
                    tris = tris_f if d == "f" else tris_b
                    pc = PF(6, 256)
                    for pr in range(2):
                        P.op("pe", lambda e, pr=pr, pc=pc, Lc=Lc, trim=trim: e.matmul(
                            out=pc[:, pr * 128:(pr + 1) * 128].ap, lhsT=Lc[:, pr * 128:(pr + 1) * 128].ap,
                            rhs=trim.ap, start=True, stop=True),
                            reads=[Lc[:, pr * 128:(pr + 1) * 128], trim], writes=[pc[:, pr * 128:(pr + 1) * 128]])
                    pd = PF(7, 256)
                    P.op("pe", lambda e, pd=pd, Lc=Lc, tris=tris: e.matmul(
                        out=pd.ap, lhsT=tris.ap, rhs=Lc.ap, start=True, stop=True),
                        reads=[tris, Lc], writes=[pd])
                    act(Ef, pc, AF.Exp)
                    act(Einv, pc, AF.Exp, scale=-1.0)
                    act(expD, pd, AF.Exp)
                    for pr in range(2):
                        ecol = 127 if d == "f" else 0
                        cp(dec[d][:, pr * 4 + tix: pr * 4 + tix + 1],
                           Ef[:, pr * 128 + ecol: pr * 128 + ecol + 1], eng="pool")
                    for (src2, dst, mul) in ((q2, qeT[d], Ef), (k2, keT[d], Einv)):
                        dst2 = SBf(dst[0].c0 * 2, 1024)
                        P.op("dve", lambda e, src2=src2, dst2=dst2, mul=mul, tcs=tcs: e.tensor_tensor(
                            out=dst2.v3(2)[:, :, tcs], in0=src2.v3(2)[:, :, tcs], in1=mul.v3(2), op=ALU.mult),
                            reads=[src2[:, tcs], src2[:, 512 + tcs.start:512 + tcs.stop], mul],
                            writes=[dst[0][:, tcs], dst[1][:, tcs]])
                    tt((kdf if d == "f" else kdb)[tix], ktok, expD, ALU.mult)
            for seq in seqs:
                cur = {"f": 0, "b": 0}
                if is_prompt:
                    for d in "fb":
                        for pr in range(2):
                            mset(Sst[d][0][pr], 0.0)
                for d in "fb":
                    for pr in range(2):
                        cp(Sbf[d][pr], Sst[d][0][pr], eng="pool")

                def state_update(d, tix):
                    kd = (kdf if d == "f" else kdb)[tix]
                    c = cur[d]
                    for pr in range(2):
                        pu = PF(2 + pr, 256)
                        mm(pu, kd[:, pr * 128:(pr + 1) * 128], vbf[tix][:, pr * 256:(pr + 1) * 256], True, True)
                        for hh in range(2):
                            rows = slice(hh * 64, (hh + 1) * 64)
                            stt(Sst[d][1 - c][pr][rows, :], Sst[d][c][pr][rows, :],
                                dec[d][rows, pr * 4 + tix: pr * 4 + tix + 1],
                                pu[rows, hh * 128:(hh + 1) * 128], ALU.mult, ALU.add)
                        cp(Sbf[d][pr], Sst[d][1 - c][pr], eng="pool")
                    cur[d] = 1 - c

                for tix in seq:
                    tcs = slice(tix * 128, (tix + 1) * 128)
                    for h in range(GH):
                        pr, hh = h // 2, h % 2
                        rows = slice(hh * 64, (hh + 1) * 64)
                        for di, d in enumerate("fb"):
                            pat = PF(6 + hh, 128, c0=di * 128)
                            mm(pat, keT[d][pr][rows, tcs], qeT[d][pr][rows, tcs], True, True)
                        tt(attm2[hh], PF(6 + hh, 256), mask2, ALU.mult)
                        po = PF(2 * hh, 512)[:, h * 128:(h + 1) * 128]
                        mm(po, qeT["f"][pr][rows, tcs], Sbf["f"][pr][rows, :], True, False)
                        mm(po, attm2[hh][:, 0:128], vbf[tix][:, h * 128:(h + 1) * 128], False, False)
                        mm(po, attm2[hh][:, 128:256], vbf[tix][:, h * 128:(h + 1) * 128], False, True)
                    for hh in range(2):
                        cs_ = slice(hh * 128, (hh + 1) * 128)
                        pof = PF(2 * hh, 512)
                        cp(o_f[tix], pof, eng="act", oap=o_f[tix].v3(2)[:, :, cs_], iap=pof.v3(2)[:, :, cs_])
                    state_update("f", tix)
                for tix in reversed(seq):
                    tcs = slice(tix * 128, (tix + 1) * 128)
                    pob = [PF(1, 512), PF(3, 512)]
                    for h in range(GH):
                        pr, hh = h // 2, h % 2
                        rows = slice(hh * 64, (hh + 1) * 64)
                        mm(pob[hh][:, h * 128:(h + 1) * 128], qeT["b"][pr][rows, tcs], Sbf["b"][pr][rows, :],
                           True, True)
                    for hh in range(2):
                        cs_ = slice(hh * 128, (hh + 1) * 128)
                        tt(osum, pob[hh], o_f[tix], ALU.add,
                           oap=osum.v3(2)[:, :, cs_], aap=pob[hh].v3(2)[:, :, cs_], bap=o_f[tix].v3(2)[:, :, cs_])
                    state_update("b", tix)
                    pg = PF(5, 512)
                    for kc in range(KC):
                        mm(pg, hT[:, kc * 512 + tix * 128: kc * 512 + (tix + 1) * 128],
                           w_in_s(kc, 1728, 2240), kc == 0, kc == KC - 1)
                    act(eg, pg, AF.Exp, scale=-1.0)
                    act(Lg, eg, AF.Ln, bias=1.0)
                    act(eg, Lg, AF.Exp, scale=-1.0)
                    tt(Gg, pg, gla4_b, ALU.mult)
                    tt(Gg, Gg, eg, ALU.mult)
                    for h in range(GH):
                        act(junkA[:, h * 128:(h + 1) * 128], osum[:, h * 128:(h + 1) * 128], AF.Square,
                            accum=ssq[:, h:h + 1])
                    rstd_from_ss(tiny[:, 16:20], ssq, tiny[:, 20:24], tiny[:, 24:28], 128)
                    for h in range(GH):
                        stt(ogb[:, h * 128:(h + 1) * 128], osum[:, h * 128:(h + 1) * 128], tiny[:, 16 + h:17 + h],
                            Gg[:, h * 128:(h + 1) * 128], ALU.mult, ALU.mult)
                    pt = PB(4, 512)
                    for h in range(GH):
                        tr(pt[:, h * 128:(h + 1) * 128], ogb[:, h * 128:(h + 1) * 128], ident)
                    cp(ogT, pt, eng="act", oap=ogT.v3(GH)[:, :, tcs], iap=pt.v3(GH))
                if is_prompt:
                    si = seqs.index(seq)
                    for pr in range(2):
                        dma(nsf_d[si, pr], Sst["f"][cur["f"]][pr].ap, reads=[Sst["f"][cur["f"]][pr]])
                        dma(nsb_d[si, pr], Sst["b"][cur["b"]][pr].ap, reads=[Sst["b"][cur["b"]][pr]])
                else:
                    pass

        phase(2)
        own_group(xown_d, 1, False, ckvT_s, krt_s, [256 + 1536 + t * 128 for t in range(4)], qT_s, ogT_s,
                  [[0, 1, 2, 3]])
        phase(3)
        own_group(xp_d, 0, True, ckvT_p, krt_p, [0, 128, 256, 384], qT_p, ogT_p, [[0, 1], [2, 3]])

        phase(4)
        Cb = Bump(C_BASE, D_BASE)
        w_ukv = Cb.b(2 * 1024)
        w_outA = Cb.b(NH * 1024, p=64)
        w_outB = Cb.b(GH * 1024)
        kTh = [Cb.b(2304, p=96) for _ in range(2)]
        dma(w_ukv.v3(2), wukv_d.rearrange("(k p) n -> p k n", p=128), writes=[w_ukv], q="pool")
        dma(w_outA.v3(NH), wout_d[0:512, :].rearrange("(h d) n -> d h n", d=64), writes=[w_outA], q="pool")
        dma(w_outB.v3(GH), wout_d[512:1024, :].rearrange("(h e) n -> e h n", e=128), writes=[w_outB], q="pool")

        E = Bump(E_BASE, SB_BYTES)
        x1 = [E.f(1024) for _ in range(8)]
        Vp = E.b(18 * NH * 65)
        attnT = E.b(NH * 512, p=65)
        PT = [E.b(512) for _ in range(4)]
        rden = E.f(512, p=65)
        bcs = E.f(512, p=64)
        mixh = E.f(512)
        gt1s = E.f(1024)
        assert E.cur <= SB_BYTES

        def attention(ckvT, krt, nkeys_list, qT, q_groups):
            nk = (ckvT.c1 - ckvT.c0) // 2
            ntile = nk // 128
            mset(Vp, 1.0, oap=Vp.ap)
            for kt in range(ntile):
                pv = PF(kt % 2, 512)
                for c2 in range(2):
                    mm(pv, ckvT[:, c2 * nk + kt * 128: c2 * nk + (kt + 1) * 128],
                       w_ukv[:, c2 * 1024 + 512: c2 * 1024 + 1024], c2 == 0, c2 == 1)
                dstap = Vp[:, kt * NH * 65:(kt + 1) * NH * 65].v3(NH)[:, :, 0:64]
                cp(Vp[:, kt * NH * 65:(kt + 1) * NH * 65], pv, eng="act" if kt % 2 else "dve",
                   oap=dstap, iap=pv.v3(NH))
            def build(h):
                kt_h = kTh[h % 2]
                for kb in range(0, nk, 512):
                    n = min(512, nk - kb)
                    pk = PF([3, 1][(kb // 512) % 2], n)
                    for c2 in range(2):
                        mm(pk[0:64, :], w_ukv[:, c2 * 1024 + h * 64: c2 * 1024 + (h + 1) * 64],
                           ckvT[:, c2 * nk + kb: c2 * nk + kb + n], c2 == 0, c2 == 1)
                    cp(kt_h[0:64, kb:kb + n], pk[0:64, :], eng="act" if (kb // 512) % 2 else "dve")
                cp(kt_h[64:96, 0:nk], krt[64:96, 0:nk], eng="dve")

            pending = []

            def finish(h, q0, nq, pacc):
                if nq == 512:
                    recip(rden[64:65, 0:nq], pacc[64:65, :])
                else:
                    act(rden[64:65, 256:256 + nq], pacc[64:65, :], AF.Ln)
                    act(rden[64:65, 0:nq], rden[64:65, 256:256 + nq], AF.Exp, scale=-1.0)
                pbc = PF(0, nq)
                mm(pbc[0:64, :], ones_fA[64:65, 0:64], rden[64:65, 0:nq], True, True)
                cp(bcs[:, 0:nq], pbc[0:64, :], eng="act")
                tt(attnT[0:64, h * 512 + q0: h * 512 + q0 + nq], pacc[0:64, :], bcs[:, 0:nq], ALU.mult)

            SCB = [4, 5, 2]
            build(0)
            gi = 0
            for h in range(NH):
                kt_h = kTh[h % 2]
                if h + 1 < NH:
                    build(h + 1)
                for (q0, nq, ktiles) in q_groups:
                    pacc = PF(6 + gi % 2, nq)
                    gi += 1
                    n = len(ktiles)
                    pscs = {}

                    def score(i, kt_h=kt_h, h=h, q0=q0, nq=nq, ktiles=ktiles, pscs=pscs):
                        psc = PF(SCB[i % 3], nq)
                        pscs[i] = psc
                        kt = ktiles[i]
                        mm(psc, kt_h[0:96, kt * 128:(kt + 1) * 128], qT[0:96, h * 512 + q0: h * 512 + q0 + nq],
                           True, True)

                    for i in range(min(2, n)):
                        score(i)
                    while pending:
                        finish(*pending.pop(0))
                    for i, kt in enumerate(ktiles):
                        pt_ = PT[i % 4][:, 0:nq]
                        act(pt_, pscs[i], AF.Exp, scale=SCALE)
                        if i + 2 < n:
                            score(i + 2)
                        mm(pacc[0:65, :], Vp[:, (kt * NH + h) * 65:(kt * NH + h + 1) * 65], pt_,
                           i == 0, i == n - 1)
                    pending.append((h, q0, nq, pacc))
            while pending:
                finish(*pending.pop(0))

        def out_proj(x_d, modr, ogT, x1s):
            dma(gt1s.ap, mod2_d[6 + modr], reads=[("mod2", 0, 1, (6 + modr) * 10, (6 + modr) * 10 + 10)],
                writes=[gt1s])
            for tix in range(4):
                xt = x1s[tix]
                dma(xt.ap, x_d[tix * 128:(tix + 1) * 128, :], writes=[xt])
                for half in range(2):
                    pm = PF(2 + half, 512)
                    cs = slice(half * 512, (half + 1) * 512)
                    for h in range(NH):
                        mm(pm, attnT[0:64, h * 512 + tix * 128: h * 512 + (tix + 1) * 128],
                           w_outA[0:64, h * 1024 + half * 512: h * 1024 + (half + 1) * 512], h == 0, False)
                    for h in range(GH):
                        mm(pm, ogT[:, h * 512 + tix * 128: h * 512 + (tix + 1) * 128],
                           w_outB[:, h * 1024 + half * 512: h * 1024 + (half + 1) * 512], False, h == GH - 1)
                    tt(mixh, pm, gt1s[:, cs], ALU.mult)
                    tt(xt[:, cs], xt[:, cs], mixh, ALU.add)

        wblk = [SBf(70 * K1, KC * 512), SBf(196 * K1, KC * 512)]
        NWB = NFF // 2

        def load_wblk(wb):
            buf = wblk[wb % 2]
            v = buf.v3(KC)
            dma(v[:, :, 0:256], wffi_d[:, wb * 256:(wb + 1) * 256].rearrange("(k p) n -> p k n", p=128),
                writes=[buf], q="pool")
            dma(v[:, :, 256:512],
                wffi_d[:, DFF + wb * 256: DFF + (wb + 1) * 256].rearrange("(k p) n -> p k n", p=128),
                writes=[buf], q="pool")

        load_wblk(0)
        load_wblk(1)
        attention(ckvT_s, krt_s, None, qT_s, [(0, 512, list(range(18)))])
        out_proj(xown_d, 1, ogT_s, x1[0:4])
        phase(5)
        attention(ckvT_p, krt_p, None, qT_p, [(0, 256, [0, 1]), (256, 256, [2, 3])])
        out_proj(xp_d, 0, ogT_p, x1[4:8])

        phase(6)
        w_ffo = SBf(C_BASE, NFF * 1024)
        h2T = SBf(D_BASE, KC * 1024)
        uT = [SBf(94 * K1, NFF * 512), SBf(151 * K1, NFF * 512)]
        Ff = Bump(174 * K1, SB_BYTES)
        silu_t = [Ff.f(512) for _ in range(2)]
        h2b = [Ff.b(1024) for _ in range(2)]
        junk2 = Ff.f(1024)
        junkA2 = Ff.b(1024)
        mods2 = [SF(10 * K1 + 4096 * i, 1024) for i in range(4)] + [Ff.f(1024) for _ in range(2)]
        for i in range(6):
            dma(mods2[i].ap, mod2_d[i], reads=[("mod2", 0, 1, i * 10, i * 10 + 10)], writes=[mods2[i]])

        for t8 in range(8):
            r = 1 if t8 < 4 else 0
            xt = x1[t8]
            ss, t1, t2, rs = tslot()
            act(junkA2, xt, AF.Square, accum=ss)
            rstd_from_ss(rs, ss, t1, t2, D)
            hb = h2b[t8 % 2]
            stt(junk2, xt, rs, mods2[1 * 2 + r], ALU.mult, ALU.mult)
            tt(hb, junk2, mods2[0 * 2 + r], ALU.add)
            pt = PB(4 + t8 % 2)
            for kc in range(KC):
                tr(pt[:, kc * 128:(kc + 1) * 128], hb[:, kc * 128:(kc + 1) * 128], ident)
            cp(h2T, pt, eng="act", oap=h2T.v3(KC)[:, :, t8 * 128:(t8 + 1) * 128], iap=pt.v3(KC))
        for wb in range(NWB):
            buf = wblk[wb % 2]
            for sub in range(2):
                fb = 2 * wb + sub
                for g in range(2):
                    pa = PF(0 + g * 2, 512)
                    pg = PF(1 + g * 2, 512)
                    for kc in range(KC):
                        mm(pa, buf[:, kc * 512 + sub * 128: kc * 512 + sub * 128 + 128],
                           h2T[:, kc * 1024 + g * 512: kc * 1024 + (g + 1) * 512], kc == 0, kc == KC - 1)
                    for kc in range(KC):
                        mm(pg, buf[:, kc * 512 + 256 + sub * 128: kc * 512 + 256 + sub * 128 + 128],
                           h2T[:, kc * 1024 + g * 512: kc * 1024 + (g + 1) * 512], kc == 0, kc == KC - 1)
                    st_ = silu_t[g]
                    act(st_, pa, AF.Silu)
                    tt(uT[g][:, fb * 512:(fb + 1) * 512], st_, pg, ALU.mult)
            if wb + 2 < NWB:
                load_wblk(wb + 2)
            if wb < 3:
                k0, k1 = [(0, 8), (8, 16), (16, NFF)][wb]
                dma(w_ffo.v3(NFF)[:, k0:k1, :], wffo_d[k0 * 128:k1 * 128, :].rearrange("(k p) n -> p k n", p=128),
                    writes=[w_ffo[:, k0 * 1024:k1 * 1024]], q="pool")
        fin_b = SF(D_BASE, 1024)
        dma(fin_b.ap, fin_d, reads=[h2T], writes=[fin_b])
        yst = [SF(D_BASE + 4096 * (1 + i), 1024) for i in range(2)]
        for t8 in range(8):
            r = 1 if t8 < 4 else 0
            g, tg = t8 // 4, t8 % 4
            xt = x1[t8]
            for half in range(2):
                pm = PF(4 + half + 2 * (t8 % 2), 512)
                cs = slice(half * 512, (half + 1) * 512)
                for fc in range(NFF):
                    mm(pm, uT[g][:, fc * 512 + tg * 128: fc * 512 + (tg + 1) * 128],
                       w_ffo[:, fc * 1024 + half * 512: fc * 1024 + (half + 1) * 512], fc == 0, fc == NFF - 1)
                tt(junk2[:, cs], pm, mods2[2 * 2 + r][:, cs], ALU.mult)
            tt(xt, xt, junk2, ALU.add)
            ss, t1, t2, rs = tslot()
            act(junkA2, xt, AF.Square, accum=ss)
            rstd_from_ss(rs, ss, t1, t2, D)
            yo = yst[t8 % 2]
            stt(yo, xt, rs, fin_b, ALU.mult, ALU.mult)
            if t8 < 4:
                dma(ys_d[t8 * 128:(t8 + 1) * 128, :], yo.ap, reads=[yo])
            else:
                dma(yp_d[(t8 - 4) * 128:(t8 - 3) * 128, :], yo.ap, reads=[yo])

        P.finalize()
        P.emit(nc, es)
    return nc


_ROPE_P = np.array(list(range(8, 16)) + list(range(0, 8)) + list(range(24, 32)) + list(range(16, 24)))
_ROPE_S = np.array([-1.0] * 8 + [1.0] * 8 + [-1.0] * 8 + [1.0] * 8, dtype=np.float32)


def _rope_tables(n_tokens):
    t = np.arange(n_tokens)
    row = (t // 64).astype(np.float32)
    col = (t % 64).astype(np.float32)
    half = 16
    inv = (np.float32(10000.0) ** (-np.arange(0, half, 2, dtype=np.float32) / np.float32(half))).astype(np.float32)
    ang_r = row[:, None] * inv
    ang_c = col[:, None] * inv
    ang = np.concatenate([ang_r, ang_r, ang_c, ang_c], axis=-1).astype(np.float32)
    return np.cos(ang).astype(np.float32), (np.sin(ang).astype(np.float32) * _ROPE_S[None, :])


def _bc(v, n=128):
    return np.ascontiguousarray(np.broadcast_to(np.asarray(v, np.float32).reshape(1, -1), (n, v.size)))


_WIN_SEGS = ((384, 672, 0), (2240, 2272, 288), (928, 1184, 320), (1184, 1696, 576), (1696, 1728, 1088),
             (0, 384, 1120), (672, 928, 1504), (1728, 2240, 1760))
_WIN_SPLIT = 1120


def _win_col(c0, c1):
    for a, b, d in _WIN_SEGS:
        if a <= c0 and c1 <= b:
            return d + (c0 - a), d + (c1 - a)
    raise ValueError((c0, c1))


_NC_CACHE = {}


def kernel(x_prompt, x_sample, cache_kv_latent, cache_k_rope, state_gla_fwd, state_gla_bwd,
           c, c_ctx, w_ada, b_ada, norm_attn, w_in, mla_q_norm, w_uq, mla_kv_norm, w_ukv,
           w_gate_f, b_gate_f, w_gate_b, b_gate_b, gla_norm, w_out, norm_ffn, w_ffn_in,
           w_ffn_out, final_norm):
    f32 = np.float32
    A = lambda a: np.ascontiguousarray(np.asarray(a, dtype=f32))
    x_prompt, x_sample = A(x_prompt), A(x_sample)
    cos_all, sin_all = _rope_tables(2048)
    w_in0 = A(w_in)[0]
    w_in_nat = np.concatenate([w_in0, w_in0[:, 640:672][:, _ROPE_P]], axis=1)
    w_in_dev = np.zeros_like(w_in_nat)
    for a_, b_, d_ in _WIN_SEGS:
        w_in_dev[:, d_:d_ + (b_ - a_)] = w_in_nat[:, a_:b_]
    w_in_dev = np.ascontiguousarray(w_in_dev)
    w_uq0 = A(w_uq)[0]
    w_uqp = np.zeros((384, 768), f32)
    for h in range(8):
        w_uqp[:, h * 96 + 64:(h + 1) * 96] = w_uq0[:, h * 96 + 64:(h + 1) * 96][:, _ROPE_P]
    w_ukv0 = A(w_ukv)[0].reshape(256, 8, 128)
    w_ukv2 = np.ascontiguousarray(np.concatenate(
        [w_ukv0[:, :, :64].reshape(256, 512), w_ukv0[:, :, 64:].reshape(256, 512)], axis=1))
    wgf, wgb = A(w_gate_f)[0], A(w_gate_b)[0]
    bgf, bgb = A(b_gate_f)[0], A(b_gate_b)[0]
    Wf = np.zeros((33, 256), f32); Wf[0:16] = wgf; Wf[32] = bgf
    Wb = np.zeros((33, 256), f32); Wb[16:32] = wgb; Wb[32] = bgb
    Wown = np.ascontiguousarray(np.concatenate([Wf, Wb], axis=1))
    s, t = np.arange(128)[:, None], np.arange(128)[None, :]
    ng = f32(-1.0 / 16.0)
    tri = np.concatenate([(s <= t) * ng, (s >= t) * ng, (s > t) * ng, (s < t) * ng], axis=1).astype(f32)
    mask = np.concatenate([(s <= t), (s >= t)], axis=1).astype(f32)
    ident = np.eye(128, dtype=f32)
    shared = {
        "w_ada": A(w_ada)[0], "b_ada": A(b_ada)[0].reshape(1, -1),
        "norm_attn_b": _bc(A(norm_attn)[0]), "norm_ffn_b": _bc(A(norm_ffn)[0]),
        "w_in": w_in_dev, "w_uq": w_uq0, "w_uqp": w_uqp, "w_ukv2": w_ukv2, "w_out": A(w_out)[0],
        "w_ffn_in": A(w_ffn_in)[0], "w_ffn_out": A(w_ffn_out)[0],
        "kv_norm_b": _bc(A(mla_kv_norm)[0]), "q_norm_b": _bc(A(mla_q_norm)[0]),
        "gla_norm4_b": _bc(np.tile(A(gla_norm)[0], 4)), "final_norm_b": _bc(A(final_norm)),
        "ident": ident, "tri": np.ascontiguousarray(tri), "mask": np.ascontiguousarray(mask),
        "Wown": Wown,
    }
    sf = A(state_gla_fwd)[:, 0].reshape(2, 2, 128, 128)
    sbw = A(state_gla_bwd)[:, 0].reshape(2, 2, 128, 128)
    in_maps = []
    for core in range(8):
        b, qd = core // 4, core % 4
        nb = 12 - 4 * qd
        bw_tiles = list(range(15, 4 * qd + 3, -1))
        fw_tiles = list(range(0, 4 * qd))
        xs = x_sample[b]
        pre, cosk, sink = [], [], []
        Wstep = np.zeros((33, 12 * 256), f32)
        for j, tl in enumerate(bw_tiles + fw_tiles):
            rows = np.arange(tl * 128, (tl + 1) * 128)
            if j < nb:
                rows = rows[::-1]
            pre.append(xs[rows]); cosk.append(cos_all[rows]); sink.append(sin_all[rows])
            Wstep[:, j * 256:(j + 1) * 256] = Wb if j < nb else Wf
        own_rows = np.arange(4 * qd * 128, (4 * qd + 4) * 128)
        cosk.append(cos_all[own_rows]); sink.append(sin_all[own_rows])
        keep = np.ones(13, f32); cap = np.zeros(13, f32)
        INIT = np.zeros((13, 2, 128, 128), f32)
        keep[0] = 0.0
        INIT[0] = sbw[b] if nb > 0 else sf[b]
        if nb > 0:
            keep[nb] = 0.0; cap[nb] = 1.0; INIT[nb] = sf[b]
        SB0 = sbw[b] if nb == 0 else np.zeros((2, 128, 128), f32)
        keepcap = _bc(np.concatenate([keep, cap]))
        cond2 = np.stack([A(c_ctx), A(c)[b]], axis=0)
        condT = np.ascontiguousarray(cond2.reshape(2, 8, 128).transpose(2, 1, 0).reshape(128, 16))
        m = dict(shared)
        m.update({
            "xp": np.ascontiguousarray(x_prompt[2 * core:2 * core + 2].reshape(512, 1024)),
            "xs_pre": np.ascontiguousarray(np.concatenate(pre, axis=0)),
            "xs_own": np.ascontiguousarray(xs[own_rows]),
            "ckv_ctx": A(cache_kv_latent)[b, 0], "kr_ctx": A(cache_k_rope)[b, 0],
            "cosk": np.ascontiguousarray(np.concatenate(cosk, axis=0)),
            "sink": np.ascontiguousarray(np.concatenate(sink, axis=0)),
            "cosq": np.ascontiguousarray(cos_all[own_rows].T), "sinq": np.ascontiguousarray(sin_all[own_rows].T),
            "R0": np.zeros((2, 128, 128), f32), "SB0": np.ascontiguousarray(SB0),
            "INIT": INIT, "keepcap": keepcap, "Wstep": Wstep, "condT": condT,
        })
        in_maps.append(m)
    if "nc" not in _NC_CACHE:
        _NC_CACHE["nc"] = build_program()
    res = run_bass_kernel_spmd(_NC_CACHE["nc"], in_maps, core_ids=list(range(8)))
    R = res.results
    y_prompt = np.stack([np.asarray(R[cidx]["y_p"]).reshape(2, 256, 1024) for cidx in range(8)]).reshape(16, 256, 1024)
    y_sample = np.stack([np.asarray(R[cidx]["y_s"]) for cidx in range(8)]).reshape(2, 2048, 1024)
    new_kv = np.stack([np.asarray(R[cidx]["new_kv"]).reshape(2, 256, 256) for cidx in range(8)]).reshape(16, 1, 256, 256)
    new_kr = np.stack([np.asarray(R[cidx]["new_kr"]).reshape(2, 256, 32) for cidx in range(8)]).reshape(16, 1, 256, 32)
    new_sf = np.stack([np.asarray(R[cidx]["new_sf"]) for cidx in range(8)]).reshape(16, 1, 4, 64, 128)
    new_sb = np.stack([np.asarray(R[cidx]["new_sb"]) for cidx in range(8)]).reshape(16, 1, 4, 64, 128)
    return (y_prompt.astype(f32), y_sample.astype(f32), new_kv.astype(f32), new_kr.astype(f32),
            new_sf.astype(f32), new_sb.astype(f32))
```

```python
import math
import os
import numpy as np
import concourse.bass as bass
import concourse.mybir as mybir
from concourse.bass_utils import run_bass_kernel_spmd
from contextlib import ExitStack

F32 = mybir.dt.float32
BF16 = mybir.dt.bfloat16
AF = mybir.ActivationFunctionType
ALU = mybir.AluOpType

N_DMA_SEMS = 8


def _env(name, default):
    if os.environ.get("KDEBUG") == "1":
        return os.environ.get(name, default)
    return default
ENGINES = ("pe", "act", "dve", "pool", "sp")


class T:
    def __init__(self, arena, base, esize, p0, p1, c0, c1):
        self.arena, self.base, self.esize = arena, base, esize
        self.p0, self.p1, self.c0, self.c1 = p0, p1, c0, c1

    def __getitem__(self, key):
        if not isinstance(key, tuple):
            key = (key, slice(None))
        ps, cs = key
        a, b, _ = ps.indices(self.p1 - self.p0)
        c, d, _ = cs.indices(self.c1 - self.c0)
        return T(self.arena, self.base, self.esize, self.p0 + a, self.p0 + b, self.c0 + c, self.c0 + d)

    @property
    def ap(self):
        return self.base[self.p0:self.p1, self.c0:self.c1]

    @property
    def reg(self):
        return (self.arena, self.p0, self.p1, self.c0 * self.esize, self.c1 * self.esize)

    def v3(self, a):
        return self.ap.rearrange("p (a b) -> p a b", a=a)


class Prog:
    WINDOW = int(_env("KWINDOW", "100000"))
    SCHED = _env("KSCHED", "1") == "1"
    BARRIERS = _env("KBARRIERS", "0") == "1"
    SLACK = float(_env("KSLACK", "0.0"))

    SCHED_PH = _env("KSCHED_PH", "")

    def __init__(self):
        self.all = []
        self.cur_phase = 0.0
        self.ops = {e: [] for e in ENGINES}
        self.count = {e: 0 for e in ENGINES}
        self.dma_last = {}

    @staticmethod
    def _overlap(a, b):
        return a[1] < b[2] and b[1] < a[2] and a[3] < b[4] and b[3] < a[4]

    @staticmethod
    def _contains(a, b):
        return a[1] <= b[1] and b[2] <= a[2] and a[3] <= b[3] and b[4] <= a[4]

    enabled = True

    def op(self, eng, fn, reads=(), writes=(), dma=False, cost=None):
        if not self.enabled:
            return None

        def norm(r):
            r = r.reg if isinstance(r, T) else r
            if r[0] == "ps":
                return ("ps", r[1] // 32 * 32, (r[2] + 31) // 32 * 32, r[3] // 2048 * 2048, (r[4] + 2047) // 2048 * 2048)
            return r

        raw_r = [r.reg if isinstance(r, T) else r for r in reads]
        raw_w = [w.reg if isinstance(w, T) else w for w in writes]
        self.all.append(dict(eng=eng, fn=fn, reads=[norm(r) for r in reads], writes=[norm(w) for w in writes],
                             dma=dma, raw_r=raw_r, raw_w=raw_w, cost=cost, ph=self.cur_phase))
        return None

    def _deps(self):
        records = {}
        preds = []
        for i, o in enumerate(self.all):
            p = set()
            for r in o["reads"]:
                for rec in records.get(r[0], ()):
                    if rec[1] == "W" and self._overlap(r, rec[0]):
                        p.add(rec[2])
                    elif (r[0] == "ps" and rec[1] == "R" and r[3] < rec[0][4] and rec[0][3] < r[4]
                          and self.all[rec[2]]["eng"] != o["eng"]):
                        p.add(rec[2])
            for w in o["writes"]:
                for rec in records.get(w[0], ()):
                    if self._overlap(w, rec[0]):
                        p.add(rec[2])
            p.discard(i)
            preds.append(p)
            for w in o["writes"]:
                lst = records.setdefault(w[0], [])
                lst[:] = [rec for rec in lst if not self._contains(w, rec[0])]
                lst.append((w, "W", i))
            for r in o["reads"]:
                lst = records.setdefault(r[0], [])
                lst[:] = [rec for rec in lst if not (rec[1] == "R" and rec[0] == r and rec[2] in p)]
                lst.append((r, "R", i))
        return preds

    def _cost(self, o):
        if o["cost"] is not None:
            return o["cost"]
        w = o["raw_w"][0] if o["raw_w"] else o["raw_r"][0]
        width = max(1, (w[4] - w[3]) // 4)
        e = o["eng"]
        if o["dma"]:
            return 0.15
        if e == "pe":
            return 0.05 + width / 2100.0
        if e == "act":
            return 0.2 + width / 1100.0
        if e == "pool":
            return 0.2 + width / 400.0
        return 0.12 + width / 800.0

    def _dma_latency(self, o):
        w = o["raw_w"][0] if o["raw_w"] else o["raw_r"][0]
        nbytes = (w[2] - w[1]) * (w[4] - w[3])
        return 2.0 + nbytes / 120e3

    def finalize(self):
        n = len(self.all)
        preds = self._deps()
        order = list(range(n))
        order_only = [set() for _ in range(n)]
        fix = [e for e in _env("KFIX", "").split(",") if e]
        for e in fix:
            prev = None
            for i, o in enumerate(self.all):
                if o["eng"] == e:
                    if prev is not None and prev not in preds[i]:
                        preds[i].add(prev)
                        order_only[i].add(prev)
                    prev = i
        if self.SCHED:
            succs = [[] for _ in range(n)]
            npred = [len(p) for p in preds]
            for i, p in enumerate(preds):
                for j in p:
                    succs[j].append(i)
            finish = [0.0] * n
            rdy_t = [0.0] * n
            blevel = [0.0] * n
            for i in range(n - 1, -1, -1):
                o = self.all[i]
                c = self._cost(o) + (self._dma_latency(o) if o["dma"] else 0.0)
                m = 0.0
                for k in succs[i]:
                    if blevel[k] > m:
                        m = blevel[k]
                blevel[i] = c + m
            efree = {e: 0.0 for e in ENGINES}
            dma_free = [0.0]
            ready = [i for i in range(n) if npred[i] == 0]
            allowed = None
            if self.SCHED_PH:
                allowed = {float(x) for x in self.SCHED_PH.split(",")}
            done = [False] * n
            lowest = 0
            order = []
            LAT = 0.25
            while len(order) < n:
                while lowest < n and done[lowest]:
                    lowest += 1
                ph = self.all[lowest]["ph"]
                lim = lowest + self.WINDOW
                if allowed is not None and ph not in allowed:
                    lim = lowest + 1
                elif self.BARRIERS:
                    k = lowest
                    while k < n and k < lim and self.all[k]["ph"] == ph:
                        k += 1
                    lim = k
                best, best_t, bkey = None, None, None
                cands = []
                tmin = None
                for i in ready:
                    if i >= lim:
                        continue
                    t = max(efree[self.all[i]["eng"]], rdy_t[i])
                    cands.append((t, i))
                    if tmin is None or t < tmin:
                        tmin = t
                for t, i in cands:
                    if t <= tmin + self.SLACK:
                        key = (-blevel[i], i)
                        if best is None or key < bkey:
                            best, best_t, bkey = i, t, key
                if best is None:
                    best = min(ready)
                    best_t = max(efree[self.all[best]["eng"]], rdy_t[best])
                o = self.all[best]
                c = self._cost(o)
                efree[o["eng"]] = best_t + c
                if o["dma"]:
                    st = max(best_t + c, dma_free[0])
                    lat = self._dma_latency(o)
                    dma_free[0] = st + (lat - 2.0)
                    finish[best] = st + lat
                else:
                    finish[best] = best_t + c
                done[best] = True
                ready.remove(best)
                order.append(best)
                for k in succs[best]:
                    npred[k] -= 1
                    rdy_t[k] = max(rdy_t[k], finish[best] + LAT)
                    if npred[k] == 0:
                        ready.append(k)
        if self.SCHED and _env("KVERBOSE", ""):
            print("sched makespan(us)", max(finish), "window", self.WINDOW, "slack", self.SLACK)
        tok = [None] * n
        seen = {e: {} for e in ENGINES}
        dma_rr = {"sp": 0, "pool": 0}
        pe_seq = []
        for i in order:
            o = self.all[i]
            eng = o["eng"]
            need = {}

            def want(t):
                if need.get(t[0], 0) < t[1]:
                    need[t[0]] = t[1]

            for j in preds[i]:
                assert tok[j] is not None, "dependency scheduled after its consumer"
                if eng == "pe" and self.all[j]["eng"] == "pe":
                    continue
                if j in order_only[i]:
                    continue
                want(tok[j])
            if o["dma"]:
                k = dma_rr[eng]
                dma_rr[eng] = (k + 1) % N_DMA_SEMS
                skey = f"dma_{eng}_{k}"
                last = self.dma_last.get(skey, 0)
                if last:
                    want((skey, last))
                tok[i] = (skey, last + 16)
                self.dma_last[skey] = last + 16
                inc = (skey, 16)
            else:
                self.count[eng] += 1
                tok[i] = (f"e_{eng}", self.count[eng])
                inc = (f"e_{eng}", 1)
                if eng == "pe":
                    pe_seq.append(i)
            waits = []
            sn = seen[eng]
            for sk, v in need.items():
                if sn.get(sk, 0) < v:
                    sn[sk] = v
                    waits.append((sk, v))
            self.ops[eng].append((waits, o["fn"], inc))
        info = []
        for i in pe_seq:
            o = self.all[i]
            r, w = o["raw_r"][0], o["raw_w"][0]
            info.append(((r[1] // 32, (r[2] + 31) // 32), (w[3] // 2048, (w[4] + 2047) // 2048)))
        for a in range(len(info)):
            for b in range(a + 1, min(a + 5, len(info))):
                (a0, a1), (b0, b1) = info[a]
                (c0, c1), (d0, d1) = info[b]
                if (a1 <= c0 or c1 <= a0) and (b0 < d1 and d0 < b1):
                    raise RuntimeError(f"PE row-group/bank hazard between PE ops {a} and {b}: {info[a]} {info[b]}")

    def emit(self, nc, es):
        keys = [f"e_{e}" for e in ENGINES]
        for q in ("sp", "pool"):
            keys += [f"dma_{q}_{i}" for i in range(N_DMA_SEMS)]
        sems = {k: es.enter_context(nc.semaphore(k)) for k in keys}
        finals = [(f"e_{e}", self.count[e]) for e in ENGINES if self.count[e]]
        finals += list(self.dma_last.items())
        ops = self.ops
        block = es.enter_context(nc.Block())

        def run(engine_obj, name, last=False):
            for waits, fn, inc in ops[name]:
                for s, v in waits:
                    engine_obj.wait_ge(sems[s], v)
                fn(engine_obj).then_inc(sems[inc[0]], inc[1])
            if last:
                for s, v in finals:
                    engine_obj.wait_ge(sems[s], v)

        @block.tensor
        def _(e):
            run(e, "pe")

        @block.scalar
        def _(e):
            run(e, "act")

        @block.vector
        def _(e):
            run(e, "dve")

        @block.gpsimd
        def _(e):
            run(e, "pool")

        @block.sync
        def _(e):
            run(e, "sp", last=True)


D = 1024
KC = 8
NH = 8
QK = 96
GH = 4
DFF = 2816
NFF = DFF // 128
EPS = 1e-6
SB_BYTES = 204 * 1024
NPRE = 12
SCALE = QK ** -0.5

K1 = 1024
A_BASE = 0
C_BASE = 26 * K1
D_BASE = 78 * K1
E_BASE = 119 * K1


def build_program():
    nc = bass.Bass("TRN2", target_bir_lowering=False)
    P = Prog()
    _kstop = float(_env("KSTOP", "99"))

    def phase(n):
        P.enabled = n <= _kstop
        P.cur_phase = float(n)

    def din(name, shape):
        return nc.dram_tensor(name, list(shape), F32, kind="ExternalInput").ap()

    def dout(name, shape):
        return nc.dram_tensor(name, list(shape), F32, kind="ExternalOutput").ap()

    xp_d = din("xp", [512, D])
    xpre_d = din("xs_pre", [NPRE * 128, D])
    xown_d = din("xs_own", [512, D])
    ckvctx_d = din("ckv_ctx", [256, 256])
    krctx_d = din("kr_ctx", [256, 32])
    cosk_d = din("cosk", [2048, 32])
    sink_d = din("sink", [2048, 32])
    cosq_d = din("cosq", [32, 512])
    sinq_d = din("sinq", [32, 512])
    r0_d = din("R0", [2, 128, 128])
    sb0_d = din("SB0", [2, 128, 128])
    init_d = din("INIT", [13, 2, 128, 128])
    keepcap_d = din("keepcap", [128, 26])
    wstep_d = din("Wstep", [33, NPRE * 256])
    wown_d = din("Wown", [33, 512])
    condT_d = din("condT", [128, 16])
    wada_d = din("w_ada", [D, 6 * D])
    bada_d = din("b_ada", [1, 6 * D])
    nattn_d = din("norm_attn_b", [128, D])
    nffn_d = din("norm_ffn_b", [128, D])
    win_d = din("w_in", [D, 2272])
    wuq_d = din("w_uq", [384, 768])
    wuqp_d = din("w_uqp", [384, 768])
    wukv_d = din("w_ukv2", [256, 1024])
    wout_d = din("w_out", [D, D])
    wffi_d = din("w_ffn_in", [D, 2 * DFF])
    wffo_d = din("w_ffn_out", [DFF, D])
    kvn_d = din("kv_norm_b", [128, 256])
    qn_d = din("q_norm_b", [128, 384])
    gla4_d = din("gla_norm4_b", [128, 512])
    fin_d = din("final_norm_b", [128, D])
    ident_d = din("ident", [128, 128])
    tri_d = din("tri", [128, 4 * 128])
    mask_d = din("mask", [128, 2 * 128])
    yp_d = dout("y_p", [512, D])
    ys_d = dout("y_s", [512, D])
    nkv_d = dout("new_kv", [512, 256])
    nkr_d = dout("new_kr", [512, 32])
    nsf_d = dout("new_sf", [2, 2, 128, 128])
    nsb_d = dout("new_sb", [2, 2, 128, 128])
    mod2_d = nc.dram_tensor("mod2_scr", [8, 128, D], F32).ap()

    es = ExitStack()
    with es:
        sb = es.enter_context(nc.sbuf_tensor("sb", [128, SB_BYTES // 4], F32))
        ps = es.enter_context(nc.psum_tensor("ps", [128, 4096], F32))
        sbF = sb[:]
        sbB = sb[:].bitcast(BF16)
        psF = ps[:]
        psB = ps[:].bitcast(BF16)

        def SF(off, n, p=128):
            assert off % 4 == 0 and off + 4 * n <= SB_BYTES, (off, n)
            return T("sb", sbF, 4, 0, p, off // 4, off // 4 + n)

        def SBf(off, n, p=128):
            assert off % 2 == 0 and off + 2 * n <= SB_BYTES, (off, n)
            return T("sb", sbB, 2, 0, p, off // 2, off // 2 + n)

        def PF(bank, n=512, c0=0, p=128):
            return T("ps", psF, 4, 0, p, bank * 512 + c0, bank * 512 + c0 + n)

        def PB(bank, n=1024, c0=0, p=128):
            return T("ps", psB, 2, 0, p, bank * 1024 + c0, bank * 1024 + c0 + n)

        class Bump:
            def __init__(self, base, limit):
                self.cur, self.limit = base, limit

            def f(self, n, p=128):
                self.cur = (self.cur + 63) // 64 * 64
                t = SF(self.cur, n, p)
                self.cur += 4 * n
                assert self.cur <= self.limit, (self.cur, self.limit)
                return t

            def b(self, n, p=128):
                self.cur = (self.cur + 63) // 64 * 64
                t = SBf(self.cur, n, p)
                self.cur += 2 * n
                assert self.cur <= self.limit, (self.cur, self.limit)
                return t

        def mm(out, lhsT, rhs, start, stop):
            P.op("pe", lambda e: e.matmul(out=out.ap, lhsT=lhsT.ap, rhs=rhs.ap, start=start, stop=stop),
                 reads=[lhsT, rhs], writes=[out])

        def tr(out, in_, ident):
            P.op("pe", lambda e: e.transpose(out=out.ap, in_=in_.ap, identity=ident.ap),
                 reads=[in_, ident], writes=[out])

        def act(out, in_, func, scale=None, bias=None, accum=None, oap=None, iap=None):
            kw = {}
            if scale is not None:
                kw["scale"] = scale.ap if isinstance(scale, T) else scale
            if bias is not None:
                kw["bias"] = bias
            if accum is not None:
                kw["accum_out"] = accum.ap
            rd = [in_] + ([scale] if isinstance(scale, T) else [])
            wr = [out] + ([accum] if accum is not None else [])
            o = oap if oap is not None else out.ap
            i = iap if iap is not None else in_.ap
            P.op("act", lambda e: e.activation(out=o, in_=i, func=func, **kw), reads=rd, writes=wr)

        def ts(out, in0, s1, s2, op0, op1=None, eng="dve", oap=None, iap=None):
            rd = [in0] + [s for s in (s1, s2) if isinstance(s, T)]
            a1 = s1.ap if isinstance(s1, T) else s1
            a2 = s2.ap if isinstance(s2, T) else s2
            o = oap if oap is not None else out.ap
            i = iap if iap is not None else in0.ap
            if op1 is None:
                fn = lambda e: e.tensor_scalar(out=o, in0=i, scalar1=a1, scalar2=None, op0=op0)
            else:
                fn = lambda e: e.tensor_scalar(out=o, in0=i, scalar1=a1, scalar2=a2, op0=op0, op1=op1)
            P.op(eng, fn, reads=rd, writes=[out])

        def tt(out, a, b, op, eng="dve", oap=None, aap=None, bap=None):
            o = oap if oap is not None else out.ap
            x = aap if aap is not None else a.ap
            y = bap if bap is not None else b.ap
            P.op(eng, lambda e: e.tensor_tensor(out=o, in0=x, in1=y, op=op), reads=[a, b], writes=[out])

        def stt(out, a, s, b, op0, op1):
            sa = s.ap if isinstance(s, T) else s
            rd = [a, b] + ([s] if isinstance(s, T) else [])
            P.op("dve", lambda e: e.scalar_tensor_tensor(out=out.ap, in0=a.ap, scalar=sa, in1=b.ap,
                                                         op0=op0, op1=op1), reads=rd, writes=[out])

        def cp(out, in_, eng="dve", oap=None, iap=None):
            o = oap if oap is not None else out.ap
            i = iap if iap is not None else in_.ap
            if eng == "act":
                P.op("act", lambda e: e.activation(out=o, in_=i, func=AF.Copy), reads=[in_], writes=[out])
            else:
                P.op(eng, lambda e: e.tensor_copy(out=o, in_=i), reads=[in_], writes=[out])

        def recip(out, in_):
            P.op("dve", lambda e: e.reciprocal(out=out.ap, in_=in_.ap), reads=[in_], writes=[out])

        def mset(out, val, eng="pool", oap=None):
            o = oap if oap is not None else out.ap
            P.op(eng, lambda e: e.memset(o, val), writes=[out])

        def dma(out_ap, in_ap, reads=(), writes=(), q="sp"):
            P.op(q, lambda e: e.dma_start(out=out_ap, in_=in_ap), reads=reads, writes=writes, dma=True)

        def rstd_from_ss(rstd, ss, tmp1, tmp2, n):
            ts(tmp1, ss, 1.0 / n, EPS, ALU.mult, ALU.add)
            act(tmp2, tmp1, AF.Ln)
            act(rstd, tmp2, AF.Exp, scale=-0.5)

        A = Bump(A_BASE, C_BASE)
        ident = A.b(128)
        maskf = A.b(128)
        maskb = A.b(128)
        ones_bf = A.b(512)
        tri = A.f(512)
        keepcap = A.f(26)
        kvn_b = A.f(256)
        qn_b = A.f(384)
        gla4_b = A.f(512)
        tiny = A.f(192)
        _tslot = [0]

        def tslot():
            k = _tslot[0] % 40
            _tslot[0] += 1
            return tuple(tiny[:, 32 + 4 * k + c: 33 + 4 * k + c] for c in range(4))
        condT = A.f(16)
        ones_fA = A.f(64)
        scond = A.f(16)
        assert A.cur <= 10 * K1, A.cur
        mods = [SF(10 * K1 + 4096 * i, 1024) for i in range(4)]
        SH1, GM1 = 0, 1

        tri_f, tri_b, tris_f, tris_b = (tri[:, 128 * i:128 * (i + 1)] for i in range(4))

        dma(ident.ap, ident_d, writes=[ident], q="pool")
        dma(maskf.ap, mask_d[:, 0:128], writes=[maskf], q="pool")
        dma(maskb.ap, mask_d[:, 128:256], writes=[maskb], q="pool")
        dma(tri.ap, tri_d, writes=[tri])
        dma(keepcap.ap, keepcap_d, writes=[keepcap])
        dma(kvn_b.ap, kvn_d, writes=[kvn_b])
        dma(qn_b.ap, qn_d, writes=[qn_b])
        dma(gla4_b.ap, gla4_d, writes=[gla4_b])
        dma(condT.ap, condT_d, writes=[condT])
        mset(ones_bf, 1.0)
        mset(ones_fA, 1.0)

        Cb = Bump(C_BASE, D_BASE)
        w_in = Cb.b(KC * 2272)
        w_uq = Cb.b(3 * 768)
        w_uqp = Cb.b(3 * 768)
        w_own = Cb.b(512, p=33)
        w_step = Cb.b(NPRE * 256, p=33)

        def w_in_s(kc, c0, c1):
            d0, d1 = _win_col(c0, c1)
            return w_in[:, kc * 2272 + d0: kc * 2272 + d1]

        phase(0.5)
        E = Bump(E_BASE, SB_BYTES)
        wada_blk = [E.b(KC * 1024) for _ in range(2)]
        bada_blk = [E.b(1024, p=1) for _ in range(2)]
        lhs_rep = E.b(2 * KC * 128)
        nrm_b = [E.f(1024) for _ in range(2)]
        mstage = [E.f(1024) for _ in range(2)]
        sig = E.f(16)
        dma(nrm_b[0].ap, nattn_d, writes=[nrm_b[0]])
        dma(nrm_b[1].ap, nffn_d, writes=[nrm_b[1]])
        act(sig, condT, AF.Exp, scale=-1.0)
        ts(sig, sig, 1.0, None, ALU.add)
        recip(scond, sig)
        tt(scond, scond, condT, ALU.mult)
        ones_f = E.f(128)
        mset(ones_f, 1.0)
        for r in range(2):
            for kc in range(KC):
                dst = lhs_rep[:, (r * KC + kc) * 128:(r * KC + kc + 1) * 128]
                ts(dst, ones_f, scond[:, kc * 2 + r: kc * 2 + r + 1], None, ALU.mult)

        def load_wada(kind):
            buf = wada_blk[kind % 2]
            dma(buf.v3(KC), wada_d[:, kind * 1024:(kind + 1) * 1024].rearrange("(k p) n -> p k n", p=128),
                writes=[buf], q="pool")
            bb = bada_blk[kind % 2]
            dma(bb.ap, bada_d[:, kind * 1024:(kind + 1) * 1024], writes=[bb], q="pool")

        load_wada(0)
        pending_w = []

        def queue_weight_loads():
            for (d0, d1) in ((0, _WIN_SPLIT), (_WIN_SPLIT, 2272)):
                dma(w_in.v3(KC)[:, :, d0:d1], win_d[:, d0:d1].rearrange("(k p) n -> p k n", p=128),
                    writes=[w_in[:, kc * 2272 + d0: kc * 2272 + d1] for kc in range(KC)], q="pool")
            dma(w_uq.v3(3), wuq_d.rearrange("(k p) n -> p k n", p=128), writes=[w_uq], q="pool")
            dma(w_uqp.v3(3), wuqp_d.rearrange("(k p) n -> p k n", p=128), writes=[w_uqp], q="pool")
            dma(w_own.ap, wown_d, writes=[w_own], q="pool")
            dma(w_step.ap, wstep_d, writes=[w_step], q="pool")

        mod_kind = 0
        for blk in range(12):
            kind = blk // 2
            half = blk % 2
            if blk == 2:
                queue_weight_loads()
            if half == 0 and kind + 1 < 6:
                load_wada(kind + 1)
            buf = wada_blk[kind % 2]
            bb = bada_blk[kind % 2]
            for r in range(2):
                pt = PF(r * 2 + half % 2)
                for kc in range(KC):
                    mm(pt, lhs_rep[:, (r * KC + kc) * 128:(r * KC + kc + 1) * 128],
                       buf[:, kc * 1024 + half * 512: kc * 1024 + (half + 1) * 512], kc == 0, False)
                mm(pt, ones_bf[0:1, 0:128], bb[:, half * 512:(half + 1) * 512], False, True)
                cs = slice(half * 512, (half + 1) * 512)
                if kind < 2:
                    dst = mods[r * 2 + kind][:, cs]
                else:
                    dst = mstage[r][:, cs]
                if kind in (1, 4):
                    nb = nrm_b[0 if kind == 1 else 1][:, cs]
                    stt(dst, pt, 1.0, nb, ALU.add, ALU.mult)
                else:
                    cp(dst, pt, eng="act")
                if kind >= 2 and half == 1:
                    mi = 6 + r if kind == 2 else (kind - 3) * 2 + r
                    dma(mod2_d[mi], mstage[r].ap, reads=[mstage[r]],
                        writes=[("mod2", 0, 1, mi * 10, mi * 10 + 10)])

        Db = Bump(D_BASE, E_BASE)
        ckvT_s = Db.b(2 * 2304)
        krt_s = Db.b(2304, p=96)
        qT_s = Db.b(NH * 512, p=96)
        ogT_s = Db.b(GH * 512)
        ckvT_p = Db.b(2 * 512)
        krt_p = Db.b(512, p=96)
        qT_p = Db.b(NH * 512, p=96)
        ogT_p = Db.b(GH * 512)

        E = Bump(E_BASE, SB_BYTES)
        xst = [E.f(1024) for _ in range(2)]
        junk = E.f(1024)
        junkA = E.b(1024)
        hbf = [E.b(1024) for _ in range(2)]
        hT = E.b(KC * 512)
        kvst = [E.b(384) for _ in range(2)]
        kvf = E.f(320)
        ckvf = [E.f(256) for _ in range(2)]
        krf = [E.f(32) for _ in range(2)]
        ropet = E.f(64)
        cosk_t = [E.f(32) for _ in range(2)]
        sink_t = [E.f(32) for _ in range(2)]
        U1 = E.f(2048)
        cosq_t = T("sb", sbF, 4, 0, 96, U1.c0, U1.c0 + 512)
        sinq_t = T("sb", sbF, 4, 0, 96, U1.c0 + 512, U1.c0 + 1024)
        ropeq1 = T("sb", sbF, 4, 0, 96, U1.c0 + 1024, U1.c0 + 1536)
        ropeq2 = T("sb", sbF, 4, 0, 96, U1.c0 + 1536, U1.c0 + 2048)
        U2 = E.f(1536)
        _o2 = U2.c0 * 4
        qlat = SF(_o2, 384)
        qnb = SBf(_o2 + 1536, 384)
        qnT = SBf(_o2 + 2304, 3 * 512)
        Gg = SF(_o2, 512)
        osum = SF(_o2 + 2048, 512)
        ogb = SBf(_o2 + 4096, 512)
        lowsT = E.b(512, p=33)
        Lg = E.f(512)
        ktok = E.f(256)
        vbf = [E.b(512) for _ in range(4)]
        kdf = [E.b(256) for _ in range(4)]
        kdb = [E.b(256) for _ in range(4)]
        qTf = [E.f(512) for _ in range(2)]
        kTf = [E.f(512) for _ in range(2)]
        Ef = E.f(256)
        eg = E.f(512)
        Einv = eg[:, 0:256]
        expD = eg[:, 256:512]
        qeT = {d: [E.b(512) for _ in range(2)] for d in "fb"}
        keT = {d: [E.b(512) for _ in range(2)] for d in "fb"}
        dec = {d: E.f(8) for d in "fb"}
        attm = [E.b(128) for _ in range(4)]
        assert attm[1].c0 == attm[0].c1 and attm[3].c0 == attm[2].c1 and maskb.c0 == maskf.c1
        attm2 = [SBf(attm[0].c0 * 2, 256), SBf(attm[2].c0 * 2, 256)]
        mask2 = SBf(maskf.c0 * 2, 256)
        o_f = [U1[:, 512 * i:512 * (i + 1)] for i in range(4)]
        Sst = {d: [[E.f(128) for _ in range(2)] for _ in range(2)] for d in "fb"}
        Sbf = {d: [E.b(128) for _ in range(2)] for d in "fb"}
        _u = [U1[:, 128 * i:128 * (i + 1)] for i in range(14)]
        Rst = [[_u[0], _u[1]], [_u[2], _u[3]]]
        Rin = [_u[4], _u[5]]
        Sbacc = [[_u[6], _u[7]], [_u[8], _u[9]]]
        initb = [[_u[10], _u[11]], [_u[12], _u[13]]]
        decp = E.f(2)
        ssq = E.f(4)

        def norm1_tile(x_src_ap, gi, tix, modr, xbuf):
            xt = xst[xbuf]
            dma(xt.ap, x_src_ap, writes=[xt])
            ss, t1, t2, rs = tslot()
            act(junkA, xt, AF.Square, accum=ss)
            rstd_from_ss(rs, ss, t1, t2, D)
            hb = hbf[xbuf]
            stt(junk, xt, rs, mods[modr * 2 + GM1], ALU.mult, ALU.mult)
            tt(hb, junk, mods[modr * 2 + SH1], ALU.add)
            pt = PB(4)
            for kc in range(KC):
                tr(pt[:, kc * 128:(kc + 1) * 128], hb[:, kc * 128:(kc + 1) * 128], ident)
            cp(hT, pt, eng="act",
               oap=hT.v3(KC)[:, :, tix * 128:(tix + 1) * 128], iap=pt.v3(KC))

        def kv_tile(tix, rope_row0, ckvT, krt, key0, prompt_out_row=None, kbuf=0):
            pk = PF(5, 320)
            for kc in range(KC):
                mm(pk[:, 0:288], hT[:, kc * 512 + tix * 128: kc * 512 + (tix + 1) * 128],
                   w_in_s(kc, 384, 672), kc == 0, kc == KC - 1)
            if rope_row0 is not None:
                for kc in range(KC):
                    mm(pk[:, 288:320], hT[:, kc * 512 + tix * 128: kc * 512 + (tix + 1) * 128],
                       w_in_s(kc, 2240, 2272), kc == 0, kc == KC - 1)
            ncv = 320 if rope_row0 is not None else 288
            cp(kvf[:, 0:ncv], pk[:, 0:ncv], eng="act")
            ss, t1, t2, rs = tslot()
            act(junkA[:, 0:256], kvf[:, 0:256], AF.Square, accum=ss)
            rstd_from_ss(rs, ss, t1, t2, 256)
            st = kvst[kbuf]
            if prompt_out_row is not None:
                cf = ckvf[kbuf]
                stt(cf, kvf[:, 0:256], rs, kvn_b, ALU.mult, ALU.mult)
                cp(st[:, 0:256], cf, eng="pool")
                dma(nkv_d[prompt_out_row:prompt_out_row + 128, :], cf.ap, reads=[cf])
                kf = krf[kbuf]
                cp(kf, kvf[:, 256:288], eng="pool")
                cp(st[:, 320:352], kvf[:, 256:288], eng="pool")
                dma(nkr_d[prompt_out_row:prompt_out_row + 128, :], kf.ap, reads=[kf])
            else:
                stt(st[:, 0:256], kvf[:, 0:256], rs, kvn_b, ALU.mult, ALU.mult)
                ck, sk = cosk_t[kbuf], sink_t[kbuf]
                dma(ck.ap, cosk_d[rope_row0:rope_row0 + 128, :], writes=[ck])
                dma(sk.ap, sink_d[rope_row0:rope_row0 + 128, :], writes=[sk])
                tt(ropet[:, 0:32], kvf[:, 256:288], ck, ALU.mult)
                tt(ropet[:, 32:64], kvf[:, 288:320], sk, ALU.mult)
                tt(st[:, 320:352], ropet[:, 0:32], ropet[:, 32:64], ALU.add)
            kv_transposes(st, ckvT, krt, key0)

        def kv_transposes(st, ckvT, krt, key0):
            nk = (ckvT.c1 - ckvT.c0) // 2
            pt = PB(4, 384)
            tr(pt[:, 0:128], st[:, 0:128], ident)
            tr(pt[:, 128:256], st[:, 128:256], ident)
            tr(pt[0:96, 256:384], st[:, 256:352], ident)
            cp(ckvT[:, key0:key0 + 128], pt[:, 0:128], eng="act")
            cp(ckvT[:, nk + key0:nk + key0 + 128], pt[:, 128:256], eng="act")
            cp(krt[64:96, key0:key0 + 128], pt[64:96, 256:384], eng="dve")

        def gla_common_tile(tix, wg, ncol, slot):
            pa = PF(0, 512)
            pb = PF(1, 256)
            for kc in range(KC):
                lhs = hT[:, kc * 512 + tix * 128: kc * 512 + (tix + 1) * 128]
                mm(pa, lhs, w_in_s(kc, 1184, 1696), kc == 0, kc == KC - 1)
            for kc in range(KC):
                lhs = hT[:, kc * 512 + tix * 128: kc * 512 + (tix + 1) * 128]
                mm(pb, lhs, w_in_s(kc, 928, 1184), kc == 0, kc == KC - 1)
            cp(vbf[slot], pa, eng="act")
            cp(ktok, pb, eng="dve")
            pg = PF(5, ncol)
            mm(pg, lowsT[:, tix * 128:(tix + 1) * 128], wg, True, True)
            act(eg[:, 0:ncol], pg, AF.Exp, scale=-1.0)
            act(Lg[:, 0:ncol], eg[:, 0:ncol], AF.Ln, bias=1.0)

        def lows_group():
            pl = PF(6, 512)
            for kc in range(KC):
                mm(pl[0:32, :], w_in_s(kc, 1696, 1728), hT[:, kc * 512:(kc + 1) * 512], kc == 0, kc == KC - 1)
            cp(lowsT[0:32, :], pl[0:32, :], eng="dve")

        mset(lowsT[32:33, :], 1.0)
        for _k in range(2):
            mset(kvst[_k], 0.0)

        phase(1)
        for pr in range(2):
            dma(Rst[0][pr].ap, r0_d[pr], writes=[Rst[0][pr]])
            dma(Sbacc[0][pr].ap, sb0_d[pr], writes=[Sbacc[0][pr]])
        for t in range(2):
            cst = ckvf[t]
            dma(cst.ap, ckvctx_d[t * 128:(t + 1) * 128, :], writes=[cst])
            kf = krf[t]
            dma(kf.ap, krctx_d[t * 128:(t + 1) * 128, :], writes=[kf])
            st = kvst[t]
            cp(st[:, 0:256], cst, eng="pool")
            cp(st[:, 320:352], kf, eng="pool")
            kv_transposes(st, ckvT_s, krt_s, t * 128)

        rp = 0
        for g in range(3):
            for tix in range(4):
                j = g * 4 + tix
                norm1_tile(xpre_d[j * 128:(j + 1) * 128, :], g, tix, 1, j % 2)
            lows_group()
            for tix in range(4):
                j = g * 4 + tix
                kv_tile(tix, j * 128, ckvT_s, krt_s, 256 + j * 128, kbuf=j % 2)
                for pr in range(2):
                    dma(initb[j % 2][pr].ap, init_d[j, pr], writes=[initb[j % 2][pr]])
                gla_common_tile(tix, w_step[:, j * 256:(j + 1) * 256], 256, 0)
                pd = PF(6, 256)
                P.op("pe", lambda e, pd=pd: e.matmul(out=pd.ap, lhsT=tris_f.ap, rhs=Lg[:, 0:256].ap,
                                                     start=True, stop=True),
                     reads=[tris_f, Lg[:, 0:256]], writes=[pd])
                pdec = PF(7, 2)
                for pr in range(2):
                    P.op("pe", lambda e, pr=pr, pdec=pdec: e.matmul(
                        out=pdec[:, pr:pr + 1].ap, lhsT=Lg[:, pr * 128:(pr + 1) * 128].ap,
                        rhs=tri_f[:, 127:128].ap, start=True, stop=True),
                        reads=[Lg[:, pr * 128:(pr + 1) * 128], tri_f], writes=[pdec[:, pr:pr + 1]])
                act(expD, pd, AF.Exp)
                act(decp, pdec, AF.Exp)
                tt(kdf[0], ktok, expD, ALU.mult)
                for pr in range(2):
                    pu = PF(2 + pr, 256)
                    mm(pu, kdf[0][:, pr * 128:(pr + 1) * 128], vbf[0][:, pr * 256:(pr + 1) * 256], True, True)
                    Rold = Rst[rp][pr]
                    Rnew = Rst[1 - rp][pr]
                    stt(Rin[pr], Rold, keepcap[:, j:j + 1], initb[j % 2][pr], ALU.mult, ALU.add)
                    stt(Sbacc[1 - rp][pr], Rold, keepcap[:, 13 + j:14 + j], Sbacc[rp][pr], ALU.mult, ALU.add)
                    for hh in range(2):
                        rows = slice(hh * 64, (hh + 1) * 64)
                        stt(Rnew[rows, :], Rin[pr][rows, :], decp[rows, pr:pr + 1],
                            pu[rows, hh * 128:(hh + 1) * 128], ALU.mult, ALU.add)
                rp = 1 - rp
        for pr in range(2):
            dma(initb[0][pr].ap, init_d[12, pr], writes=[initb[0][pr]])
            stt(Sst["f"][0][pr], Rst[rp][pr], keepcap[:, 12:13], initb[0][pr], ALU.mult, ALU.add)
            stt(Sst["b"][0][pr], Rst[rp][pr], keepcap[:, 25:26], Sbacc[rp][pr], ALU.mult, ALU.add)

        def own_group(x_d, modr, is_prompt, ckvT, krt, key0s, qT, ogT, seqs):
            for tix in range(4):
                norm1_tile(x_d[tix * 128:(tix + 1) * 128, :], 0, tix, modr, tix % 2)
            lows_group()
            for tix in range(4):
                if is_prompt:
                    kv_tile(tix, None, ckvT, krt, key0s[tix], prompt_out_row=tix * 128, kbuf=tix % 2)
                else:
                    kv_tile(tix, 1536 + tix * 128, ckvT, krt, key0s[tix], kbuf=tix % 2)
            for tix in range(4):
                pq = PF(0, 384)
                for kc in range(KC):
                    mm(pq, hT[:, kc * 512 + tix * 128: kc * 512 + (tix + 1) * 128],
                       w_in_s(kc, 0, 384), kc == 0, kc == KC - 1)
                cp(qlat, pq, eng="act")
                ss, t1, t2, rs = tslot()
                act(junkA[:, 0:384], qlat, AF.Square, accum=ss)
                rstd_from_ss(rs, ss, t1, t2, 384)
                stt(qnb, qlat, rs, qn_b, ALU.mult, ALU.mult)
                pt = PB(4, 384)
                for c3 in range(3):
                    tr(pt[:, c3 * 128:(c3 + 1) * 128], qnb[:, c3 * 128:(c3 + 1) * 128], ident)
                cp(qnT, pt, eng="act", oap=qnT.v3(3)[:, :, tix * 128:(tix + 1) * 128], iap=pt.v3(3))
            if not is_prompt:
                dma(cosq_t[64:96, :].ap, cosq_d, writes=[cosq_t[64:96, :]])
                dma(sinq_t[64:96, :].ap, sinq_d, writes=[sinq_t[64:96, :]])
            for h in range(NH):
                pa = PF(2 + (h % 2) * 2, 512)
                for c3 in range(3):
                    mm(pa[0:96, :], w_uq[:, c3 * 768 + h * 96: c3 * 768 + (h + 1) * 96],
                       qnT[:, c3 * 512:(c3 + 1) * 512], c3 == 0, c3 == 2)
                dst = qT[:, h * 512:(h + 1) * 512]
                if is_prompt:
                    cp(dst[0:96, :], pa[0:96, :], eng="act")
                else:
                    pb = PF(3 + (h % 2) * 2, 512)
                    for c3 in range(3):
                        mm(pb[0:96, :], w_uqp[:, c3 * 768 + h * 96: c3 * 768 + (h + 1) * 96],
                           qnT[:, c3 * 512:(c3 + 1) * 512], c3 == 0, c3 == 2)
                    cp(dst[0:64, :], pa[0:64, :], eng="act")
                    tt(ropeq1[64:96, :], pa[64:96, :], cosq_t[64:96, :], ALU.mult)
                    tt(ropeq2[64:96, :], pb[64:96, :], sinq_t[64:96, :], ALU.mult)
                    tt(dst[64:96, :], ropeq1[64:96, :], ropeq2[64:96, :], ALU.add)
            for pr in range(2):
                pq = PF(2 + pr, 512)
                for kc in range(KC):
                    mm(pq, w_in_s(kc, 672 + pr * 128, 672 + (pr + 1) * 128), hT[:, kc * 512:(kc + 1) * 512],
                       kc == 0, kc == KC - 1)
                act(qTf[pr], pq, AF.Copy, scale=0.125)
                pk = PF(6 + pr, 512)
                for kc in range(KC):
                    mm(pk, w_in_s(kc, 928 + pr * 128, 928 + (pr + 1) * 128), hT[:, kc * 512:(kc + 1) * 512],
                       kc == 0, kc == KC - 1)
                cp(kTf[pr], pk, eng="dve")
            for tix in range(4):
                gla_common_tile(tix, w_own, 512, tix)
                tcs = slice(tix * 128, (tix + 1) * 128)
                for di, d in enumerate("fb"):
                    Lc = Lg[:, di * 256:(di + 1) * 256]
                    trim = tri_f if d == "f" else tri_b
                    tris = tris_f if d == "f" else tris_b
                    pc = PF(6, 256)
                    for pr in range(2):
                        P.op("pe", lambda e, pr=pr, pc=pc, Lc=Lc, trim=trim: e.matmul(
                            out=pc[:, pr * 128:(pr + 1) * 128].ap, lhsT=Lc[:, pr * 128:(pr + 1) * 128].ap,
                            rhs=trim.ap, start=True, stop=True),
                            reads=[Lc[:, pr * 128:(pr + 1) * 128], trim], writes=[pc[:, pr * 128:(pr + 1) * 128]])
                    pd = PF(7, 256)
                    P.op("pe", lambda e, pd=pd, Lc=Lc, tris=tris: e.matmul(
                        out=pd.ap, lhsT=tris.ap, rhs=Lc.ap, start=True, stop=True),
                        reads=[tris, Lc], writes=[pd])
                    act(Ef, pc, AF.Exp)
                    act(Einv, pc, AF.Exp, scale=-1.0)
                    act(expD, pd, AF.Exp)
                    for pr in range(2):
                        ecol = 127 if d == "f" else 0
                        cp(dec[d][:, pr * 4 + tix: pr * 4 + tix + 1],
                           Ef[:, pr * 128 + ecol: pr * 128 + ecol + 1], eng="pool")
                        tt(qeT[d][pr][:, tcs], qTf[pr][:, tcs], Ef[:, pr * 128:(pr + 1) * 128], ALU.mult)
                        tt(keT[d][pr][:, tcs], kTf[pr][:, tcs], Einv[:, pr * 128:(pr + 1) * 128], ALU.mult)
                    tt((kdf if d == "f" else kdb)[tix], ktok, expD, ALU.mult)
            for seq in seqs:
                cur = {"f": 0, "b": 0}
                if is_prompt:
                    for d in "fb":
                        for pr in range(2):
                            mset(Sst[d][0][pr], 0.0)
                for d in "fb":
                    for pr in range(2):
                        cp(Sbf[d][pr], Sst[d][0][pr], eng="pool")

                def state_update(d, tix):
                    kd = (kdf if d == "f" else kdb)[tix]
                    c = cur[d]
                    for pr in range(2):
                        pu = PF(2 + pr, 256)
                        mm(pu, kd[:, pr * 128:(pr + 1) * 128], vbf[tix][:, pr * 256:(pr + 1) * 256], True, True)
                        for hh in range(2):
                            rows = slice(hh * 64, (hh + 1) * 64)
                            stt(Sst[d][1 - c][pr][rows, :], Sst[d][c][pr][rows, :],
                                dec[d][rows, pr * 4 + tix: pr * 4 + tix + 1],
                                pu[rows, hh * 128:(hh + 1) * 128], ALU.mult, ALU.add)
                        cp(Sbf[d][pr], Sst[d][1 - c][pr], eng="act")
                    cur[d] = 1 - c

                for tix in seq:
                    tcs = slice(tix * 128, (tix + 1) * 128)
                    for h in range(GH):
                        pr, hh = h // 2, h % 2
                        rows = slice(hh * 64, (hh + 1) * 64)
                        for di, d in enumerate("fb"):
                            pat = PF(6 + hh, 128, c0=di * 128)
                            mm(pat, keT[d][pr][rows, tcs], qeT[d][pr][rows, tcs], True, True)
                        tt(attm2[hh], PF(6 + hh, 256), mask2, ALU.mult)
                        po = PF(2 * hh, 512)[:, h * 128:(h + 1) * 128]
                        mm(po, qeT["f"][pr][rows, tcs], Sbf["f"][pr][rows, :], True, False)
                        mm(po, attm2[hh][:, 0:128], vbf[tix][:, h * 128:(h + 1) * 128], False, False)
                        mm(po, attm2[hh][:, 128:256], vbf[tix][:, h * 128:(h + 1) * 128], False, True)
                    for hh in range(2):
                        cs_ = slice(hh * 128, (hh + 1) * 128)
                        pof = PF(2 * hh, 512)
                        cp(o_f[tix], pof, eng="act", oap=o_f[tix].v3(2)[:, :, cs_], iap=pof.v3(2)[:, :, cs_])
                    state_update("f", tix)
                for tix in reversed(seq):
                    tcs = slice(tix * 128, (tix + 1) * 128)
                    pob = [PF(1, 512), PF(3, 512)]
                    for h in range(GH):
                        pr, hh = h // 2, h % 2
                        rows = slice(hh * 64, (hh + 1) * 64)
                        mm(pob[hh][:, h * 128:(h + 1) * 128], qeT["b"][pr][rows, tcs], Sbf["b"][pr][rows, :],
                           True, True)
                    for hh in range(2):
                        cs_ = slice(hh * 128, (hh + 1) * 128)
                        tt(osum, pob[hh], o_f[tix], ALU.add,
                           oap=osum.v3(2)[:, :, cs_], aap=pob[hh].v3(2)[:, :, cs_], bap=o_f[tix].v3(2)[:, :, cs_])
                    state_update("b", tix)
                    pg = PF(5, 512)
                    for kc in range(KC):
                        mm(pg, hT[:, kc * 512 + tix * 128: kc * 512 + (tix + 1) * 128],
                           w_in_s(kc, 1728, 2240), kc == 0, kc == KC - 1)
                    act(eg, pg, AF.Exp, scale=-1.0)
                    act(Lg, eg, AF.Ln, bias=1.0)
                    act(eg, Lg, AF.Exp, scale=-1.0)
                    tt(Gg, pg, gla4_b, ALU.mult)
                    tt(Gg, Gg, eg, ALU.mult)
                    for h in range(GH):
                        act(junkA[:, h * 128:(h + 1) * 128], osum[:, h * 128:(h + 1) * 128], AF.Square,
                            accum=ssq[:, h:h + 1])
                    rstd_from_ss(tiny[:, 16:20], ssq, tiny[:, 20:24], tiny[:, 24:28], 128)
                    for h in range(GH):
                        stt(ogb[:, h * 128:(h + 1) * 128], osum[:, h * 128:(h + 1) * 128], tiny[:, 16 + h:17 + h],
                            Gg[:, h * 128:(h + 1) * 128], ALU.mult, ALU.mult)
                    pt = PB(4, 512)
                    for h in range(GH):
                        tr(pt[:, h * 128:(h + 1) * 128], ogb[:, h * 128:(h + 1) * 128], ident)
                    cp(ogT, pt, eng="act", oap=ogT.v3(GH)[:, :, tcs], iap=pt.v3(GH))
                if is_prompt:
                    si = seqs.index(seq)
                    for pr in range(2):
                        dma(nsf_d[si, pr], Sst["f"][cur["f"]][pr].ap, reads=[Sst["f"][cur["f"]][pr]])
                        dma(nsb_d[si, pr], Sst["b"][cur["b"]][pr].ap, reads=[Sst["b"][cur["b"]][pr]])
                else:
                    pass

        phase(2)
        own_group(xown_d, 1, False, ckvT_s, krt_s, [256 + 1536 + t * 128 for t in range(4)], qT_s, ogT_s,
                  [[0, 1, 2, 3]])
        phase(3)
        own_group(xp_d, 0, True, ckvT_p, krt_p, [0, 128, 256, 384], qT_p, ogT_p, [[0, 1], [2, 3]])

        phase(4)
        Cb = Bump(C_BASE, D_BASE)
        w_ukv = Cb.b(2 * 1024)
        w_outA = Cb.b(NH * 1024, p=64)
        w_outB = Cb.b(GH * 1024)
        kTh = [Cb.b(2304, p=96) for _ in range(2)]
        dma(w_ukv.v3(2), wukv_d.rearrange("(k p) n -> p k n", p=128), writes=[w_ukv], q="pool")
        dma(w_outA.v3(NH), wout_d[0:512, :].rearrange("(h d) n -> d h n", d=64), writes=[w_outA], q="pool")
        dma(w_outB.v3(GH), wout_d[512:1024, :].rearrange("(h e) n -> e h n", e=128), writes=[w_outB], q="pool")

        E = Bump(E_BASE, SB_BYTES)
        x1 = [E.f(1024) for _ in range(8)]
        Vp = E.b(18 * NH * 65)
        attnT = E.b(NH * 512, p=65)
        PT = [E.b(512) for _ in range(4)]
        rden = E.f(512, p=65)
        bcs = E.f(512, p=64)
        mixh = E.f(512)
        gt1s = E.f(1024)
        assert E.cur <= SB_BYTES

        def attention(ckvT, krt, nkeys_list, qT, q_groups):
            nk = (ckvT.c1 - ckvT.c0) // 2
            ntile = nk // 128
            mset(Vp, 1.0, oap=Vp.ap)
            for kt in range(ntile):
                pv = PF(kt % 2, 512)
                for c2 in range(2):
                    mm(pv, ckvT[:, c2 * nk + kt * 128: c2 * nk + (kt + 1) * 128],
                       w_ukv[:, c2 * 1024 + 512: c2 * 1024 + 1024], c2 == 0, c2 == 1)
                dstap = Vp[:, kt * NH * 65:(kt + 1) * NH * 65].v3(NH)[:, :, 0:64]
                cp(Vp[:, kt * NH * 65:(kt + 1) * NH * 65], pv, eng="act" if kt % 2 else "dve",
                   oap=dstap, iap=pv.v3(NH))
            def build(h):
                kt_h = kTh[h % 2]
                for kb in range(0, nk, 512):
                    n = min(512, nk - kb)
                    pk = PF([3, 1][(kb // 512) % 2], n)
                    for c2 in range(2):
                        mm(pk[0:64, :], w_ukv[:, c2 * 1024 + h * 64: c2 * 1024 + (h + 1) * 64],
                           ckvT[:, c2 * nk + kb: c2 * nk + kb + n], c2 == 0, c2 == 1)
                    cp(kt_h[0:64, kb:kb + n], pk[0:64, :], eng="act" if (kb // 512) % 2 else "dve")
                cp(kt_h[64:96, 0:nk], krt[64:96, 0:nk], eng="dve")

            pending = []

            def finish(h, q0, nq, pacc):
                if nq == 512:
                    recip(rden[64:65, 0:nq], pacc[64:65, :])
                else:
                    act(rden[64:65, 256:256 + nq], pacc[64:65, :], AF.Ln)
                    act(rden[64:65, 0:nq], rden[64:65, 256:256 + nq], AF.Exp, scale=-1.0)
                pbc = PF(0, nq)
                mm(pbc[0:64, :], ones_fA[64:65, 0:64], rden[64:65, 0:nq], True, True)
                cp(bcs[:, 0:nq], pbc[0:64, :], eng="act")
                tt(attnT[0:64, h * 512 + q0: h * 512 + q0 + nq], pacc[0:64, :], bcs[:, 0:nq], ALU.mult)

            SCB = [4, 5, 2]
            build(0)
            gi = 0
            for h in range(NH):
                kt_h = kTh[h % 2]
                if h + 1 < NH:
                    build(h + 1)
                for (q0, nq, ktiles) in q_groups:
                    pacc = PF(6 + gi % 2, nq)
                    gi += 1
                    n = len(ktiles)
                    pscs = {}

                    def score(i, kt_h=kt_h, h=h, q0=q0, nq=nq, ktiles=ktiles, pscs=pscs):
                        psc = PF(SCB[i % 3], nq)
                        pscs[i] = psc
                        kt = ktiles[i]
                        mm(psc, kt_h[0:96, kt * 128:(kt + 1) * 128], qT[0:96, h * 512 + q0: h * 512 + q0 + nq],
                           True, True)

                    for i in range(min(2, n)):
                        score(i)
                    while pending:
                        finish(*pending.pop(0))
                    for i, kt in enumerate(ktiles):
                        pt_ = PT[i % 4][:, 0:nq]
                        act(pt_, pscs[i], AF.Exp, scale=SCALE)
                        if i + 2 < n:
                            score(i + 2)
                        mm(pacc[0:65, :], Vp[:, (kt * NH + h) * 65:(kt * NH + h + 1) * 65], pt_,
                           i == 0, i == n - 1)
                    pending.append((h, q0, nq, pacc))
            while pending:
                finish(*pending.pop(0))

        def out_proj(x_d, modr, ogT, x1s):
            dma(gt1s.ap, mod2_d[6 + modr], reads=[("mod2", 0, 1, (6 + modr) * 10, (6 + modr) * 10 + 10)],
                writes=[gt1s])
            for tix in range(4):
                xt = x1s[tix]
                dma(xt.ap, x_d[tix * 128:(tix + 1) * 128, :], writes=[xt])
                for half in range(2):
                    pm = PF(2 + half, 512)
                    cs = slice(half * 512, (half + 1) * 512)
                    for h in range(NH):
                        mm(pm, attnT[0:64, h * 512 + tix * 128: h * 512 + (tix + 1) * 128],
                           w_outA[0:64, h * 1024 + half * 512: h * 1024 + (half + 1) * 512], h == 0, False)
                    for h in range(GH):
                        mm(pm, ogT[:, h * 512 + tix * 128: h * 512 + (tix + 1) * 128],
                           w_outB[:, h * 1024 + half * 512: h * 1024 + (half + 1) * 512], False, h == GH - 1)
                    tt(mixh, pm, gt1s[:, cs], ALU.mult)
                    tt(xt[:, cs], xt[:, cs], mixh, ALU.add)

        wblk = [SBf(70 * K1, KC * 512), SBf(196 * K1, KC * 512)]
        NWB = NFF // 2

        def load_wblk(wb):
            buf = wblk[wb % 2]
            v = buf.v3(KC)
            dma(v[:, :, 0:256], wffi_d[:, wb * 256:(wb + 1) * 256].rearrange("(k p) n -> p k n", p=128),
                writes=[buf], q="pool")
            dma(v[:, :, 256:512],
                wffi_d[:, DFF + wb * 256: DFF + (wb + 1) * 256].rearrange("(k p) n -> p k n", p=128),
                writes=[buf], q="pool")

        load_wblk(0)
        load_wblk(1)
        attention(ckvT_s, krt_s, None, qT_s, [(0, 512, list(range(18)))])
        out_proj(xown_d, 1, ogT_s, x1[0:4])
        phase(5)
        attention(ckvT_p, krt_p, None, qT_p, [(0, 256, [0, 1]), (256, 256, [2, 3])])
        out_proj(xp_d, 0, ogT_p, x1[4:8])

        phase(6)
        w_ffo = SBf(C_BASE, NFF * 1024)
        h2T = SBf(D_BASE, KC * 1024)
        uT = [SBf(94 * K1, NFF * 512), SBf(151 * K1, NFF * 512)]
        Ff = Bump(174 * K1, SB_BYTES)
        silu_t = [Ff.f(512) for _ in range(2)]
        h2b = [Ff.b(1024) for _ in range(2)]
        junk2 = Ff.f(1024)
        junkA2 = Ff.b(1024)
        mods2 = [SF(10 * K1 + 4096 * i, 1024) for i in range(4)] + [Ff.f(1024) for _ in range(2)]
        for i in range(6):
            dma(mods2[i].ap, mod2_d[i], reads=[("mod2", 0, 1, i * 10, i * 10 + 10)], writes=[mods2[i]])

        for t8 in range(8):
            r = 1 if t8 < 4 else 0
            xt = x1[t8]
            ss, t1, t2, rs = tslot()
            act(junkA2, xt, AF.Square, accum=ss)
            rstd_from_ss(rs, ss, t1, t2, D)
            hb = h2b[t8 % 2]
            stt(junk2, xt, rs, mods2[1 * 2 + r], ALU.mult, ALU.mult)
            tt(hb, junk2, mods2[0 * 2 + r], ALU.add)
            pt = PB(4 + t8 % 2)
            for kc in range(KC):
                tr(pt[:, kc * 128:(kc + 1) * 128], hb[:, kc * 128:(kc + 1) * 128], ident)
            cp(h2T, pt, eng="act", oap=h2T.v3(KC)[:, :, t8 * 128:(t8 + 1) * 128], iap=pt.v3(KC))
        for wb in range(NWB):
            buf = wblk[wb % 2]
            for sub in range(2):
                fb = 2 * wb + sub
                for g in range(2):
                    pa = PF(0 + g * 2, 512)
                    pg = PF(1 + g * 2, 512)
                    for kc in range(KC):
                        mm(pa, buf[:, kc * 512 + sub * 128: kc * 512 + sub * 128 + 128],
                           h2T[:, kc * 1024 + g * 512: kc * 1024 + (g + 1) * 512], kc == 0, kc == KC - 1)
                    for kc in range(KC):
                        mm(pg, buf[:, kc * 512 + 256 + sub * 128: kc * 512 + 256 + sub * 128 + 128],
                           h2T[:, kc * 1024 + g * 512: kc * 1024 + (g + 1) * 512], kc == 0, kc == KC - 1)
                    st_ = silu_t[g]
                    act(st_, pa, AF.Silu)
                    tt(uT[g][:, fb * 512:(fb + 1) * 512], st_, pg, ALU.mult)
            if wb + 2 < NWB:
                load_wblk(wb + 2)
            if wb < 3:
                k0, k1 = [(0, 8), (8, 16), (16, NFF)][wb]
                dma(w_ffo.v3(NFF)[:, k0:k1, :], wffo_d[k0 * 128:k1 * 128, :].rearrange("(k p) n -> p k n", p=128),
                    writes=[w_ffo[:, k0 * 1024:k1 * 1024]], q="pool")
        fin_b = SF(D_BASE, 1024)
        dma(fin_b.ap, fin_d, reads=[h2T], writes=[fin_b])
        yst = [SF(D_BASE + 4096 * (1 + i), 1024) for i in range(2)]
        for t8 in range(8):
            r = 1 if t8 < 4 else 0
            g, tg = t8 // 4, t8 % 4
            xt = x1[t8]
            for half in range(2):
                pm = PF(4 + half + 2 * (t8 % 2), 512)
                cs = slice(half * 512, (half + 1) * 512)
                for fc in range(NFF):
                    mm(pm, uT[g][:, fc * 512 + tg * 128: fc * 512 + (tg + 1) * 128],
                       w_ffo[:, fc * 1024 + half * 512: fc * 1024 + (half + 1) * 512], fc == 0, fc == NFF - 1)
                tt(junk2[:, cs], pm, mods2[2 * 2 + r][:, cs], ALU.mult)
            tt(xt, xt, junk2, ALU.add)
            ss, t1, t2, rs = tslot()
            act(junkA2, xt, AF.Square, accum=ss)
            rstd_from_ss(rs, ss, t1, t2, D)
            yo = yst[t8 % 2]
            stt(yo, xt, rs, fin_b, ALU.mult, ALU.mult)
            if t8 < 4:
                dma(ys_d[t8 * 128:(t8 + 1) * 128, :], yo.ap, reads=[yo])
            else:
                dma(yp_d[(t8 - 4) * 128:(t8 - 3) * 128, :], yo.ap, reads=[yo])

        P.finalize()
        P.emit(nc, es)
    return nc


_ROPE_P = np.array(list(range(8, 16)) + list(range(0, 8)) + list(range(24, 32)) + list(range(16, 24)))
_ROPE_S = np.array([-1.0] * 8 + [1.0] * 8 + [-1.0] * 8 + [1.0] * 8, dtype=np.float32)


def _rope_tables(n_tokens):
    t = np.arange(n_tokens)
    row = (t // 64).astype(np.float32)
    col = (t % 64).astype(np.float32)
    half = 16
    inv = (np.float32(10000.0) ** (-np.arange(0, half, 2, dtype=np.float32) / np.float32(half))).astype(np.float32)
    ang_r = row[:, None] * inv
    ang_c = col[:, None] * inv
    ang = np.concatenate([ang_r, ang_r, ang_c, ang_c], axis=-1).astype(np.float32)
    return np.cos(ang).astype(np.float32), (np.sin(ang).astype(np.float32) * _ROPE_S[None, :])


def _bc(v, n=128):
    return np.ascontiguousarray(np.broadcast_to(np.asarray(v, np.float32).reshape(1, -1), (n, v.size)))


_WIN_SEGS = ((384, 672, 0), (2240, 2272, 288), (928, 1184, 320), (1184, 1696, 576), (1696, 1728, 1088),
             (0, 384, 1120), (672, 928, 1504), (1728, 2240, 1760))
_WIN_SPLIT = 1120


def _win_col(c0, c1):
    for a, b, d in _WIN_SEGS:
        if a <= c0 and c1 <= b:
            return d + (c0 - a), d + (c1 - a)
    raise ValueError((c0, c1))


_NC_CACHE = {}


def kernel(x_prompt, x_sample, cache_kv_latent, cache_k_rope, state_gla_fwd, state_gla_bwd,
           c, c_ctx, w_ada, b_ada, norm_attn, w_in, mla_q_norm, w_uq, mla_kv_norm, w_ukv,
           w_gate_f, b_gate_f, w_gate_b, b_gate_b, gla_norm, w_out, norm_ffn, w_ffn_in,
           w_ffn_out, final_norm):
    f32 = np.float32
    A = lambda a: np.ascontiguousarray(np.asarray(a, dtype=f32))
    x_prompt, x_sample = A(x_prompt), A(x_sample)
    cos_all, sin_all = _rope_tables(2048)
    w_in0 = A(w_in)[0]
    w_in_nat = np.concatenate([w_in0, w_in0[:, 640:672][:, _ROPE_P]], axis=1)
    w_in_dev = np.zeros_like(w_in_nat)
    for a_, b_, d_ in _WIN_SEGS:
        w_in_dev[:, d_:d_ + (b_ - a_)] = w_in_nat[:, a_:b_]
    w_in_dev = np.ascontiguousarray(w_in_dev)
    w_uq0 = A(w_uq)[0]
    w_uqp = np.zeros((384, 768), f32)
    for h in range(8):
        w_uqp[:, h * 96 + 64:(h + 1) * 96] = w_uq0[:, h * 96 + 64:(h + 1) * 96][:, _ROPE_P]
    w_ukv0 = A(w_ukv)[0].reshape(256, 8, 128)
    w_ukv2 = np.ascontiguousarray(np.concatenate(
        [w_ukv0[:, :, :64].reshape(256, 512), w_ukv0[:, :, 64:].reshape(256, 512)], axis=1))
    wgf, wgb = A(w_gate_f)[0], A(w_gate_b)[0]
    bgf, bgb = A(b_gate_f)[0], A(b_gate_b)[0]
    Wf = np.zeros((33, 256), f32); Wf[0:16] = wgf; Wf[32] = bgf
    Wb = np.zeros((33, 256), f32); Wb[16:32] = wgb; Wb[32] = bgb
    Wown = np.ascontiguousarray(np.concatenate([Wf, Wb], axis=1))
    s, t = np.arange(128)[:, None], np.arange(128)[None, :]
    ng = f32(-1.0 / 16.0)
    tri = np.concatenate([(s <= t) * ng, (s >= t) * ng, (s > t) * ng, (s < t) * ng], axis=1).astype(f32)
    mask = np.concatenate([(s <= t), (s >= t)], axis=1).astype(f32)
    ident = np.eye(128, dtype=f32)
    shared = {
        "w_ada": A(w_ada)[0], "b_ada": A(b_ada)[0].reshape(1, -1),
        "norm_attn_b": _bc(A(norm_attn)[0]), "norm_ffn_b": _bc(A(norm_ffn)[0]),
        "w_in": w_in_dev, "w_uq": w_uq0, "w_uqp": w_uqp, "w_ukv2": w_ukv2, "w_out": A(w_out)[0],
        "w_ffn_in": A(w_ffn_in)[0], "w_ffn_out": A(w_ffn_out)[0],
        "kv_norm_b": _bc(A(mla_kv_norm)[0]), "q_norm_b": _bc(A(mla_q_norm)[0]),
        "gla_norm4_b": _bc(np.tile(A(gla_norm)[0], 4)), "final_norm_b": _bc(A(final_norm)),
        "ident": ident, "tri": np.ascontiguousarray(tri), "mask": np.ascontiguousarray(mask),
        "Wown": Wown,
    }
    sf = A(state_gla_fwd)[:, 0].reshape(2, 2, 128, 128)
    sbw = A(state_gla_bwd)[:, 0].reshape(2, 2, 128, 128)
    in_maps = []
    for core in range(8):
        b, qd = core // 4, core % 4
        nb = 12 - 4 * qd
        bw_tiles = list(range(15, 4 * qd + 3, -1))
        fw_tiles = list(range(0, 4 * qd))
        xs = x_sample[b]
        pre, cosk, sink = [], [], []
        Wstep = np.zeros((33, 12 * 256), f32)
        for j, tl in enumerate(bw_tiles + fw_tiles):
            rows = np.arange(tl * 128, (tl + 1) * 128)
            if j < nb:
                rows = rows[::-1]
            pre.append(xs[rows]); cosk.append(cos_all[rows]); sink.append(sin_all[rows])
            Wstep[:, j * 256:(j + 1) * 256] = Wb if j < nb else Wf
        own_rows = np.arange(4 * qd * 128, (4 * qd + 4) * 128)
        cosk.append(cos_all[own_rows]); sink.append(sin_all[own_rows])
        keep = np.ones(13, f32); cap = np.zeros(13, f32)
        INIT = np.zeros((13, 2, 128, 128), f32)
        keep[0] = 0.0
        INIT[0] = sbw[b] if nb > 0 else sf[b]
        if nb > 0:
            keep[nb] = 0.0; cap[nb] = 1.0; INIT[nb] = sf[b]
        SB0 = sbw[b] if nb == 0 else np.zeros((2, 128, 128), f32)
        keepcap = _bc(np.concatenate([keep, cap]))
        cond2 = np.stack([A(c_ctx), A(c)[b]], axis=0)
        condT = np.ascontiguousarray(cond2.reshape(2, 8, 128).transpose(2, 1, 0).reshape(128, 16))
        m = dict(shared)
        m.update({
            "xp": np.ascontiguousarray(x_prompt[2 * core:2 * core + 2].reshape(512, 1024)),
            "xs_pre": np.ascontiguousarray(np.concatenate(pre, axis=0)),
            "xs_own": np.ascontiguousarray(xs[own_rows]),
            "ckv_ctx": A(cache_kv_latent)[b, 0], "kr_ctx": A(cache_k_rope)[b, 0],
            "cosk": np.ascontiguousarray(np.concatenate(cosk, axis=0)),
            "sink": np.ascontiguousarray(np.concatenate(sink, axis=0)),
            "cosq": np.ascontiguousarray(cos_all[own_rows].T), "sinq": np.ascontiguousarray(sin_all[own_rows].T),
            "R0": np.zeros((2, 128, 128), f32), "SB0": np.ascontiguousarray(SB0),
            "INIT": INIT, "keepcap": keepcap, "Wstep": Wstep, "condT": condT,
        })
        in_maps.append(m)
    if "nc" not in _NC_CACHE:
        _NC_CACHE["nc"] = build_program()
    res = run_bass_kernel_spmd(_NC_CACHE["nc"], in_maps, core_ids=list(range(8)))
    R = res.results
    y_prompt = np.stack([np.asarray(R[cidx]["y_p"]).reshape(2, 256, 1024) for cidx in range(8)]).reshape(16, 256, 1024)
    y_sample = np.stack([np.asarray(R[cidx]["y_s"]) for cidx in range(8)]).reshape(2, 2048, 1024)
    new_kv = np.stack([np.asarray(R[cidx]["new_kv"]).reshape(2, 256, 256) for cidx in range(8)]).reshape(16, 1, 256, 256)
    new_kr = np.stack([np.asarray(R[cidx]["new_kr"]).reshape(2, 256, 32) for cidx in range(8)]).reshape(16, 1, 256, 32)
    new_sf = np.stack([np.asarray(R[cidx]["new_sf"]) for cidx in range(8)]).reshape(16, 1, 4, 64, 128)
    new_sb = np.stack([np.asarray(R[cidx]["new_sb"]) for cidx in range(8)]).reshape(16, 1, 4, 64, 128)
    return (y_prompt.astype(f32), y_sample.astype(f32), new_kv.astype(f32), new_kr.astype(f32),
            new_sf.astype(f32), new_sb.astype(f32))
```
